# Optimizing a Trainium2 kernel written in Bass

```python
import math
import jax, jax.numpy as jnp
from jax import lax
import numpy as np

D_MODEL = 2048
BATCH = 8
SEQ = 2048
DEPTH = 2
DEC_BATCH = 32
DEC_SEQ = 64
PAST_LEN = 1024

CHUNK = 64
N_A = DEPTH // 2
N_B = DEPTH - N_A
D_FF = 5632
RET_HEADS = 8
RET_DK = D_MODEL // RET_HEADS
RET_DV = 2 * D_MODEL // RET_HEADS
MLA_HEADS = 16
QK_NOPE = 128
QK_ROPE = 64
QK_HEAD = QK_NOPE + QK_ROPE
V_HEAD = 128
KV_LORA = 512
Q_LORA = 512
Q_BLOCK = 128
ROPE_BASE = 10000.0
EPS = 1e-6
NEG_INF = -1e30

kernel_name = "yoco_retention_mla_macaron_stream_step"

F32 = jnp.float32


def rmsnorm(x, g):
    xf = x.astype(F32)
    y = xf * lax.rsqrt(jnp.mean(xf * xf, axis=-1, keepdims=True) + EPS)
    return (y * g.astype(F32)).astype(x.dtype)


def rope(x, pos):
    half = x.shape[-1] // 2
    inv = jnp.power(ROPE_BASE, -jnp.arange(half, dtype=F32) / half)
    ang = pos.astype(F32)[:, None] * inv[None, :]
    cos = jnp.cos(ang)[:, None, :]
    sin = jnp.sin(ang)[:, None, :]
    x1 = x[..., :half].astype(F32)
    x2 = x[..., half:].astype(F32)
    return jnp.concatenate([x1 * cos - x2 * sin, x1 * sin + x2 * cos], axis=-1).astype(x.dtype)


def swiglu(x, g, w_in, w_out):
    a, b = jnp.split(rmsnorm(x, g) @ w_in, 2, axis=-1)
    return (jax.nn.silu(a) * b) @ w_out


def retention_block(q, k, v, S, log_gamma):
    L = q.shape[2]
    n = jnp.arange(L, dtype=F32)
    diff = n[:, None] - n[None, :]
    lg = log_gamma[:, None, None]
    decay = jnp.where(diff >= 0, jnp.exp(lg * jnp.maximum(diff, 0.0)), 0.0).astype(q.dtype)
    inner = jnp.einsum('bhqd,bhkd->bhqk', q, k) * decay
    cross = jnp.exp(log_gamma[:, None] * (n + 1.0))[:, :, None].astype(q.dtype)
    o = jnp.einsum('bhqk,bhkv->bhqv', inner, v) + cross * jnp.einsum('bhqd,bhdv->bhqv', q, S)
    kdec = jnp.exp(log_gamma[:, None] * (L - 1.0 - n))[:, :, None].astype(q.dtype)
    S_new = jnp.exp(log_gamma * L)[:, None, None].astype(S.dtype) * S + \
        jnp.einsum('bhkd,bhkv->bhdv', k * kdec, v).astype(S.dtype)
    return o, S_new


def retention(xn, pos, state, w_in, gn_gain, w_out, prompt):
    B, T, _ = xn.shape
    hk, hv = RET_HEADS * RET_DK, RET_HEADS * RET_DV
    q, k, v, g = jnp.split(xn @ w_in, [hk, 2 * hk, 2 * hk + hv], axis=-1)
    q = rope(q.reshape(B, T, RET_HEADS, RET_DK), pos) * (RET_DK ** -0.5)
    k = rope(k.reshape(B, T, RET_HEADS, RET_DK), pos)
    v = v.reshape(B, T, RET_HEADS, RET_DV)
    q, k, v = jnp.swapaxes(q, 1, 2), jnp.swapaxes(k, 1, 2), jnp.swapaxes(v, 1, 2)
    log_gamma = jnp.log1p(-jnp.exp2(-5.0 - jnp.arange(RET_HEADS, dtype=F32)))
    if prompt:
        nC = T // CHUNK

        def to_chunks(t):
            return jnp.moveaxis(t.reshape(B, RET_HEADS, nC, CHUNK, t.shape[-1]), 2, 0)

        def step(S, blk):
            qb, kb, vb = blk
            o_b, S = retention_block(qb, kb, vb, S, log_gamma)
            return S, o_b

        S0 = jnp.zeros((B, RET_HEADS, RET_DK, RET_DV), q.dtype)
        S_fin, o = lax.scan(step, S0, (to_chunks(q), to_chunks(k), to_chunks(v)))
        o = jnp.moveaxis(o, 0, 2).reshape(B, RET_HEADS, T, RET_DV)
    else:
        o, S_fin = retention_block(q, k, v, state, log_gamma)
    of = jnp.swapaxes(o, 1, 2).astype(F32)
    mu = jnp.mean(of, axis=-1, keepdims=True)
    var = jnp.mean(jnp.square(of - mu), axis=-1, keepdims=True)
    of = ((of - mu) * lax.rsqrt(var + EPS)).reshape(B, T, hv) * gn_gain.astype(F32)
    y = jax.nn.silu(g) * of.astype(g.dtype)
    return y @ w_out, S_fin


def shared_latent(h, pos, kv_norm, w_dkv, kv_lat_norm):
    ckr = rmsnorm(h, kv_norm) @ w_dkv
    c = rmsnorm(ckr[..., :KV_LORA], kv_lat_norm)
    kr = rope(ckr[..., KV_LORA:][:, :, None, :], pos)[:, :, 0, :]
    return c, kr


def mla_keys_values(c, kr, w_ukv, k_norm):
    B, T, _ = c.shape
    kv = (c @ w_ukv).reshape(B, T, MLA_HEADS, QK_NOPE + V_HEAD)
    k = jnp.concatenate([kv[..., :QK_NOPE],
                         jnp.broadcast_to(kr[:, :, None, :], (B, T, MLA_HEADS, QK_ROPE))], axis=-1)
    return rmsnorm(k, k_norm), kv[..., QK_NOPE:]


def mla_queries(hn, pos, w_dq, q_lat_norm, w_uq, q_norm):
    B, T, _ = hn.shape
    q = (rmsnorm(hn @ w_dq, q_lat_norm) @ w_uq).reshape(B, T, MLA_HEADS, QK_HEAD)
    q = jnp.concatenate([q[..., :QK_NOPE], rope(q[..., QK_NOPE:], pos)], axis=-1)
    return rmsnorm(q, q_norm) * (QK_HEAD ** -0.5)


def attend(q, k, v, mask):
    s = jnp.einsum('bqhd,bkhd->bhqk', q, k).astype(F32)
    if mask is not None:
        s = jnp.where(mask[None, None], s, NEG_INF)
    p = jax.nn.softmax(s, axis=-1).astype(v.dtype)
    return jnp.einsum('bhqk,bkhd->bqhd', p, v)


def prompt_attention(q, k, v):
    B, T, _, _ = q.shape
    nq = T // Q_BLOCK
    kchunk = jnp.arange(T) // CHUNK
    qb = jnp.moveaxis(q.reshape(B, nq, Q_BLOCK, MLA_HEADS, QK_HEAD), 1, 0)
    starts = jnp.arange(nq) * Q_BLOCK

    def one(args):
        qi, s0 = args
        qchunk = (s0 + jnp.arange(Q_BLOCK)) // CHUNK
        return attend(qi, k, v, kchunk[None, :] <= qchunk[:, None])

    o = lax.map(one, (qb, starts))
    return jnp.moveaxis(o, 0, 1).reshape(B, T, MLA_HEADS * V_HEAD)


def run_trunk(x, pos, ret_state, ckv_past, krope_past, prompt, p):
    h = x
    new_ret = []
    c_new = kr_new = k_sh = v_sh = None
    for l in range(DEPTH):
        h = h + 0.5 * swiglu(h, p['ffn1_norm'][l], p['ffn1_w_in'][l], p['ffn1_w_out'][l])
        hn = rmsnorm(h, p['mix_norm'][l])
        if l < N_A:
            o, S = retention(hn, pos, None if prompt else ret_state[l],
                             p['ret_w_in'][l], p['ret_gn_gain'][l], p['ret_w_out'][l], prompt)
            new_ret.append(S)
        else:
            j = l - N_A
            q = mla_queries(hn, pos, p['w_dq'][j], p['q_lat_norm'][j], p['w_uq'][j], p['q_norm'][j])
            if prompt:
                o = prompt_attention(q, k_sh, v_sh)
            else:
                B, T = q.shape[0], q.shape[1]
                o = attend(q, k_sh, v_sh, None).reshape(B, T, MLA_HEADS * V_HEAD)
            o = o @ p['w_o'][j]
        h = h + o
        h = h + 0.5 * swiglu(h, p['ffn2_norm'][l], p['ffn2_w_in'][l], p['ffn2_w_out'][l])
        if l == N_A - 1:
            c_new, kr_new = shared_latent(h, pos, p['kv_norm'], p['w_dkv'], p['kv_lat_norm'])
            if prompt:
                c_all, kr_all = c_new, kr_new
            else:
                c_all = jnp.concatenate([ckv_past.astype(c_new.dtype), c_new], axis=1)
                kr_all = jnp.concatenate([krope_past.astype(kr_new.dtype), kr_new], axis=1)
            k_sh, v_sh = mla_keys_values(c_all, kr_all, p['w_ukv'], p['k_norm'])
    return h, jnp.stack(new_ret), c_new, kr_new


def setup_inputs(seed: int = 0) -> dict:
    key = jax.random.key(seed)
    ks = iter(jax.random.split(key, 40))

    def nrm(shape, scale):
        return jax.random.normal(next(ks), shape, F32) * scale

    def gain(shape):
        return 1.0 + 0.02 * jax.random.normal(next(ks), shape, F32)

    D = D_MODEL
    ret_in = 2 * RET_HEADS * RET_DK + 2 * RET_HEADS * RET_DV
    return {
        "x_prompt": nrm((BATCH, SEQ, D), 1.0),
        "x_sample": nrm((DEC_BATCH, DEC_SEQ, D), 1.0),
        "state_ret": nrm((N_A, DEC_BATCH, RET_HEADS, RET_DK, RET_DV), 1.0),
        "cache_ckv": nrm((DEC_BATCH, PAST_LEN, KV_LORA), 1.0),
        "cache_krope": nrm((DEC_BATCH, PAST_LEN, QK_ROPE), 1.0),
        "ffn1_norm": gain((DEPTH, D)),
        "ffn1_w_in": nrm((DEPTH, D, 2 * D_FF), D ** -0.5),
        "ffn1_w_out": nrm((DEPTH, D_FF, D), D_FF ** -0.5),
        "mix_norm": gain((DEPTH, D)),
        "ffn2_norm": gain((DEPTH, D)),
        "ffn2_w_in": nrm((DEPTH, D, 2 * D_FF), D ** -0.5),
        "ffn2_w_out": nrm((DEPTH, D_FF, D), D_FF ** -0.5),
        "ret_w_in": nrm((N_A, D, ret_in), D ** -0.5),
        "ret_gn_gain": gain((N_A, RET_HEADS * RET_DV)),
        "ret_w_out": nrm((N_A, RET_HEADS * RET_DV, D), (RET_HEADS * RET_DV) ** -0.5),
        "kv_norm": gain((D,)),
        "w_dkv": nrm((D, KV_LORA + QK_ROPE), D ** -0.5),
        "kv_lat_norm": gain((KV_LORA,)),
        "w_ukv": nrm((KV_LORA, MLA_HEADS * (QK_NOPE + V_HEAD)), KV_LORA ** -0.5),
        "k_norm": gain((QK_HEAD,)),
        "w_dq": nrm((N_B, D, Q_LORA), D ** -0.5),
        "q_lat_norm": gain((N_B, Q_LORA)),
        "w_uq": nrm((N_B, Q_LORA, MLA_HEADS * QK_HEAD), Q_LORA ** -0.5),
        "q_norm": gain((N_B, QK_HEAD)),
        "w_o": nrm((N_B, MLA_HEADS * V_HEAD, D), (MLA_HEADS * V_HEAD) ** -0.5),
    }


def reference(x_prompt, x_sample, state_ret, cache_ckv, cache_krope,
              ffn1_norm, ffn1_w_in, ffn1_w_out, mix_norm, ffn2_norm, ffn2_w_in, ffn2_w_out,
              ret_w_in, ret_gn_gain, ret_w_out,
              kv_norm, w_dkv, kv_lat_norm, w_ukv, k_norm,
              w_dq, q_lat_norm, w_uq, q_norm, w_o):
    p = dict(ffn1_norm=ffn1_norm, ffn1_w_in=ffn1_w_in, ffn1_w_out=ffn1_w_out,
             mix_norm=mix_norm, ffn2_norm=ffn2_norm, ffn2_w_in=ffn2_w_in, ffn2_w_out=ffn2_w_out,
             ret_w_in=ret_w_in, ret_gn_gain=ret_gn_gain, ret_w_out=ret_w_out,
             kv_norm=kv_norm, w_dkv=w_dkv, kv_lat_norm=kv_lat_norm, w_ukv=w_ukv, k_norm=k_norm,
             w_dq=w_dq, q_lat_norm=q_lat_norm, w_uq=w_uq, q_norm=q_norm, w_o=w_o)
    pos_prompt = jnp.arange(x_prompt.shape[1])
    past = cache_ckv.shape[1]
    pos_sample = past + jnp.arange(x_sample.shape[1])
    y_prompt, ret_p, ckv_p, kr_p = run_trunk(x_prompt, pos_prompt, None, None, None, True, p)
    y_sample, ret_s, ckv_s, kr_s = run_trunk(x_sample, pos_sample, state_ret, cache_ckv,
                                             cache_krope, False, p)
    return (y_prompt, y_sample, ret_p, ckv_p, kr_p, ret_s, ckv_s, kr_s)
```

```python
import math
import os
KDBG = int(os.environ.get('KDBG', '0'))
from contextlib import ExitStack
import numpy as np
import concourse.bass as bass
import concourse.mybir as mybir
from concourse.bass_utils import run_bass_kernel_spmd

F32 = mybir.dt.float32
BF16 = mybir.dt.bfloat16
AF = mybir.ActivationFunctionType
ALU = mybir.AluOpType

D = 2048
DFF = 5632
NKC = 16
NHC = 44
RH = 8
MH = 16
SEQ = 2048
PAST = 1024
DEC_S = 64
TT = 512
NPT = SEQ // TT
EPS = 1e-6
SLOT = 4096
NSLOT = 5
HOLD = 2
NCORES = 8
GAMMA = [1.0 - 2.0 ** (-5.0 - h) for h in range(RH)]

G_F1 = [0, 16]
G_MIX = [32, 48]
G_F2 = [64, 80]
G_KVN = 96
G_KVLAT = 112
G_QLAT = 116
G_GN = 120
G_KNOPE = 152
G_KROPE = 153
G_QNOPE = 154
G_QROPE = 155
NG = 160


def weight_tiles():
    tiles = []
    for l in range(2):
        for f in (1, 2):
            for hc in range(NHC):
                tiles.append((f"f{f}in{l}_{hc}", 4096))
            for nc_ in range(NKC):
                for half in range(2):
                    tiles.append((f"f{f}out{l}_{nc_}_{half}", 2816))
    for h in range(RH):
        for nm in ("q", "k", "v0", "v1", "g0", "g1"):
            tiles.append((f"rin_{h}_{nm}", 4096))
    for nc_ in range(NKC):
        tiles.append((f"rout_{nc_}", 4096))
    tiles += [("dkv_0", 4096), ("dkv_1", 4096), ("dkv_2", 2048)]
    tiles += [("dq_0", 4096), ("dq_1", 4096)]
    for hg in range(4):
        tiles.append((f"uq_{hg}", 4096))
    for hg in range(4):
        tiles.append((f"ukv_{hg}", 4096))
    for np_ in range(8):
        tiles.append((f"wo_{np_}", 4096))
    off = 0
    out = {}
    for nm, e in tiles:
        out[nm] = (off, e)
        off += e
    return out, off


WL, WTOT = weight_tiles()


def pack_weights(inp):
    W = np.empty((128, WTOT), np.float32)

    def put(nm, arr):
        off, e = WL[nm]
        W[:, off:off + e] = arr.reshape(128, e)

    def kc_tile(M):
        K, n = M.shape
        return M.reshape(K // 128, 128, n).transpose(1, 0, 2)

    for l in range(2):
        for f in (1, 2):
            win = inp[f"ffn{f}_w_in"][l]
            a = win.reshape(16, 128, 2, NHC, 128).transpose(1, 3, 0, 2, 4)
            for hc in range(NHC):
                put(f"f{f}in{l}_{hc}", a[:, hc])
            wout = inp[f"ffn{f}_w_out"][l]
            b = wout.reshape(2, 22, 128, 16, 128).transpose(2, 3, 0, 1, 4)
            for nc_ in range(NKC):
                for half in range(2):
                    put(f"f{f}out{l}_{nc_}_{half}", b[:, nc_, half])
    rin = inp["ret_w_in"][0]
    for h in range(RH):
        put(f"rin_{h}_q", kc_tile(rin[:, h * 256:(h + 1) * 256]))
        put(f"rin_{h}_k", kc_tile(rin[:, 2048 + h * 256:2048 + (h + 1) * 256]))
        put(f"rin_{h}_v0", kc_tile(rin[:, 4096 + h * 512:4096 + h * 512 + 256]))
        put(f"rin_{h}_v1", kc_tile(rin[:, 4096 + h * 512 + 256:4096 + (h + 1) * 512]))
        put(f"rin_{h}_g0", kc_tile(rin[:, 8192 + h * 512:8192 + h * 512 + 256]))
        put(f"rin_{h}_g1", kc_tile(rin[:, 8192 + h * 512 + 256:8192 + (h + 1) * 512]))
    rout = inp["ret_w_out"][0]
    for nc_ in range(NKC):
        put(f"rout_{nc_}", kc_tile(rout[:, nc_ * 128:(nc_ + 1) * 128]))
    dkv = inp["w_dkv"]
    put("dkv_0", kc_tile(dkv[:, 0:256]))
    put("dkv_1", kc_tile(dkv[:, 256:512]))
    put("dkv_2", kc_tile(np.concatenate([dkv[:, 512:576], dkv[:, 544:576], dkv[:, 512:544]], 1)))
    dq = inp["w_dq"][0]
    put("dq_0", kc_tile(dq[:, 0:256]))
    put("dq_1", kc_tile(dq[:, 256:512]))
    uq = inp["w_uq"][0].reshape(512, MH, 192)
    uq = np.concatenate([uq, uq[:, :, 160:192], uq[:, :, 128:160]], 2)
    for hg in range(4):
        put(f"uq_{hg}", kc_tile(uq[:, hg * 4:(hg + 1) * 4].reshape(512, 1024)))
    ukv = inp["w_ukv"]
    for hg in range(4):
        put(f"ukv_{hg}", kc_tile(ukv[:, hg * 1024:(hg + 1) * 1024]))
    wo = inp["w_o"][0]
    for np_ in range(8):
        put(f"wo_{np_}", kc_tile(wo[:, np_ * 256:(np_ + 1) * 256]))
    return W


def pack_gains(inp):
    G = np.zeros((128, NG), np.float32)

    def fm(v):
        return v.reshape(-1, 128).T

    for l in range(2):
        G[:, G_F1[l]:G_F1[l] + 16] = fm(inp["ffn1_norm"][l])
        G[:, G_MIX[l]:G_MIX[l] + 16] = fm(inp["mix_norm"][l])
        G[:, G_F2[l]:G_F2[l] + 16] = fm(inp["ffn2_norm"][l])
    G[:, G_KVN:G_KVN + 16] = fm(inp["kv_norm"])
    G[:, G_KVLAT:G_KVLAT + 4] = fm(inp["kv_lat_norm"])
    G[:, G_QLAT:G_QLAT + 4] = fm(inp["q_lat_norm"][0])
    G[:, G_GN:G_GN + 32] = fm(inp["ret_gn_gain"][0])
    G[:, G_KNOPE] = inp["k_norm"][:128]
    G[:64, G_KROPE] = inp["k_norm"][128:]
    G[:, G_QNOPE] = inp["q_norm"][0][:128]
    G[:64, G_QROPE] = inp["q_norm"][0][128:]
    return G


def const_tables():
    cm = np.zeros((128, 6 * 128), np.float32)
    cm[:, 0:128] = np.eye(128)
    cm[:, 128:256] = 1.0 / 2048
    cm[:, 256:384] = 1.0 / 512
    cm[:, 384:512] = 1.0 / 192
    m = np.arange(128)
    cm[:, 512:640] = (m[None, :] >= m[:, None])
    cm[:, 640:768] = ((m[:, None] // 64) <= (m[None, :] // 64))
    dqk = np.zeros((128, RH, 2, 128), np.float32)
    n = np.arange(128, dtype=np.float64)
    for h in range(RH):
        lg = math.log1p(-2.0 ** (-5.0 - h))
        dqk[:, h, 0, :] = (np.exp(lg * (n + 1.0)) / 16.0)[None, :]
        dqk[:, h, 1, :] = np.exp(-lg * (n + 1.0))[None, :]
    pos = np.zeros((NPT + 1, TT), np.float32)
    for t in range(NPT):
        pos[t] = np.arange(t * TT, (t + 1) * TT)
    pos[NPT, :256] = np.tile(PAST + np.arange(DEC_S), 4)
    inv_r = np.power(np.float32(10000.0), -np.arange(128, dtype=np.float32) / np.float32(128))
    inv_m = np.power(np.float32(10000.0), -np.arange(32, dtype=np.float32) / np.float32(32))
    rr = np.zeros((NPT + 1, 128, 2 * TT), np.float32)
    rm = np.zeros((NPT + 1, 64, 2 * TT), np.float32)
    for t in range(NPT + 1):
        ang = (pos[t][None, :] * inv_r[:, None]).astype(np.float32)
        rr[t, :, :TT] = np.cos(ang)
        rr[t, :, TT:] = np.sin(ang)
        angm = (pos[t][None, :] * inv_m[:, None]).astype(np.float32)
        c = np.cos(angm)
        s = np.sin(angm)
        rm[t, :, :TT] = np.concatenate([c, c], 0)
        rm[t, :, TT:] = np.concatenate([-s, s], 0)
    return cm, dqk.reshape(128, RH * 2 * 128), rr, rm


class Op:
    __slots__ = ("eng", "fn", "deps", "needed", "sem", "val", "is_dma", "lane")


class _Rec:
    def __getattr__(self, name):
        def f(*a, **k):
            self.__dict__["call"] = (name, a, k)
            return self
        return f


class Sched:
    def __init__(self, nc, n_lanes):
        self.nc = nc
        self.E = {"pe": nc.tensor, "act": nc.scalar, "dve": nc.vector, "pool": nc.gpsimd, "sp": nc.sync}
        self.ops = []
        self.lw = {}
        self.rd = {}
        self.n_lanes = n_lanes
        self.lane_last = {}
        self.lane_next = 0

    def add(self, eng, fn, reads=(), writes=(), dma=False, lane=None):
        op = Op()
        op.eng = eng
        rec = _Rec()
        fn(rec)
        op.fn = rec.call
        op.is_dma = dma
        op.needed = dma
        op.sem = None
        op.val = 0
        op.lane = None
        deps = []
        for t in reads:
            w = self.lw.get(t)
            if w is not None:
                deps.append(w)
        for t in writes:
            w = self.lw.get(t)
            if w is not None:
                deps.append(w)
            r = self.rd.get(t)
            if r:
                deps.extend(r.values())
        if dma:
            if lane is None:
                lane = self.lane_next
                self.lane_next = (self.lane_next + 1) % self.n_lanes
            op.lane = lane
            prev = self.lane_last.get(lane)
            if prev is not None:
                deps.append(prev)
            self.lane_last[lane] = op
        key = ("dma", op.lane) if dma else eng
        for t in reads:
            self.rd.setdefault(t, {})[key] = op
        for t in writes:
            self.lw[t] = op
            self.rd[t] = {}
        fdeps = []
        seen = set()
        for d in deps:
            if id(d) in seen or d is op:
                continue
            seen.add(id(d))
            if (not d.is_dma) and (not dma) and d.eng == eng and eng == "pe":
                continue
            d.needed = True
            fdeps.append(d)
        op.deps = fdeps
        self.ops.append(op)
        return op

    def emit(self, sems_eng, sems_lane):
        cnt = {e: 0 for e in self.E}
        lane_cnt = {}
        waited = {e: {} for e in self.E}
        for op in self.ops:
            eng = self.E[op.eng]
            need = {}
            for d in op.deps:
                k = id(d.sem)
                if k not in need or need[k][1] < d.val:
                    need[k] = (d.sem, d.val)
            w = waited[op.eng]
            for k, (sem, val) in need.items():
                if w.get(k, 0) < val:
                    eng.wait_ge(sem, val)
                    w[k] = val
            nm_, a_, k_ = op.fn
            inst = getattr(eng, nm_)(*a_, **k_)
            if op.is_dma:
                lane_cnt[op.lane] = lane_cnt.get(op.lane, 0) + 1
                op.sem = sems_lane[op.lane]
                op.val = 16 * lane_cnt[op.lane]
                inst.then_inc(op.sem, 16)
            elif op.needed:
                cnt[op.eng] += 1
                op.sem = sems_eng[op.eng]
                op.val = cnt[op.eng]
                inst.then_inc(op.sem, 1)
        sp = self.E["sp"]
        for lane, c in lane_cnt.items():
            sp.wait_ge(sems_lane[lane], 16 * c)


def build_program(n_prompt_tiles=NPT, do_sample=True, stop_after=None):
    nc = bass.Bass("TRN2", target_bir_lowering=False)

    def din(name, shape):
        return nc.dram_tensor(name, shape, F32, kind="ExternalInput").ap()

    def dout(name, shape):
        return nc.dram_tensor(name, shape, F32, kind="ExternalOutput").ap()

    xp = din("xp", [SEQ, D])
    xs = din("xs", [256, D])
    sret = din("sret", [4, RH, 256, 512])
    cckv = din("cckv", [4, PAST, 512])
    ckr = din("ckr", [4, PAST, 64])
    wts = din("wts", [128, WTOT])
    gains_d = din("gains", [128, NG])
    cmat_d = din("cmat", [128, 768])
    dqk_d = din("dqk", [128, RH * 2 * 128])
    rr_d = din("rope_ret", [NPT + 1, 128, 2 * TT])
    rm_d = din("rope_mla", [NPT + 1, 64, 2 * TT])
    yp = dout("yp", [SEQ, D])
    ys = dout("ys", [256, D])
    retp = dout("retp", [RH, 256, 512])
    ckvp = dout("ckvp", [SEQ, 512])
    krp = dout("krp", [SEQ, 64])
    rets = dout("rets", [4, RH, 256, 512])
    ckvs = dout("ckvs", [256, 512])
    krs = dout("krs", [256, 64])

    N_GEN_LANES = 16
    with ExitStack() as es:
        def sb(name, shape, dt):
            return es.enter_context(nc.sbuf_tensor(name, shape, dt))

        hT = sb("hT", [128, NKC * TT], F32)
        xnT = sb("xnT", [128, NKC * TT], BF16)
        hid = sb("hid", [128, NHC * TT], BF16)
        ring = sb("ring", [128, NSLOT * SLOT], BF16)
        cT = sb("cT", [128, 4 * SEQ], BF16)
        krT = sb("krT", [128, SEQ], BF16)
        gains = sb("gains_sb", [128, NG], F32)
        cmat = sb("cmat_sb", [128, 768], F32)
        dqh = sb("dqh", [128, 2 * 256], F32)
        identb = sb("identb", [128, 128], BF16)
        onesb = sb("onesb", [128, 128], BF16)
        epst = sb("epst", [128, 1], F32)
        rope = sb("rope", [128, 2 * TT], F32)
        rstd = sb("rstd", [128, 2 * TT], F32)
        sqb = sb("sqb", [128, 2 * TT], F32)
        tmpf = sb("tmpf", [128, 4 * TT], F32)
        UA = sb("UA", [128, 2048], BF16)
        UB = sb("UB", [128, 2048], BF16)
        ktok2 = sb("ktok2", [128, 2048], BF16)
        vtokB = sb("vtokB", [128, 2048], BF16)
        innb = sb("innb", [128, 128], BF16)
        onb = sb("onb", [128, 512], BF16)
        Sf = sb("Sf", [128, 2 * 1024], F32)
        Sb = sb("Sb", [128, 1024], BF16)
        stt = sb("stt", [128, 16], F32)
        sskr = sb("sskr", [128, 16], F32)
        rkb = sb("rkb", [128, 16], F32)
        comb = sb("comb", [128, 2], F32)
        kn2 = UA[:, 0:1024]
        knr2 = UA[:, 1024:2048]
        vsb2 = UB[:, 0:1024]
        pbuf = UB[:, 1024:2048]
        ps = [es.enter_context(nc.psum_tensor(f"ps{i}", [128, 512], F32)) for i in range(8)]
        sems_eng = {e: es.enter_context(nc.semaphore("s_" + e)) for e in ["pe", "act", "dve", "pool", "sp"]}
        lanes = [es.enter_context(nc.semaphore(f"ln{i}")) for i in range(N_GEN_LANES + NSLOT)]

        s = Sched(nc, N_GEN_LANES)
        ident = cmat[:, 0:128]
        on2048 = cmat[:, 128:256]
        on512 = cmat[:, 256:384]
        on192 = cmat[:, 384:512]
        rmask = cmat[:, 512:640]
        amask = cmat[:, 640:768]

        def hidtok(c0, c1):
            return [("hid", b) for b in range(c0 // 512, (c1 + 511) // 512)]

        def PS(i):
            return ("ps", i)

        s.add("sp", lambda e: e.dma_start(out=gains[:], in_=gains_d[:, :]), writes=["gains"], dma=True)
        s.add("sp", lambda e: e.dma_start(out=cmat[:], in_=cmat_d[:, :]), writes=["cmat"], dma=True)
        s.add("dve", lambda e: e.tensor_copy(out=identb[:], in_=ident), reads=["cmat"], writes=["identb"])
        s.add("dve", lambda e: e.memset(onesb[:], 1.0), writes=["onesb"])
        s.add("dve", lambda e: e.memset(epst[:], EPS), writes=["eps"])
        s.add("dve", lambda e: e.scalar_tensor_tensor(out=comb[:, 0:1], in0=gains[:, G_QNOPE:G_QNOPE + 1], scalar=float(192 ** -0.5),
                                                      in1=gains[:, G_KNOPE:G_KNOPE + 1], op0=ALU.mult, op1=ALU.mult),
              reads=["gains"], writes=["comb"])
        s.add("dve", lambda e: e.scalar_tensor_tensor(out=comb[:64, 1:2], in0=gains[:64, G_QROPE:G_QROPE + 1], scalar=float(192 ** -0.5),
                                                      in1=gains[:64, G_KROPE:G_KROPE + 1], op0=ALU.mult, op1=ALU.mult),
              reads=["gains"], writes=["comb"])

        class Ring:
            def __init__(self):
                self.sched = []
                self.issued = 0
                self.pos = 0

            def plan(self, names):
                self.sched.extend(names)

            def _issue(self, i):
                nm = self.sched[i]
                off, e_ = WL[nm]
                slot = i % NSLOT
                dst = ring[:, slot * SLOT:slot * SLOT + e_]
                src = wts[:, off:off + e_]
                s.add("pool", lambda e, dst=dst, src=src: e.dma_start(out=dst, in_=src),
                      writes=[("ring", slot)], dma=True, lane=N_GEN_LANES + slot)

            def get(self, nm):
                i = self.pos
                assert self.sched[i] == nm, (self.sched[i], nm)
                while self.issued < min(len(self.sched), i + NSLOT - HOLD):
                    self._issue(self.issued)
                    self.issued += 1
                self.pos += 1
                slot = i % NSLOT
                return ring[:, slot * SLOT:slot * SLOT + WL[nm][1]], ("ring", slot)

        R = Ring()

        def pass_weight_names(sample):
            n = []
            for l in range(2):
                n += [f"f1in{l}_{hc}" for hc in range(NHC)]
                n += [f"f1out{l}_{c}_{hf}" for c in range(NKC) for hf in range(2)]
                if l == 0:
                    for h in range(RH):
                        n += [f"rin_{h}_{x}" for x in ("q", "k", "v0", "v1", "g0", "g1")]
                    n += [f"rout_{c}" for c in range(NKC)]
                else:
                    n += ["dq_0", "dq_1"] + [f"uq_{hg}" for hg in range(4)]
                    for _ in range(4 if sample else 1):
                        n += [f"ukv_{hg}" for hg in range(4)]
                    n += [f"wo_{i}" for i in range(8)]
                n += [f"f2in{l}_{hc}" for hc in range(NHC)]
                n += [f"f2out{l}_{c}_{hf}" for c in range(NKC) for hf in range(2)]
                if l == 0:
                    n += ["dkv_0", "dkv_1", "dkv_2"]
            return n

        def mm(out, lhsT, rhs, start, stop, reads, writes):
            s.add("pe", lambda e: e.matmul(out, lhsT, rhs, start=start, stop=stop), reads=reads, writes=writes)

        def tr(out, in_, idn, reads, writes):
            s.add("pe", lambda e: e.transpose(out, in_, idn), reads=reads, writes=writes)

        def rstd_from(psum_ap, out_ap, reads, writes):
            s.add("act", lambda e: e.activation(out=out_ap, in_=psum_ap, func=AF.Ln, bias=epst[:psum_ap.shape[0], 0:1], scale=1.0),
                  reads=reads + ["eps"], writes=writes)
            s.add("act", lambda e: e.activation(out=out_ap, in_=out_ap, func=AF.Exp, scale=-0.5),
                  reads=writes, writes=writes)

        def load_x(src_rows, nblk):
            for j in range(nblk):
                st = hid[:, (j % 2) * 4096:(j % 2) * 4096 + 4096].bitcast(F32)
                sttok = hidtok((j % 2) * 4096, (j % 2) * 4096 + 4096)
                s.add("sp", lambda e, st=st, j=j: e.dma_start(out=st, in_=src_rows[j * 128:(j + 1) * 128, :]),
                      writes=sttok, dma=True)
                for g in range(4):
                    b = g % 2
                    for i in range(4):
                        kc = 4 * g + i
                        tr(ps[b][:, i * 128:(i + 1) * 128], st[:, kc * 128:(kc + 1) * 128], ident,
                           reads=sttok + ["cmat"], writes=[PS(b)])
                    dst = hT[:, :].rearrange("p (k t) -> p k t", k=NKC)[:, 4 * g:4 * g + 4, j * 128:(j + 1) * 128]
                    src = ps[b][:, :].rearrange("p (k t) -> p k t", k=4)
                    s.add("act" if g % 2 else "dve",
                          (lambda e, dst=dst, src=src: e.copy(out=dst, in_=src)) if g % 2 else
                          (lambda e, dst=dst, src=src: e.tensor_copy(out=dst, in_=src)),
                          reads=[PS(b)], writes=[("hT", 4 * g + i) for i in range(4)])
                    if j == nblk - 1:
                        for i in range(4):
                            stat_chunk(4 * g + i, nblk * 128, defer=False)

        def store_y(dst_rows, nblk):
            for j in range(nblk):
                st = hid[:, (j % 2) * 4096:(j % 2) * 4096 + 4096].bitcast(F32)
                sttok = hidtok((j % 2) * 4096, (j % 2) * 4096 + 4096)
                for g in range(4):
                    b = g % 2
                    for i in range(4):
                        kc = 4 * g + i
                        tr(ps[b][:, i * 128:(i + 1) * 128], hT[:, kc * TT + j * 128:kc * TT + (j + 1) * 128], ident,
                           reads=[("hT", kc), "cmat"], writes=[PS(b)])
                    dst = st[:, g * 512:(g + 1) * 512]
                    s.add("act" if g % 2 else "dve",
                          (lambda e, dst=dst, b=b: e.copy(out=dst, in_=ps[b][:, :])) if g % 2 else
                          (lambda e, dst=dst, b=b: e.tensor_copy(out=dst, in_=ps[b][:, :])),
                          reads=[PS(b)], writes=sttok)
                s.add("sp", lambda e, st=st, j=j: e.dma_start(out=dst_rows[j * 128:(j + 1) * 128, :], in_=st),
                      reads=sttok, dma=True)

        pend_stat = []

        def stat_flush():
            while pend_stat:
                pend_stat.pop(0)()

        def stat_chunk(kc, tt, defer=True):
            q = sqb[:, (kc % 2) * TT:(kc % 2) * TT + tt]
            src = hT[:, kc * TT:kc * TT + tt]
            s.add("dve", lambda e: e.tensor_tensor(out=q, in0=src, in1=src, op=ALU.mult),
                  reads=[("hT", kc)], writes=[("sqb", kc % 2)])

            def f():
                mm(ps[7][:, :tt], on2048, q, kc == 0, kc == NKC - 1, reads=["cmat", ("sqb", kc % 2)], writes=[PS(7)])
            if defer:
                pend_stat.append(f)
            else:
                f()

        def rmsnorm(gcol, tt, reuse=False):
            stat_flush()
            rs = rstd[:, 0:tt]
            if not reuse:
                rstd_from(ps[7][:, :tt], rs, [PS(7)], [("rstd", 0)])
            for kc in range(NKC):
                src = hT[:, kc * TT:kc * TT + tt]
                dst = xnT[:, kc * TT:kc * TT + tt]
                s.add("dve", lambda e: e.scalar_tensor_tensor(
                    out=dst, in0=src, scalar=gains[:, gcol + kc:gcol + kc + 1], in1=rs, op0=ALU.mult, op1=ALU.mult),
                    reads=[("hT", kc), ("rstd", 0), "gains"], writes=[("xnT", kc)])

        def ffn(l, f, tt, reuse=False, next_norm=True):
            rmsnorm((G_F1 if f == 1 else G_F2)[l], tt, reuse=reuse)
            for hc in range(NHC):
                wt, wtok = R.get(f"f{f}in{l}_{hc}")
                wv = wt.rearrange("p (k a c) -> p k a c", k=NKC, a=2)
                pa, pb = hc % 2, 2 + hc % 2
                for kc in range(NKC):
                    mm(ps[pa][:, :tt], wv[:, kc, 0, :], xnT[:, kc * TT:kc * TT + tt], kc == 0, kc == NKC - 1,
                       reads=[wtok, ("xnT", kc)], writes=[PS(pa)])
                for kc in range(NKC):
                    mm(ps[pb][:, :tt], wv[:, kc, 1, :], xnT[:, kc * TT:kc * TT + tt], kc == 0, kc == NKC - 1,
                       reads=[wtok, ("xnT", kc)], writes=[PS(pb)])
                sa = tmpf[:, (hc % 2) * TT:(hc % 2) * TT + tt]
                s.add("act", lambda e, sa=sa, pa=pa: e.activation(out=sa, in_=ps[pa][:, :tt], func=AF.Silu),
                      reads=[PS(pa)], writes=[("tmpf", hc % 2)])
                dst = hid[:, hc * TT:hc * TT + tt]
                s.add("dve", lambda e, sa=sa, pb=pb, dst=dst: e.tensor_tensor(out=dst, in0=sa, in1=ps[pb][:, :tt], op=ALU.mult),
                      reads=[PS(pb), ("tmpf", hc % 2)], writes=[("hid", hc)])
            for c in range(NKC):
                po = 4 + c % 2
                for half in range(2):
                    wt, wtok = R.get(f"f{f}out{l}_{c}_{half}")
                    wv = wt.rearrange("p (r c) -> p r c", r=22)
                    for r in range(22):
                        hc = half * 22 + r
                        mm(ps[po][:, :tt], wv[:, r, :], hid[:, hc * TT:hc * TT + tt], hc == 0, hc == NHC - 1,
                           reads=[wtok, ("hid", hc)], writes=[PS(po)])
                stat_flush()
                dst = hT[:, c * TT:c * TT + tt]
                s.add("dve", lambda e, dst=dst, po=po: e.scalar_tensor_tensor(
                    out=dst, in0=ps[po][:, :tt], scalar=0.5, in1=dst, op0=ALU.mult, op1=ALU.add),
                    reads=[PS(po), ("hT", c)], writes=[("hT", c)])
                if next_norm:
                    stat_chunk(c, tt)

        def load_rope(src_d, tix, nparts):
            s.add("sp", lambda e: e.dma_start(out=rope[:nparts, :], in_=src_d[tix, :, :]), writes=["rope"], dma=True)

        def fence(tokens):
            s.add("dve", lambda e: e.memset(stt[:, 15:16], 0.0), writes=list(tokens) + [("fence",)])

        U_TOKENS = [("qhT", 0), ("qhT", 1), ("khT", 0), ("khT", 1), ("kn", 0), ("kn", 1), ("knr", 0), ("knr", 1),
                    ("vsb", 0), ("vsb", 1), ("pbuf", 0), ("pbuf", 1)]

        def run(gen):
            for _ in gen:
                pass

        def chain(*gens):
            for g in gens:
                for _ in g:
                    yield

        def interleave(g1, g2, r=1):
            a = b = True
            while a or b:
                if a:
                    try:
                        next(g1)
                    except StopIteration:
                        a = False
                for _ in range(r):
                    if b:
                        try:
                            next(g2)
                        except StopIteration:
                            b = False

        def retention(tix, tt, L, sample):
            NB = tt // L
            rmsnorm(G_MIX[0], tt)
            load_rope(rr_d, tix, 128)
            fence(U_TOKENS)
            cos = rope[:, 0:tt]
            sin = rope[:, TT:TT + tt]
            VT0 = 32 * 512
            SG0 = 36 * 512
            sg = hid[:, SG0:SG0 + 4096].bitcast(F32)

            def bufs(h):
                b = h % 2
                qh = UA[:, b * 1024:(b + 1) * 1024]
                kh = UB[:, b * 1024:(b + 1) * 1024]
                kt = ktok2[:, b * 1024:(b + 1) * 1024]
                return b, qh, kh, kt

            def vt(b, j):
                if b == 0:
                    return hid[:L, VT0 + j * 512:VT0 + (j + 1) * 512], ("hid", 32 + j)
                return vtokB[:L, j * 512:(j + 1) * 512], ("vtokB", j)

            def genA(h):
                b, qh, kh, kt = bufs(h)
                dqb = dqh[:, b * 256:(b + 1) * 256]
                dqtok = ("dqh", b)
                s.add("sp", lambda e: e.dma_start(out=dqb, in_=dqk_d[:, h * 256:(h + 1) * 256]), writes=[dqtok], dma=True)
                for which, dst_t, dsc, nm, dtok in ((0, qh, dqb[:, 0:L], "q", ("qhT", b)), (1, kh, dqb[:, 128:128 + L], "k", ("khT", b))):
                    wt, wtok = R.get(f"rin_{h}_{nm}")
                    wv = wt.rearrange("p (k c) -> p k c", k=NKC)
                    for c in range(2):
                        for kc in range(NKC):
                            mm(ps[c][:, :tt], wv[:, kc, c * 128:(c + 1) * 128], xnT[:, kc * TT:kc * TT + tt],
                               kc == 0, kc == NKC - 1, reads=[wtok, ("xnT", kc)], writes=[PS(c)])
                            if kc % 8 == 7:
                                yield
                    t0 = tmpf[:, 0:tt]
                    t1 = tmpf[:, TT:TT + tt]
                    t2 = tmpf[:, 2 * TT:2 * TT + tt]
                    t3 = tmpf[:, 3 * TT:3 * TT + tt]
                    s.add("dve", lambda e: e.tensor_tensor(out=t0, in0=ps[0][:, :tt], in1=cos, op=ALU.mult),
                          reads=[PS(0), "rope"], writes=[("tmpf", 0)])
                    s.add("dve", lambda e: e.tensor_tensor(out=t1, in0=ps[1][:, :tt], in1=sin, op=ALU.mult),
                          reads=[PS(1), "rope"], writes=[("tmpf", 1)])
                    s.add("dve", lambda e: e.tensor_tensor(out=t2, in0=ps[0][:, :tt], in1=sin, op=ALU.mult),
                          reads=[PS(0), "rope"], writes=[("tmpf", 2)])
                    s.add("dve", lambda e: e.tensor_tensor(out=t3, in0=ps[1][:, :tt], in1=cos, op=ALU.mult),
                          reads=[PS(1), "rope"], writes=[("tmpf", 3)])
                    yield
                    s.add("dve", lambda e: e.tensor_tensor(out=t0, in0=t0, in1=t1, op=ALU.subtract),
                          reads=[("tmpf", 0), ("tmpf", 1)], writes=[("tmpf", 0)])
                    s.add("dve", lambda e: e.tensor_tensor(out=t2, in0=t2, in1=t3, op=ALU.add),
                          reads=[("tmpf", 2), ("tmpf", 3)], writes=[("tmpf", 2)])
                    bc = dsc.unsqueeze(1).to_broadcast([128, NB, L])
                    for c, tsrc, ti in ((0, t0, 0), (1, t2, 2)):
                        dst = dst_t[:, c * TT:c * TT + tt].rearrange("p (b l) -> p b l", b=NB)
                        srcv = tsrc.rearrange("p (b l) -> p b l", b=NB)
                        s.add("dve", lambda e: e.tensor_tensor(out=dst, in0=srcv, in1=bc, op=ALU.mult),
                              reads=[("tmpf", ti), dqtok], writes=[dtok])
                    yield
                psb = ps[2][:, :].bitcast(BF16)
                for j in range(NB):
                    for c in range(2):
                        tr(psb[:L, (j * 2 + c) * 128:(j * 2 + c + 1) * 128], kh[:, c * TT + j * L:c * TT + (j + 1) * L],
                           identb[:, :], reads=[("khT", b), "identb"], writes=[PS(2)])
                s.add("act", lambda e: e.copy(out=kt[:L, :], in_=psb[:L, :]), reads=[PS(2)], writes=[("ktok", b)])
                yield

            def genB(h):
                b = h % 2
                wv0, wtok0 = R.get(f"rin_{h}_v0")
                wv1, wtok1 = R.get(f"rin_{h}_v1")
                wv0v = wv0.rearrange("p (k c) -> p k c", k=NKC)
                wv1v = wv1.rearrange("p (k c) -> p k c", k=NKC)
                for j in range(NB):
                    pv = 2 + j % 2
                    for kc in range(NKC):
                        mm(ps[pv][:L, 0:256], xnT[:, kc * TT + j * L:kc * TT + (j + 1) * L], wv0v[:, kc, :],
                           kc == 0, kc == NKC - 1, reads=[wtok0, ("xnT", kc)], writes=[PS(pv)])
                        if kc % 8 == 7:
                            yield
                    for kc in range(NKC):
                        mm(ps[pv][:L, 256:512], xnT[:, kc * TT + j * L:kc * TT + (j + 1) * L], wv1v[:, kc, :],
                           kc == 0, kc == NKC - 1, reads=[wtok1, ("xnT", kc)], writes=[PS(pv)])
                        if kc % 8 == 7:
                            yield
                    dst, dtk = vt(b, j)
                    s.add("act", lambda e: e.copy(out=dst, in_=ps[pv][:L, :]), reads=[PS(pv)], writes=[dtk])

            def doC(h):
                for gi in range(2):
                    wg, wtokg = R.get(f"rin_{h}_g{gi}")
                    wgv = wg.rearrange("p (k c) -> p k c", k=NKC)
                    for cc in range(2):
                        c = gi * 2 + cc
                        pg = c % 2
                        for kc in range(NKC):
                            mm(ps[pg][:, :tt], wgv[:, kc, cc * 128:(cc + 1) * 128], xnT[:, kc * TT:kc * TT + tt],
                               kc == 0, kc == NKC - 1, reads=[wtokg, ("xnT", kc)], writes=[PS(pg)])
                        dst = sg[:, c * 512:c * 512 + tt]
                        s.add("act", lambda e: e.activation(out=dst, in_=ps[pg][:, :tt], func=AF.Silu),
                              reads=[PS(pg)], writes=[("hid", 36 + 2 * c), ("hid", 37 + 2 * c)])

            def genD(h):
                b, qh, kh, kt = bufs(h)
                gL = GAMMA[h] ** L
                sbuf_i = h % 2
                Sfh = Sf[:, sbuf_i * 1024:(sbuf_i + 1) * 1024]
                Stok = ("Sf", sbuf_i)
                if not sample:
                    if tix == 0:
                        s.add("dve", lambda e: e.memset(Sfh, 0.0), writes=[Stok])
                    else:
                        s.add("sp", lambda e: e.dma_start(
                            out=Sfh.rearrange("p (c v) -> p c v", c=2),
                            in_=retp[h].rearrange("(c p) v -> p c v", p=128)),
                            reads=[("retp", h)], writes=[Stok], dma=True)
                    s.add("act", lambda e: e.copy(out=Sb[:, :], in_=Sfh), reads=[Stok], writes=[("Sb",)])
                psT = ps[4][:, :].bitcast(BF16)
                for j in range(NB):
                    vj, vtk = vt(b, j)
                    if sample:
                        Sfh = Sf[:, (j % 2) * 1024:(j % 2 + 1) * 1024]
                        Stok = ("Sf", j % 2)
                        s.add("sp", lambda e: e.dma_start(
                            out=Sfh.rearrange("p (c v) -> p c v", c=2),
                            in_=sret[j, h].rearrange("(c p) v -> p c v", p=128)),
                            writes=[Stok], dma=True)
                        s.add("act", lambda e: e.copy(out=Sb[:, :], in_=Sfh), reads=[Stok], writes=[("Sb",)])
                    for c in range(2):
                        mm(ps[4][:L, :L], kh[:, c * TT + j * L:c * TT + (j + 1) * L], qh[:, c * TT + j * L:c * TT + (j + 1) * L],
                           c == 0, c == 1, reads=[("khT", b), ("qhT", b)], writes=[PS(4)])
                    s.add("dve", lambda e: e.tensor_tensor(out=innb[:L, :L], in0=ps[4][:L, :L], in1=rmask[:L, :L], op=ALU.mult),
                          reads=[PS(4), "cmat"], writes=[("innb",)])
                    for c in range(2):
                        pk = 6 + c
                        mm(ps[pk][:, :], kt[:L, (j * 2 + c) * 128:(j * 2 + c + 1) * 128], vj,
                           True, True, reads=[("ktok", b), vtk], writes=[PS(pk)])
                    yield
                    mm(ps[5][:L, :], innb[:L, :L], vj, True, False, reads=[("innb",), vtk], writes=[PS(5)])
                    for c in range(2):
                        mm(ps[5][:L, :], qh[:, c * TT + j * L:c * TT + (j + 1) * L], Sb[:, c * 512:(c + 1) * 512], False, c == 1,
                           reads=[("qhT", b), ("Sb",)], writes=[PS(5)])
                    yield
                    s.add("dve", lambda e: e.bn_stats(out=stt[:L, 0:6], in_=ps[5][:L, :]), reads=[PS(5)], writes=[("stt",)])
                    s.add("dve", lambda e: e.bn_aggr(out=stt[:L, 6:8], in_=stt[:L, 0:6]), reads=[("stt",)], writes=[("stt",)])
                    rstd_from(stt[:L, 7:8], stt[:L, 8:9], [("stt",)], [("stt2",)])
                    s.add("dve", lambda e: e.tensor_scalar(out=onb[:L, :], in0=ps[5][:L, :], scalar1=stt[:L, 6:7], scalar2=stt[:L, 8:9],
                                                           op0=ALU.subtract, op1=ALU.mult),
                          reads=[PS(5), ("stt",), ("stt2",)], writes=[("onb",)])
                    for c in range(2):
                        pk = 6 + c
                        Sc = Sfh[:, c * 512:(c + 1) * 512]
                        s.add("dve", lambda e: e.tensor_tensor(out=Sc, in0=Sc, in1=ps[pk][:, :], op=ALU.add),
                              reads=[PS(pk), Stok], writes=[Stok])
                    s.add("act", lambda e: e.mul(out=Sfh, in_=Sfh, mul=float(gL)), reads=[Stok], writes=[Stok])
                    if sample:
                        s.add("sp", lambda e: e.dma_start(
                            out=rets[j, h].rearrange("(c p) v -> p c v", p=128),
                            in_=Sfh.rearrange("p (c v) -> p c v", c=2)),
                            reads=[Stok], dma=True)
                    elif j < NB - 1:
                        s.add("act", lambda e: e.copy(out=Sb[:, :], in_=Sfh), reads=[Stok], writes=[("Sb",)])
                    yield
                    for c in range(4):
                        tr(psT[:, 512 + c * 128:512 + c * 128 + L], onb[:L, c * 128:(c + 1) * 128], identb[:L, :L],
                           reads=[("onb",), "identb"], writes=[PS(4)])
                    for c in range(4):
                        kc = h * 4 + c
                        dst = hid[:, kc * 512 + j * L:kc * 512 + (j + 1) * L]
                        sgs = sg[:, c * 512 + j * L:c * 512 + (j + 1) * L]
                        s.add("dve", lambda e: e.scalar_tensor_tensor(
                            out=dst, in0=psT[:, 512 + c * 128:512 + c * 128 + L], scalar=gains[:, G_GN + kc:G_GN + kc + 1], in1=sgs,
                            op0=ALU.mult, op1=ALU.mult),
                            reads=[PS(4), "gains", ("hid", 36 + 2 * c), ("hid", 37 + 2 * c)], writes=[("hid", kc)])
                    yield
                if not sample:
                    s.add("sp", lambda e: e.dma_start(
                        out=retp[h].rearrange("(c p) v -> p c v", p=128),
                        in_=Sfh.rearrange("p (c v) -> p c v", c=2)),
                        reads=[Stok], writes=[("retp", h)], dma=True)

            run(genA(0))
            run(genB(0))
            doC(0)
            for h in range(RH):
                if h + 1 < RH:
                    interleave(genD(h), chain(genA(h + 1), genB(h + 1)), r=2)
                    doC(h + 1)
                else:
                    run(genD(h))
            for c in range(NKC):
                wt, wtok = R.get(f"rout_{c}")
                wv = wt.rearrange("p (k c) -> p k c", k=32)
                po = c % 2
                for kc in range(32):
                    mm(ps[po][:, :tt], wv[:, kc, :], hid[:, kc * 512:kc * 512 + tt], kc == 0, kc == 31,
                       reads=[wtok, ("hid", kc)], writes=[PS(po)])
                stat_flush()
                dst = hT[:, c * TT:c * TT + tt]
                s.add("dve", lambda e: e.tensor_tensor(out=dst, in0=ps[po][:, :tt], in1=dst, op=ALU.add),
                      reads=[PS(po), ("hT", c)], writes=[("hT", c)])
                stat_chunk(c, tt)

        CF0 = 16 * 512
        KF0 = 24 * 512

        def latent(tix, tt, key0, ckv_rows, kr_rows):
            rmsnorm(G_KVN, tt)
            load_rope(rm_d, tix, 64)
            C = rope[:64, 0:tt]
            Ss = rope[:64, TT:TT + tt]
            cf = hid[:, CF0:CF0 + 4096].bitcast(F32)
            kf = hid[:64, KF0:KF0 + 1024].bitcast(F32)
            w = [R.get("dkv_0"), R.get("dkv_1"), R.get("dkv_2")]
            for c in range(4):
                wt, wtok = w[c // 2]
                wv = wt.rearrange("p (k c) -> p k c", k=NKC)
                pb = c % 2
                for kc in range(NKC):
                    mm(ps[pb][:, :tt], wv[:, kc, (c % 2) * 128:(c % 2 + 1) * 128], xnT[:, kc * TT:kc * TT + tt],
                       kc == 0, kc == NKC - 1, reads=[wtok, ("xnT", kc)], writes=[PS(pb)])
                cfc = cf[:, c * 512:c * 512 + tt]
                ctok = [("hid", 16 + 2 * c), ("hid", 17 + 2 * c)]
                s.add("act", lambda e, cfc=cfc, pb=pb: e.copy(out=cfc, in_=ps[pb][:, :tt]), reads=[PS(pb)], writes=ctok)
                q = sqb[:, (c % 2) * TT:(c % 2) * TT + tt]
                s.add("dve", lambda e, q=q, cfc=cfc: e.tensor_tensor(out=q, in0=cfc, in1=cfc, op=ALU.mult),
                      reads=ctok, writes=[("sqb", c % 2)])
                mm(ps[7][:, :tt], on512, q, c == 0, c == 3, reads=["cmat", ("sqb", c % 2)], writes=[PS(7)])
            rs = rstd[:, TT:TT + tt]
            rstd_from(ps[7][:, :tt], rs, [PS(7)], [("rstd", 1)])
            for c in range(4):
                cfc = cf[:, c * 512:c * 512 + tt]
                ctok = [("hid", 16 + 2 * c), ("hid", 17 + 2 * c)]
                s.add("dve", lambda e, cfc=cfc, c=c: e.scalar_tensor_tensor(
                    out=cfc, in0=cfc, scalar=gains[:, G_KVLAT + c:G_KVLAT + c + 1], in1=rs, op0=ALU.mult, op1=ALU.mult),
                    reads=ctok + [("rstd", 1), "gains"], writes=ctok)
                dst = cT[:, c * SEQ + key0:c * SEQ + key0 + tt]
                s.add("act", lambda e, cfc=cfc, dst=dst: e.copy(out=dst, in_=cfc), reads=ctok, writes=[("cT",)])
            wt, wtok = w[2]
            wv = wt.rearrange("p (k c) -> p k c", k=NKC)
            for i in range(2):
                for kc in range(NKC):
                    mm(ps[2 + i][:64, :tt], wv[:, kc, i * 64:(i + 1) * 64], xnT[:, kc * TT:kc * TT + tt],
                       kc == 0, kc == NKC - 1, reads=[wtok, ("xnT", kc)], writes=[PS(2 + i)])
            t0 = tmpf[:64, 0:tt]
            kff = kf[:, 0:tt]
            ktk = [("hid", 24), ("hid", 25)]
            s.add("dve", lambda e: e.tensor_tensor(out=t0, in0=ps[2][:64, :tt], in1=C, op=ALU.mult),
                  reads=[PS(2), "rope"], writes=[("tmpf", 0)])
            s.add("dve", lambda e: e.tensor_tensor(out=kff, in0=ps[3][:64, :tt], in1=Ss, op=ALU.mult),
                  reads=[PS(3), "rope"], writes=ktk)
            s.add("dve", lambda e: e.tensor_tensor(out=kff, in0=kff, in1=t0, op=ALU.add),
                  reads=ktk + [("tmpf", 0)], writes=ktk)
            s.add("act", lambda e: e.copy(out=krT[:64, key0:key0 + tt], in_=kff), reads=ktk, writes=[("krT",)])
            sqk = tmpf[:64, TT:TT + tt]
            s.add("dve", lambda e: e.tensor_tensor(out=sqk, in0=kff, in1=kff, op=ALU.mult), reads=ktk, writes=[("tmpf", 1)])
            nkt = tt // 128
            if KDBG & 1:
                s.add("dve", lambda e: e.memset(sskr[:, key0 // 128:key0 // 128 + nkt], 0.3), writes=[("sskr",)])
            else:
                for j in range(nkt):
                    mm(ps[6][:, 256 + j:257 + j], sqk[:, j * 128:(j + 1) * 128], on192[:64, 0:1], True, True,
                       reads=[("tmpf", 1), "cmat"], writes=[PS(6)])
                s.add("dve", lambda e: e.tensor_copy(out=sskr[:, key0 // 128:key0 // 128 + nkt], in_=ps[6][:, 256:256 + nkt]),
                      reads=[PS(6)], writes=[("sskr",)])
            for j in range(tt // 128):
                stg = hid[:, (j % 2) * 1024:(j % 2) * 1024 + 1024].bitcast(F32)
                sttok = hidtok((j % 2) * 1024, (j % 2) * 1024 + 1024)
                pb = 4 + j % 2
                for c in range(4):
                    tr(ps[pb][:, c * 128:(c + 1) * 128], cf[:, c * 512 + j * 128:c * 512 + (j + 1) * 128], ident,
                       reads=[("hid", 16 + 2 * c), ("hid", 17 + 2 * c), "cmat"], writes=[PS(pb)])
                s.add("dve", lambda e, stg=stg, pb=pb: e.tensor_copy(out=stg, in_=ps[pb][:, :]), reads=[PS(pb)], writes=sttok)
                s.add("sp", lambda e, stg=stg, j=j: e.dma_start(out=ckv_rows[j * 128:(j + 1) * 128, :], in_=stg),
                      reads=sttok, dma=True)
                stk = hid[:, 2048 + (j % 2) * 128:2048 + (j % 2) * 128 + 128].bitcast(F32)
                stktok = [("hid", 4)]
                tr(ps[6][:, j * 64:(j + 1) * 64], kf[:, j * 128:(j + 1) * 128], ident[:64, :64],
                   reads=ktk + ["cmat"], writes=[PS(6)])
                s.add("dve", lambda e, stk=stk, j=j: e.tensor_copy(out=stk, in_=ps[6][:, j * 64:(j + 1) * 64]),
                      reads=[PS(6)], writes=stktok)
                s.add("sp", lambda e, stk=stk, j=j: e.dma_start(out=kr_rows[j * 128:(j + 1) * 128, :], in_=stk),
                      reads=stktok, dma=True)

        QN0 = 0
        QR0 = 16 * 512
        CS0 = 32 * 512
        KS0 = CS0 + 4 * 1088

        def mla(tix, tt, sample):
            rmsnorm(G_MIX[1], tt)
            load_rope(rm_d, tix, 64)
            C = rope[:64, 0:tt]
            Ss = rope[:64, TT:TT + tt]
            w = [R.get("dq_0"), R.get("dq_1")]
            qf = tmpf
            qlT = hid[:, 32 * 512:36 * 512]
            qltok = hidtok(32 * 512, 36 * 512)
            for c in range(4):
                wt, wtok = w[c // 2]
                wv = wt.rearrange("p (k c) -> p k c", k=NKC)
                pb = c % 2
                for kc in range(NKC):
                    mm(ps[pb][:, :tt], wv[:, kc, (c % 2) * 128:(c % 2 + 1) * 128], xnT[:, kc * TT:kc * TT + tt],
                       kc == 0, kc == NKC - 1, reads=[wtok, ("xnT", kc)], writes=[PS(pb)])
                qfc = qf[:, c * TT:c * TT + tt]
                s.add("act", lambda e, qfc=qfc, pb=pb: e.copy(out=qfc, in_=ps[pb][:, :tt]), reads=[PS(pb)], writes=[("tmpf", c)])
                q = sqb[:, (c % 2) * TT:(c % 2) * TT + tt]
                s.add("dve", lambda e, q=q, qfc=qfc: e.tensor_tensor(out=q, in0=qfc, in1=qfc, op=ALU.mult),
                      reads=[("tmpf", c)], writes=[("sqb", c % 2)])
                mm(ps[7][:, :tt], on512, q, c == 0, c == 3, reads=["cmat", ("sqb", c % 2)], writes=[PS(7)])
            rs = rstd[:, 0:tt]
            rstd_from(ps[7][:, :tt], rs, [PS(7)], [("rstd", 0)])
            for c in range(4):
                qfc = qf[:, c * TT:c * TT + tt]
                dst = qlT[:, c * TT:c * TT + tt]
                s.add("dve", lambda e, qfc=qfc, dst=dst, c=c: e.scalar_tensor_tensor(
                    out=dst, in0=qfc, scalar=gains[:, G_QLAT + c:G_QLAT + c + 1], in1=rs, op0=ALU.mult, op1=ALU.mult),
                    reads=[("tmpf", c), ("rstd", 0), "gains"], writes=qltok)
            for hg in range(4):
                wt, wtok = R.get(f"uq_{hg}")
                wv = wt.rearrange("p (k c) -> p k c", k=4)
                for hl in range(4):
                    h = hg * 4 + hl
                    base = hl * 256
                    for (pb, c0, c1, M) in ((0, 0, 128, 128), (1, 128, 192, 64), (2, 192, 256, 64)):
                        for cc in range(4):
                            mm(ps[pb][:M, :tt], wv[:, cc, base + c0:base + c1], qlT[:, cc * TT:cc * TT + tt],
                               cc == 0, cc == 3, reads=[wtok] + qltok, writes=[PS(pb)])
                    t0 = tmpf[:64, 0:tt]
                    t1 = tmpf[:64, TT:TT + tt]
                    s.add("dve", lambda e: e.tensor_tensor(out=t0, in0=ps[1][:64, :tt], in1=C, op=ALU.mult),
                          reads=[PS(1), "rope"], writes=[("tmpf", 0)])
                    s.add("dve", lambda e: e.tensor_tensor(out=t1, in0=ps[2][:64, :tt], in1=Ss, op=ALU.mult),
                          reads=[PS(2), "rope"], writes=[("tmpf", 1)])
                    s.add("dve", lambda e: e.tensor_tensor(out=t0, in0=t0, in1=t1, op=ALU.add),
                          reads=[("tmpf", 0), ("tmpf", 1)], writes=[("tmpf", 0)])
                    sq0 = sqb[:, 0:tt]
                    sq1 = sqb[:64, TT:TT + tt]
                    s.add("act", lambda e: e.activation(out=sq0, in_=ps[0][:, :tt], func=AF.Square),
                          reads=[PS(0)], writes=[("sqb", 0)])
                    s.add("dve", lambda e: e.tensor_tensor(out=sq1, in0=t0, in1=t0, op=ALU.mult),
                          reads=[("tmpf", 0)], writes=[("sqb", 1)])
                    mm(ps[7][:, :tt], on192, sq0, True, False, reads=["cmat", ("sqb", 0)], writes=[PS(7)])
                    mm(ps[7][:, :tt], on192[:64, :], sq1, False, True, reads=["cmat", ("sqb", 1)], writes=[PS(7)])
                    rstd_from(ps[7][:, :tt], rs, [PS(7)], [("rstd", 0)])
                    dstn = hid[:, QN0 + h * 512:QN0 + h * 512 + tt]
                    dstr = hid[:64, QR0 + h * 512:QR0 + h * 512 + tt]
                    s.add("dve", lambda e, dstn=dstn: e.scalar_tensor_tensor(
                        out=dstn, in0=ps[0][:, :tt], scalar=(gains[:, G_QNOPE:G_QNOPE + 1] if KDBG & 128 else comb[:, 0:1]), in1=rs, op0=ALU.mult, op1=ALU.mult),
                        reads=[PS(0), ("rstd", 0), "comb"], writes=[("hid", h)])
                    s.add("dve", lambda e, dstr=dstr: e.scalar_tensor_tensor(
                        out=dstr, in0=t0, scalar=(gains[:64, G_QROPE:G_QROPE + 1] if KDBG & 128 else comb[:64, 1:2]), in1=rs[:64, :], op0=ALU.mult, op1=ALU.mult),
                        reads=[("tmpf", 0), ("rstd", 0), "comb"], writes=[("hid", 16 + h)])
            fence(U_TOKENS)
            rl = rstd[:, TT:2 * TT]
            iters = []
            if not sample:
                for h in range(MH):
                    for kb in range(tix + 1):
                        iters.append(dict(h=h, q0=0, TQ=tt, csrc=cT, cstride=SEQ, ksrc=krT, k0=kb * 512, nk=512,
                                          diag=(kb == tix), ctk=[("cT",)], ktk=[("krT",)], first=(kb == 0), last=(kb == tix),
                                          load=None, nsteps=4 * (tix + 1)))
            else:
                cTs = hid[:, CS0:CS0 + 4 * 1088]
                krTs = hid[:, KS0:KS0 + 1088]
                alltok = sorted(set(hidtok(CS0, KS0) + hidtok(KS0, KS0 + 1088)))
                for sq_ in range(4):
                    for h in range(MH):
                        for kbi, (k0, nk) in enumerate(((0, 512), (512, 512), (1024, 64))):
                            iters.append(dict(h=h, q0=sq_ * 64, TQ=64, csrc=cTs, cstride=1088, ksrc=krTs, k0=k0, nk=nk,
                                              diag=False, ctk=alltok, ktk=alltok, first=(kbi == 0), last=(kbi == 2),
                                              load=(sq_ if (h == 0 and kbi == 0) else None), nsteps=9))

            def load_cache(sq_):
                q0 = sq_ * 64
                for blk in range(8):
                    cs = Sf[:, (blk % 2) * 1024:(blk % 2) * 1024 + 512]
                    s.add("sp", lambda e: e.dma_start(out=cs, in_=cckv[sq_, blk * 128:(blk + 1) * 128, :]),
                          writes=[("Sf", blk % 2)], dma=True)
                    pb = blk % 2
                    for c in range(4):
                        tr(ps[pb][:, c * 128:(c + 1) * 128], cs[:, c * 128:(c + 1) * 128], ident,
                           reads=[("Sf", blk % 2), "cmat"], writes=[PS(pb)])
                    dst = cTs.rearrange("p (c k) -> p c k", c=4)[:, :, blk * 128:(blk + 1) * 128]
                    s.add("act", lambda e: e.copy(out=dst, in_=ps[pb][:, :].rearrange("p (c k) -> p c k", c=4)),
                          reads=[PS(pb)], writes=alltok)
                krs_ = tmpf[:, 0:512]
                s.add("sp", lambda e: e.dma_start(out=krs_.rearrange("p (b r) -> p b r", b=8),
                                                  in_=ckr[sq_].rearrange("(b p) r -> p b r", p=128)),
                      writes=[("tmpf", 0)], dma=True)
                for blk in range(8):
                    pb = blk // 4
                    tr(ps[pb][:64, (blk % 4) * 128:(blk % 4 + 1) * 128], krs_[:, blk * 64:(blk + 1) * 64], ident,
                       reads=[("tmpf", 0), "cmat"], writes=[PS(pb)])
                for i in range(2):
                    s.add("act", lambda e: e.copy(out=krTs[:64, i * 512:(i + 1) * 512], in_=ps[i][:64, :]),
                          reads=[PS(i)], writes=alltok)
                for c in range(4):
                    s.add("dve", lambda e: e.tensor_copy(out=cTs[:, c * 1088 + 1024:c * 1088 + 1088],
                                                         in_=cT[:, c * SEQ + q0:c * SEQ + q0 + 64]),
                          reads=[("cT",)], writes=alltok)
                s.add("dve", lambda e: e.tensor_copy(out=krTs[:64, 1024:1088], in_=krT[:64, q0:q0 + 64]),
                      reads=[("krT",)], writes=alltok)
                sqc = tmpf[:, TT:TT + 512]
                s.add("dve", lambda e: e.scalar_tensor_tensor(out=sqc, in0=krs_, scalar=float(1.0 / 192), in1=krs_,
                                                              op0=ALU.mult, op1=ALU.mult),
                      reads=[("tmpf", 0)], writes=[("tmpf", 1)])
                s.add("dve", lambda e: e.reduce_sum(out=sskr[:, 0:8], in_=sqc.rearrange("p (b r) -> p b r", b=8),
                                                    axis=mybir.AxisListType.X),
                      reads=[("tmpf", 1)], writes=[("sskr",)])
                sqn = tmpf[:64, 2 * TT:2 * TT + 64]
                s.add("dve", lambda e: e.tensor_tensor(out=sqn, in0=krTs[:64, 1024:1088], in1=krTs[:64, 1024:1088], op=ALU.mult),
                      reads=alltok, writes=[("tmpf", 2)])
                mm(ps[1][:64, 500:501], sqn, on192[:64, 0:1], True, True, reads=[("tmpf", 2), "cmat"], writes=[PS(1)])
                s.add("dve", lambda e: e.tensor_copy(out=sskr[:64, 8:9], in_=ps[1][:64, 500:501]), reads=[PS(1)], writes=[("sskr",)])

            wcur = {}

            def kphase(i, it):
                b2 = i % 2
                if it["load"] is not None:
                    load_cache(it["load"])
                h = it["h"]
                hg, hl = h // 4, h % 4
                if it["first"] and hl == 0:
                    wcur["w"] = R.get(f"ukv_{hg}")
                wt, wtok = wcur["w"]
                wv = wt.rearrange("p (k c) -> p k c", k=4)
                base = hl * 256
                csrc, cstride, ksrc, k0, nk = it["csrc"], it["cstride"], it["ksrc"], it["k0"], it["nk"]
                ctk, ktk = it["ctk"], it["ktk"]
                pk = b2
                knb = kn2[:, b2 * 512:b2 * 512 + nk]
                knrb = knr2[:64, b2 * 512:b2 * 512 + nk]
                for cc in range(4):
                    mm(ps[pk][:, :nk], wv[:, cc, base:base + 128], csrc[:, cc * cstride + k0:cc * cstride + k0 + nk],
                       cc == 0, cc == 3, reads=[wtok] + ctk, writes=[PS(pk)])
                nsub = (nk + 127) // 128
                for ks in range(nsub):
                    nkk = min(128, nk - ks * 128)
                    for cc in range(4):
                        mm(ps[2][:nkk, ks * 128:(ks + 1) * 128],
                           csrc[:, cc * cstride + k0 + ks * 128:cc * cstride + k0 + ks * 128 + nkk],
                           wv[:, cc, base + 128:base + 256], cc == 0, cc == 3, reads=[wtok] + ctk, writes=[PS(2)])
                sq0 = sqb[:, b2 * TT:b2 * TT + nk]
                s.add("dve", lambda e: e.tensor_copy(out=knb, in_=ps[pk][:, :nk]), reads=[PS(pk)], writes=[("kn", b2)])
                s.add("dve", lambda e: e.tensor_tensor(out=sq0, in0=knb, in1=knb, op=ALU.mult),
                      reads=[("kn", b2)], writes=[("sqb", b2)])
                np0 = min(128, nk)
                kt0 = k0 // 128
                rk = rkb[:np0, b2 * 8:b2 * 8 + nsub]
                if KDBG & 4:
                    s.add("dve", lambda e: e.tensor_copy(out=rk, in_=sskr[:np0, kt0:kt0 + nsub]), reads=[("sskr",)], writes=[("rk", b2)])
                else:
                    for ks in range(nsub):
                        nkk = min(128, nk - ks * 128)
                        mm(ps[7][:nkk, b2 * 8 + ks:b2 * 8 + ks + 1], sq0[:, ks * 128:ks * 128 + nkk], on192[:, 0:1], True, True,
                           reads=[("sqb", b2), "cmat"], writes=[PS(7)])
                    s.add("dve", lambda e: e.tensor_tensor(out=rk, in0=ps[7][:np0, b2 * 8:b2 * 8 + nsub], in1=sskr[:np0, kt0:kt0 + nsub],
                                                           op=ALU.add), reads=[PS(7), ("sskr",)], writes=[("rk", b2)])
                if not (KDBG & 32):
                    rstd_from(rk, rk, [("rk", b2)], [("rk", b2)])
                np_ = min(128, nk)
                s.add("act", lambda e: e.copy(out=vsb2[:np_, b2 * 512:b2 * 512 + nsub * 128], in_=ps[2][:np_, :nsub * 128]),
                      reads=[PS(2)], writes=[("vsb", b2)])

            stepc = {}

            def sphase(i, it):
                b2 = i % 2
                h, q0, TQ, nk, diag = it["h"], it["q0"], it["TQ"], it["nk"], it["diag"]
                if it["first"]:
                    stepc["n"] = 0
                qn = hid[:, QN0 + h * 512 + q0:QN0 + h * 512 + q0 + TQ]
                qr = hid[:64, QR0 + h * 512 + q0:QR0 + h * 512 + q0 + TQ]
                nsub = (nk + 127) // 128
                pend = []
                for ks in range(nsub):
                    step = stepc["n"]
                    nkk = min(128, nk - ks * 128)
                    qlo = ks * 128 if diag else 0
                    ncol = TQ - qlo
                    pscore = 3 + step % 2
                    mm(ps[pscore][:nkk, :ncol], kn2[:, b2 * 512 + ks * 128:b2 * 512 + ks * 128 + nkk], qn[:, qlo:TQ], True, False,
                       reads=[("kn", b2), ("hid", h)], writes=[PS(pscore)])
                    ksrc, k0 = it["ksrc"], it["k0"]
                    mm(ps[pscore][:nkk, :ncol], ksrc[:64, k0 + ks * 128:k0 + ks * 128 + nkk], qr[:, qlo:TQ], False, True,
                       reads=it["ktk"] + [("hid", 16 + h)], writes=[PS(pscore)])
                    pb_ = pbuf[:nkk, (step % 2) * 512:(step % 2) * 512 + ncol]
                    ptok = ("pbuf", step % 2)
                    s.add("act", lambda e: e.activation(
                        out=pb_, in_=ps[pscore][:nkk, :ncol], func=AF.Exp,
                        scale=(0.07 if KDBG & 2 else rkb[:nkk, b2 * 8 + ks:b2 * 8 + ks + 1])),
                        reads=[PS(pscore), ("rk", b2)], writes=[ptok])
                    if diag:
                        pd = pbuf[:nkk, (step % 2) * 512:(step % 2) * 512 + 128]
                        s.add("dve", lambda e: e.tensor_tensor(out=pd, in0=pd, in1=amask, op=ALU.mult),
                              reads=[ptok, "cmat"], writes=[ptok])
                    first = step == 0
                    last = step == it["nsteps"] - 1

                    def pv(nkk=nkk, ks=ks, pb_=pb_, ptok=ptok, first=first, last=last, qlo=qlo):
                        mm(ps[5][:, qlo:TQ], vsb2[:nkk, b2 * 512 + ks * 128:b2 * 512 + (ks + 1) * 128], pb_, first, last,
                           reads=[("vsb", b2), ptok], writes=[PS(5)])
                        mm(ps[6][:, qlo:TQ], onesb[:nkk, :], pb_, first, last,
                           reads=["onesb", ptok], writes=[PS(6)])
                    if pend:
                        pend.pop()()
                    pend.append(pv)
                    stepc["n"] = step + 1
                if pend:
                    pend.pop()()
                if it["last"]:
                    assert stepc["n"] == it["nsteps"], (stepc["n"], it["nsteps"])
                    s.add("dve", lambda e: e.reciprocal(out=rl[:, :TQ], in_=ps[6][:, :TQ]), reads=[PS(6)], writes=[("rstd", 1)])
                    dst = xnT[:, h * TT + q0:h * TT + q0 + TQ]
                    s.add("dve", lambda e: e.tensor_tensor(out=dst, in0=ps[5][:, :TQ], in1=rl[:, :TQ], op=ALU.mult),
                          reads=[PS(5), ("rstd", 1)], writes=[("xnT", h)])

            kphase(0, iters[0])
            for i, it in enumerate(iters):
                if i + 1 < len(iters):
                    kphase(i + 1, iters[i + 1])
                if not (KDBG & 16):
                    sphase(i, it)
            for np_ in range(8):
                wt, wtok = R.get(f"wo_{np_}")
                wv = wt.rearrange("p (k c) -> p k c", k=NKC)
                for i in range(2):
                    c = np_ * 2 + i
                    po = c % 2
                    for kc in range(NKC):
                        mm(ps[po][:, :tt], wv[:, kc, i * 128:(i + 1) * 128], xnT[:, kc * TT:kc * TT + tt], kc == 0, kc == NKC - 1,
                           reads=[wtok, ("xnT", kc)], writes=[PS(po)])
                    stat_flush()
                    dst = hT[:, c * TT:c * TT + tt]
                    s.add("dve", lambda e, dst=dst, po=po: e.tensor_tensor(out=dst, in0=ps[po][:, :tt], in1=dst, op=ALU.add),
                          reads=[PS(po), ("hT", c)], writes=[("hT", c)])
                    stat_chunk(c, tt)

        passes = [(t, False) for t in range(n_prompt_tiles)] + ([(NPT, True)] if do_sample else [])
        for (tix, sample) in passes:
            R.plan(pass_weight_names(sample)[:None])
        for (tix, sample) in passes:
            tt = 256 if sample else TT
            if sample:
                load_x(xs, 2)
            else:
                load_x(xp[tix * TT:(tix + 1) * TT, :], 4)
            ffn(0, 1, tt)
            retention(tix, tt, 64 if sample else 128, sample)
            ffn(0, 2, tt)
            if sample:
                latent(tix, tt, 0, ckvs, krs)
            else:
                latent(tix, tt, tix * TT, ckvp[tix * TT:(tix + 1) * TT, :], krp[tix * TT:(tix + 1) * TT, :])
            ffn(1, 1, tt, reuse=True)
            mla(tix, tt, sample)
            ffn(1, 2, tt, next_norm=False)
            if sample:
                store_y(ys, 2)
            else:
                store_y(yp[tix * TT:(tix + 1) * TT, :], 4)
        assert R.pos == len(R.sched), (R.pos, len(R.sched))
        s.emit(sems_eng, lanes)
    return nc


_CACHE = {}


def kernel(**inp):
    inp = {k: np.asarray(v) for k, v in inp.items()}
    W = pack_weights(inp)
    G = pack_gains(inp)
    cm, dqk, rr, rm = const_tables()
    if "nc" not in _CACHE:
        _CACHE["nc"] = build_program()
    nc = _CACHE["nc"]
    in_maps = []
    for c in range(NCORES):
        in_maps.append({
            "xp": np.ascontiguousarray(inp["x_prompt"][c]),
            "xs": np.ascontiguousarray(inp["x_sample"][4 * c:4 * c + 4].reshape(256, D)),
            "sret": np.ascontiguousarray(inp["state_ret"][0, 4 * c:4 * c + 4]),
            "cckv": np.ascontiguousarray(inp["cache_ckv"][4 * c:4 * c + 4]),
            "ckr": np.ascontiguousarray(inp["cache_krope"][4 * c:4 * c + 4]),
            "wts": W, "gains": G, "cmat": cm, "dqk": dqk, "rope_ret": rr, "rope_mla": rm,
        })
    res = run_bass_kernel_spmd(nc, in_maps, core_ids=list(range(NCORES)))
    r = res.results
    y_prompt = np.stack([r[c]["yp"] for c in range(NCORES)], 0)
    y_sample = np.concatenate([r[c]["ys"].reshape(4, DEC_S, D) for c in range(NCORES)], 0)
    ret_p = np.stack([r[c]["retp"] for c in range(NCORES)], 0)[None]
    ckv_p = np.stack([r[c]["ckvp"] for c in range(NCORES)], 0)
    kr_p = np.stack([r[c]["krp"] for c in range(NCORES)], 0)
    ret_s = np.concatenate([r[c]["rets"] for c in range(NCORES)], 0)[None]
    ckv_s = np.concatenate([r[c]["ckvs"].reshape(4, DEC_S, 512) for c in range(NCORES)], 0)
    kr_s = np.concatenate([r[c]["krs"].reshape(4, DEC_S, 64) for c in range(NCORES)], 0)
    return (y_prompt.astype(np.float32), y_sample.astype(np.float32), ret_p.astype(np.float32),
            ckv_p.astype(np.float32), kr_p.astype(np.float32), ret_s.astype(np.float32),
            ckv_s.astype(np.float32), kr_s.astype(np.float32))
```

```python
import math
from contextlib import ExitStack
import numpy as np
import concourse.bass as bass
import concourse.mybir as mybir
from concourse.bass_utils import run_bass_kernel_spmd

F32 = mybir.dt.float32
BF16 = mybir.dt.bfloat16
AF = mybir.ActivationFunctionType
ALU = mybir.AluOpType

D = 2048
DFF = 5632
NKC = 16
NHC = 44
RH = 8
MH = 16
SEQ = 2048
PAST = 1024
DEC_S = 64
TT = 512
NPT = SEQ // TT
EPS = 1e-6
SLOT = 4096
NSLOT = 5
HOLD = 2
NCORES = 8
GAMMA = [1.0 - 2.0 ** (-5.0 - h) for h in range(RH)]

G_F1 = [0, 16]
G_MIX = [32, 48]
G_F2 = [64, 80]
G_KVN = 96
G_KVLAT = 112
G_QLAT = 116
G_GN = 120
G_KNOPE = 152
G_KROPE = 153
G_QNOPE = 154
G_QROPE = 155
NG = 160


def weight_tiles():
    tiles = []
    for l in range(2):
        for f in (1, 2):
            for hc in range(NHC):
                tiles.append((f"f{f}in{l}_{hc}", 4096))
            for nc_ in range(NKC):
                for half in range(2):
                    tiles.append((f"f{f}out{l}_{nc_}_{half}", 2816))
    for h in range(RH):
        for nm in ("q", "k", "v0", "v1", "g0", "g1"):
            tiles.append((f"rin_{h}_{nm}", 4096))
    for nc_ in range(NKC):
        tiles.append((f"rout_{nc_}", 4096))
    tiles += [("dkv_0", 4096), ("dkv_1", 4096), ("dkv_2", 2048)]
    tiles += [("dq_0", 4096), ("dq_1", 4096)]
    for hg in range(4):
        tiles.append((f"uq_{hg}", 4096))
    for hg in range(4):
        tiles.append((f"ukv_{hg}", 4096))
    for np_ in range(8):
        tiles.append((f"wo_{np_}", 4096))
    off = 0
    out = {}
    for nm, e in tiles:
        out[nm] = (off, e)
        off += e
    return out, off


WL, WTOT = weight_tiles()


def pack_weights(inp):
    W = np.empty((128, WTOT), np.float32)

    def put(nm, arr):
        off, e = WL[nm]
        W[:, off:off + e] = arr.reshape(128, e)

    def kc_tile(M):
        K, n = M.shape
        return M.reshape(K // 128, 128, n).transpose(1, 0, 2)

    for l in range(2):
        for f in (1, 2):
            win = inp[f"ffn{f}_w_in"][l]
            a = win.reshape(16, 128, 2, NHC, 128).transpose(1, 3, 0, 2, 4)
            for hc in range(NHC):
                put(f"f{f}in{l}_{hc}", a[:, hc])
            wout = inp[f"ffn{f}_w_out"][l]
            b = wout.reshape(2, 22, 128, 16, 128).transpose(2, 3, 0, 1, 4)
            for nc_ in range(NKC):
                for half in range(2):
                    put(f"f{f}out{l}_{nc_}_{half}", b[:, nc_, half])
    rin = inp["ret_w_in"][0]
    for h in range(RH):
        put(f"rin_{h}_q", kc_tile(rin[:, h * 256:(h + 1) * 256]))
        put(f"rin_{h}_k", kc_tile(rin[:, 2048 + h * 256:2048 + (h + 1) * 256]))
        put(f"rin_{h}_v0", kc_tile(rin[:, 4096 + h * 512:4096 + h * 512 + 256]))
        put(f"rin_{h}_v1", kc_tile(rin[:, 4096 + h * 512 + 256:4096 + (h + 1) * 512]))
        put(f"rin_{h}_g0", kc_tile(rin[:, 8192 + h * 512:8192 + h * 512 + 256]))
        put(f"rin_{h}_g1", kc_tile(rin[:, 8192 + h * 512 + 256:8192 + (h + 1) * 512]))
    rout = inp["ret_w_out"][0]
    for nc_ in range(NKC):
        put(f"rout_{nc_}", kc_tile(rout[:, nc_ * 128:(nc_ + 1) * 128]))
    dkv = inp["w_dkv"]
    put("dkv_0", kc_tile(dkv[:, 0:256]))
    put("dkv_1", kc_tile(dkv[:, 256:512]))
    put("dkv_2", kc_tile(np.concatenate([dkv[:, 512:576], dkv[:, 544:576], dkv[:, 512:544]], 1)))
    dq = inp["w_dq"][0]
    put("dq_0", kc_tile(dq[:, 0:256]))
    put("dq_1", kc_tile(dq[:, 256:512]))
    uq = inp["w_uq"][0].reshape(512, MH, 192)
    uq = np.concatenate([uq, uq[:, :, 160:192], uq[:, :, 128:160]], 2)
    for hg in range(4):
        put(f"uq_{hg}", kc_tile(uq[:, hg * 4:(hg + 1) * 4].reshape(512, 1024)))
    ukv = inp["w_ukv"]
    for hg in range(4):
        put(f"ukv_{hg}", kc_tile(ukv[:, hg * 1024:(hg + 1) * 1024]))
    wo = inp["w_o"][0]
    for np_ in range(8):
        put(f"wo_{np_}", kc_tile(wo[:, np_ * 256:(np_ + 1) * 256]))
    return W


def pack_gains(inp):
    G = np.zeros((128, NG), np.float32)

    def fm(v):
        return v.reshape(-1, 128).T

    for l in range(2):
        G[:, G_F1[l]:G_F1[l] + 16] = fm(inp["ffn1_norm"][l])
        G[:, G_MIX[l]:G_MIX[l] + 16] = fm(inp["mix_norm"][l])
        G[:, G_F2[l]:G_F2[l] + 16] = fm(inp["ffn2_norm"][l])
    G[:, G_KVN:G_KVN + 16] = fm(inp["kv_norm"])
    G[:, G_KVLAT:G_KVLAT + 4] = fm(inp["kv_lat_norm"])
    G[:, G_QLAT:G_QLAT + 4] = fm(inp["q_lat_norm"][0])
    G[:, G_GN:G_GN + 32] = fm(inp["ret_gn_gain"][0])
    G[:, G_KNOPE] = inp["k_norm"][:128]
    G[:64, G_KROPE] = inp["k_norm"][128:]
    G[:, G_QNOPE] = inp["q_norm"][0][:128]
    G[:64, G_QROPE] = inp["q_norm"][0][128:]
    return G


def const_tables():
    cm = np.zeros((128, 6 * 128), np.float32)
    cm[:, 0:128] = np.eye(128)
    cm[:, 128:256] = 1.0 / 2048
    cm[:, 256:384] = 1.0 / 512
    cm[:, 384:512] = 1.0 / 192
    m = np.arange(128)
    cm[:, 512:640] = (m[None, :] >= m[:, None])
    cm[:, 640:768] = ((m[:, None] // 64) <= (m[None, :] // 64))
    dqk = np.zeros((128, RH, 2, 128), np.float32)
    n = np.arange(128, dtype=np.float64)
    for h in range(RH):
        lg = math.log1p(-2.0 ** (-5.0 - h))
        dqk[:, h, 0, :] = (np.exp(lg * (n + 1.0)) / 16.0)[None, :]
        dqk[:, h, 1, :] = np.exp(-lg * (n + 1.0))[None, :]
    pos = np.zeros((NPT + 1, TT), np.float32)
    for t in range(NPT):
        pos[t] = np.arange(t * TT, (t + 1) * TT)
    pos[NPT, :256] = np.tile(PAST + np.arange(DEC_S), 4)
    inv_r = np.power(np.float32(10000.0), -np.arange(128, dtype=np.float32) / np.float32(128))
    inv_m = np.power(np.float32(10000.0), -np.arange(32, dtype=np.float32) / np.float32(32))
    rr = np.zeros((NPT + 1, 128, 2 * TT), np.float32)
    rm = np.zeros((NPT + 1, 64, 2 * TT), np.float32)
    for t in range(NPT + 1):
        ang = (pos[t][None, :] * inv_r[:, None]).astype(np.float32)
        rr[t, :, :TT] = np.cos(ang)
        rr[t, :, TT:] = np.sin(ang)
        angm = (pos[t][None, :] * inv_m[:, None]).astype(np.float32)
        c = np.cos(angm)
        s = np.sin(angm)
        rm[t, :, :TT] = np.concatenate([c, c], 0)
        rm[t, :, TT:] = np.concatenate([-s, s], 0)
    return cm, dqk.reshape(128, RH * 2 * 128), rr, rm


class Op:
    __slots__ = ("eng", "fn", "deps", "needed", "sem", "val", "is_dma", "lane")


class _Rec:
    def __getattr__(self, name):
        def f(*a, **k):
            self.__dict__["call"] = (name, a, k)
            return self
        return f


class Sched:
    def __init__(self, nc, n_lanes):
        self.nc = nc
        self.E = {"pe": nc.tensor, "act": nc.scalar, "dve": nc.vector, "pool": nc.gpsimd, "sp": nc.sync}
        self.ops = []
        self.lw = {}
        self.rd = {}
        self.n_lanes = n_lanes
        self.lane_last = {}
        self.lane_next = 0

    def add(self, eng, fn, reads=(), writes=(), dma=False, lane=None):
        op = Op()
        op.eng = eng
        rec = _Rec()
        fn(rec)
        op.fn = rec.call
        op.is_dma = dma
        op.needed = dma
        op.sem = None
        op.val = 0
        op.lane = None
        deps = []
        for t in reads:
            w = self.lw.get(t)
            if w is not None:
                deps.append(w)
        for t in writes:
            w = self.lw.get(t)
            if w is not None:
                deps.append(w)
            r = self.rd.get(t)
            if r:
                deps.extend(r.values())
        if dma:
            if lane is None:
                lane = self.lane_next
                self.lane_next = (self.lane_next + 1) % self.n_lanes
            op.lane = lane
            prev = self.lane_last.get(lane)
            if prev is not None:
                deps.append(prev)
            self.lane_last[lane] = op
        key = ("dma", op.lane) if dma else eng
        for t in reads:
            self.rd.setdefault(t, {})[key] = op
        for t in writes:
            self.lw[t] = op
            self.rd[t] = {}
        fdeps = []
        seen = set()
        for d in deps:
            if id(d) in seen or d is op:
                continue
            seen.add(id(d))
            if (not d.is_dma) and (not dma) and d.eng == eng and eng == "pe":
                continue
            d.needed = True
            fdeps.append(d)
        op.deps = fdeps
        self.ops.append(op)
        return op

    def emit(self, sems_eng, sems_lane):
        cnt = {e: 0 for e in self.E}
        lane_cnt = {}
        waited = {e: {} for e in self.E}
        for op in self.ops:
            eng = self.E[op.eng]
            need = {}
            for d in op.deps:
                k = id(d.sem)
                if k not in need or need[k][1] < d.val:
                    need[k] = (d.sem, d.val)
            w = waited[op.eng]
            for k, (sem, val) in need.items():
                if w.get(k, 0) < val:
                    eng.wait_ge(sem, val)
                    w[k] = val
            nm_, a_, k_ = op.fn
            inst = getattr(eng, nm_)(*a_, **k_)
            if op.is_dma:
                lane_cnt[op.lane] = lane_cnt.get(op.lane, 0) + 1
                op.sem = sems_lane[op.lane]
                op.val = 16 * lane_cnt[op.lane]
                inst.then_inc(op.sem, 16)
            elif op.needed:
                cnt[op.eng] += 1
                op.sem = sems_eng[op.eng]
                op.val = cnt[op.eng]
                inst.then_inc(op.sem, 1)
        sp = self.E["sp"]
        for lane, c in lane_cnt.items():
            sp.wait_ge(sems_lane[lane], 16 * c)


def build_program(n_prompt_tiles=NPT, do_sample=True, stop_after=None):
    nc = bass.Bass("TRN2", target_bir_lowering=False)

    def din(name, shape):
        return nc.dram_tensor(name, shape, F32, kind="ExternalInput").ap()

    def dout(name, shape):
        return nc.dram_tensor(name, shape, F32, kind="ExternalOutput").ap()

    xp = din("xp", [SEQ, D])
    xs = din("xs", [256, D])
    sret = din("sret", [4, RH, 256, 512])
    cckv = din("cckv", [4, PAST, 512])
    ckr = din("ckr", [4, PAST, 64])
    wts = din("wts", [128, WTOT])
    gains_d = din("gains", [128, NG])
    cmat_d = din("cmat", [128, 768])
    dqk_d = din("dqk", [128, RH * 2 * 128])
    rr_d = din("rope_ret", [NPT + 1, 128, 2 * TT])
    rm_d = din("rope_mla", [NPT + 1, 64, 2 * TT])
    yp = dout("yp", [SEQ, D])
    ys = dout("ys", [256, D])
    retp = dout("retp", [RH, 256, 512])
    ckvp = dout("ckvp", [SEQ, 512])
    krp = dout("krp", [SEQ, 64])
    rets = dout("rets", [4, RH, 256, 512])
    ckvs = dout("ckvs", [256, 512])
    krs = dout("krs", [256, 64])

    N_GEN_LANES = 16
    with ExitStack() as es:
        def sb(name, shape, dt):
            return es.enter_context(nc.sbuf_tensor(name, shape, dt))

        hT = sb("hT", [128, NKC * TT], F32)
        xnT = sb("xnT", [128, NKC * TT], BF16)
        hid = sb("hid", [128, NHC * TT], BF16)
        ring = sb("ring", [128, NSLOT * SLOT], BF16)
        cT = sb("cT", [128, 4 * SEQ], BF16)
        krT = sb("krT", [128, SEQ], BF16)
        gains = sb("gains_sb", [128, NG], F32)
        cmat = sb("cmat_sb", [128, 768], F32)
        dqh = sb("dqh", [128, 2 * 256], F32)
        identb = sb("identb", [128, 128], BF16)
        onesb = sb("onesb", [128, 128], BF16)
        epst = sb("epst", [128, 1], F32)
        rope = sb("rope", [128, 2 * TT], F32)
        rstd = sb("rstd", [128, 2 * TT], F32)
        sqb = sb("sqb", [128, 2 * TT], F32)
        tmpf = sb("tmpf", [128, 4 * TT], F32)
        UA = sb("UA", [128, 2048], BF16)
        UB = sb("UB", [128, 2048], BF16)
        ktok2 = sb("ktok2", [128, 2048], BF16)
        vtokB = sb("vtokB", [128, 2048], BF16)
        innb = sb("innb", [128, 128], BF16)
        onb = sb("onb", [128, 512], BF16)
        Sf = sb("Sf", [128, 2 * 1024], F32)
        Sb = sb("Sb", [128, 1024], BF16)
        stt = sb("stt", [128, 16], F32)
        kn2 = UA[:, 0:1024]
        knr2 = UA[:, 1024:2048]
        vsb2 = UB[:, 0:1024]
        pbuf = UB[:, 1024:2048]
        ps = [es.enter_context(nc.psum_tensor(f"ps{i}", [128, 512], F32)) for i in range(8)]
        sems_eng = {e: es.enter_context(nc.semaphore("s_" + e)) for e in ["pe", "act", "dve", "pool", "sp"]}
        lanes = [es.enter_context(nc.semaphore(f"ln{i}")) for i in range(N_GEN_LANES + NSLOT)]

        s = Sched(nc, N_GEN_LANES)
        ident = cmat[:, 0:128]
        on2048 = cmat[:, 128:256]
        on512 = cmat[:, 256:384]
        on192 = cmat[:, 384:512]
        rmask = cmat[:, 512:640]
        amask = cmat[:, 640:768]

        def hidtok(c0, c1):
            return [("hid", b) for b in range(c0 // 512, (c1 + 511) // 512)]

        def PS(i):
            return ("ps", i)

        s.add("sp", lambda e: e.dma_start(out=gains[:], in_=gains_d[:, :]), writes=["gains"], dma=True)
        s.add("sp", lambda e: e.dma_start(out=cmat[:], in_=cmat_d[:, :]), writes=["cmat"], dma=True)
        s.add("dve", lambda e: e.tensor_copy(out=identb[:], in_=ident), reads=["cmat"], writes=["identb"])
        s.add("dve", lambda e: e.memset(onesb[:], 1.0), writes=["onesb"])
        s.add("dve", lambda e: e.memset(epst[:], EPS), writes=["eps"])

        class Ring:
            def __init__(self):
                self.sched = []
                self.issued = 0
                self.pos = 0

            def plan(self, names):
                self.sched.extend(names)

            def _issue(self, i):
                nm = self.sched[i]
                off, e_ = WL[nm]
                slot = i % NSLOT
                dst = ring[:, slot * SLOT:slot * SLOT + e_]
                src = wts[:, off:off + e_]
                s.add("pool", lambda e, dst=dst, src=src: e.dma_start(out=dst, in_=src),
                      writes=[("ring", slot)], dma=True, lane=N_GEN_LANES + slot)

            def get(self, nm):
                i = self.pos
                assert self.sched[i] == nm, (self.sched[i], nm)
                while self.issued < min(len(self.sched), i + NSLOT - HOLD):
                    self._issue(self.issued)
                    self.issued += 1
                self.pos += 1
                slot = i % NSLOT
                return ring[:, slot * SLOT:slot * SLOT + WL[nm][1]], ("ring", slot)

        R = Ring()

        def pass_weight_names(sample):
            n = []
            for l in range(2):
                n += [f"f1in{l}_{hc}" for hc in range(NHC)]
                n += [f"f1out{l}_{c}_{hf}" for c in range(NKC) for hf in range(2)]
                if l == 0:
                    for h in range(RH):
                        n += [f"rin_{h}_{x}" for x in ("q", "k", "v0", "v1", "g0", "g1")]
                    n += [f"rout_{c}" for c in range(NKC)]
                else:
                    n += ["dq_0", "dq_1"] + [f"uq_{hg}" for hg in range(4)]
                    for _ in range(4 if sample else 1):
                        n += [f"ukv_{hg}" for hg in range(4)]
                    n += [f"wo_{i}" for i in range(8)]
                n += [f"f2in{l}_{hc}" for hc in range(NHC)]
                n += [f"f2out{l}_{c}_{hf}" for c in range(NKC) for hf in range(2)]
                if l == 0:
                    n += ["dkv_0", "dkv_1", "dkv_2"]
            return n

        def mm(out, lhsT, rhs, start, stop, reads, writes):
            s.add("pe", lambda e: e.matmul(out, lhsT, rhs, start=start, stop=stop), reads=reads, writes=writes)

        def warm(bank, n, rows=128):
            for _ in range(n):
                mm(ps[bank][:rows, :512], identb[:, :rows], cT[:, 0:512], True, True, reads=["identb"], writes=[PS(bank)])

        def tr(out, in_, idn, reads, writes):
            s.add("pe", lambda e: e.transpose(out, in_, idn), reads=reads, writes=writes)

        def rstd_from(psum_ap, out_ap, reads, writes):
            s.add("act", lambda e: e.activation(out=out_ap, in_=psum_ap, func=AF.Ln, bias=epst[:psum_ap.shape[0], 0:1], scale=1.0),
                  reads=reads + ["eps"], writes=writes)
            s.add("act", lambda e: e.activation(out=out_ap, in_=out_ap, func=AF.Exp, scale=-0.5),
                  reads=writes, writes=writes)

        def load_x(src_rows, nblk):
            for j in range(nblk):
                st = hid[:, (j % 2) * 4096:(j % 2) * 4096 + 4096].bitcast(F32)
                sttok = hidtok((j % 2) * 4096, (j % 2) * 4096 + 4096)
                s.add("sp", lambda e, st=st, j=j: e.dma_start(out=st, in_=src_rows[j * 128:(j + 1) * 128, :]),
                      writes=sttok, dma=True)
                for g in range(4):
                    b = g % 2
                    for i in range(4):
                        kc = 4 * g + i
                        tr(ps[b][:, i * 128:(i + 1) * 128], st[:, kc * 128:(kc + 1) * 128], ident,
                           reads=sttok + ["cmat"], writes=[PS(b)])
                    dst = hT[:, :].rearrange("p (k t) -> p k t", k=NKC)[:, 4 * g:4 * g + 4, j * 128:(j + 1) * 128]
                    src = ps[b][:, :].rearrange("p (k t) -> p k t", k=4)
                    s.add("act" if g % 2 else "dve",
                          (lambda e, dst=dst, src=src: e.copy(out=dst, in_=src)) if g % 2 else
                          (lambda e, dst=dst, src=src: e.tensor_copy(out=dst, in_=src)),
                          reads=[PS(b)], writes=[("hT", 4 * g + i) for i in range(4)])
                    if j == nblk - 1:
                        for i in range(4):
                            stat_chunk(4 * g + i, nblk * 128, defer=False)

        def store_y(dst_rows, nblk):
            for j in range(nblk):
                st = hid[:, (j % 2) * 4096:(j % 2) * 4096 + 4096].bitcast(F32)
                sttok = hidtok((j % 2) * 4096, (j % 2) * 4096 + 4096)
                for g in range(4):
                    b = g % 2
                    for i in range(4):
                        kc = 4 * g + i
                        tr(ps[b][:, i * 128:(i + 1) * 128], hT[:, kc * TT + j * 128:kc * TT + (j + 1) * 128], ident,
                           reads=[("hT", kc), "cmat"], writes=[PS(b)])
                    dst = st[:, g * 512:(g + 1) * 512]
                    s.add("act" if g % 2 else "dve",
                          (lambda e, dst=dst, b=b: e.copy(out=dst, in_=ps[b][:, :])) if g % 2 else
                          (lambda e, dst=dst, b=b: e.tensor_copy(out=dst, in_=ps[b][:, :])),
                          reads=[PS(b)], writes=sttok)
                s.add("sp", lambda e, st=st, j=j: e.dma_start(out=dst_rows[j * 128:(j + 1) * 128, :], in_=st),
                      reads=sttok, dma=True)

        pend_stat = []

        def stat_flush():
            while pend_stat:
                pend_stat.pop(0)()

        def stat_chunk(kc, tt, defer=True):
            q = sqb[:, (kc % 2) * TT:(kc % 2) * TT + tt]
            src = hT[:, kc * TT:kc * TT + tt]
            s.add("dve", lambda e: e.tensor_tensor(out=q, in0=src, in1=src, op=ALU.mult),
                  reads=[("hT", kc)], writes=[("sqb", kc % 2)])

            def f():
                mm(ps[7][:, :tt], on2048, q, kc == 0, kc == NKC - 1, reads=["cmat", ("sqb", kc % 2)], writes=[PS(7)])
            if defer:
                pend_stat.append(f)
            else:
                f()

        def rmsnorm(gcol, tt, reuse=False):
            stat_flush()
            rs = rstd[:, 0:tt]
            if not reuse:
                rstd_from(ps[7][:, :tt], rs, [PS(7)], [("rstd", 0)])
            for kc in range(NKC):
                src = hT[:, kc * TT:kc * TT + tt]
                dst = xnT[:, kc * TT:kc * TT + tt]
                s.add("dve", lambda e: e.scalar_tensor_tensor(
                    out=dst, in0=src, scalar=gains[:, gcol + kc:gcol + kc + 1], in1=rs, op0=ALU.mult, op1=ALU.mult),
                    reads=[("hT", kc), ("rstd", 0), "gains"], writes=[("xnT", kc)])

        def ffn(l, f, tt, reuse=False, next_norm=True):
            rmsnorm((G_F1 if f == 1 else G_F2)[l], tt, reuse=reuse)
            for hc in range(NHC):
                wt, wtok = R.get(f"f{f}in{l}_{hc}")
                wv = wt.rearrange("p (k a c) -> p k a c", k=NKC, a=2)
                pa, pb = hc % 2, 2 + hc % 2
                for kc in range(NKC):
                    mm(ps[pa][:, :tt], wv[:, kc, 0, :], xnT[:, kc * TT:kc * TT + tt], kc == 0, kc == NKC - 1,
                       reads=[wtok, ("xnT", kc)], writes=[PS(pa)])
                for kc in range(NKC):
                    mm(ps[pb][:, :tt], wv[:, kc, 1, :], xnT[:, kc * TT:kc * TT + tt], kc == 0, kc == NKC - 1,
                       reads=[wtok, ("xnT", kc)], writes=[PS(pb)])
                sa = tmpf[:, (hc % 2) * TT:(hc % 2) * TT + tt]
                s.add("act", lambda e, sa=sa, pa=pa: e.activation(out=sa, in_=ps[pa][:, :tt], func=AF.Silu),
                      reads=[PS(pa)], writes=[("tmpf", hc % 2)])
                dst = hid[:, hc * TT:hc * TT + tt]
                s.add("dve", lambda e, sa=sa, pb=pb, dst=dst: e.tensor_tensor(out=dst, in0=sa, in1=ps[pb][:, :tt], op=ALU.mult),
                      reads=[PS(pb), ("tmpf", hc % 2)], writes=[("hid", hc)])
            for c in range(NKC):
                po = 4 + c % 2
                for half in range(2):
                    wt, wtok = R.get(f"f{f}out{l}_{c}_{half}")
                    wv = wt.rearrange("p (r c) -> p r c", r=22)
                    for r in range(22):
                        hc = half * 22 + r
                        mm(ps[po][:, :tt], wv[:, r, :], hid[:, hc * TT:hc * TT + tt], hc == 0, hc == NHC - 1,
                           reads=[wtok, ("hid", hc)], writes=[PS(po)])
                stat_flush()
                dst = hT[:, c * TT:c * TT + tt]
                s.add("dve", lambda e, dst=dst, po=po: e.scalar_tensor_tensor(
                    out=dst, in0=ps[po][:, :tt], scalar=0.5, in1=dst, op0=ALU.mult, op1=ALU.add),
                    reads=[PS(po), ("hT", c)], writes=[("hT", c)])
                if next_norm:
                    stat_chunk(c, tt)

        def load_rope(src_d, tix, nparts):
            s.add("sp", lambda e: e.dma_start(out=rope[:nparts, :], in_=src_d[tix, :, :]), writes=["rope"], dma=True)

        def fence(tokens):
            s.add("dve", lambda e: e.memset(stt[:, 15:16], 0.0), writes=list(tokens) + [("fence",)])

        U_TOKENS = [("qhT", 0), ("qhT", 1), ("khT", 0), ("khT", 1), ("kn", 0), ("kn", 1), ("knr", 0), ("knr", 1),
                    ("vsb", 0), ("vsb", 1), ("pbuf", 0), ("pbuf", 1)]

        def run(gen):
            for _ in gen:
                pass

        def chain(*gens):
            for g in gens:
                for _ in g:
                    yield

        def interleave(g1, g2, r=1):
            a = b = True
            while a or b:
                if a:
                    try:
                        next(g1)
                    except StopIteration:
                        a = False
                for _ in range(r):
                    if b:
                        try:
                            next(g2)
                        except StopIteration:
                            b = False

        def retention(tix, tt, L, sample):
            NB = tt // L
            rmsnorm(G_MIX[0], tt)
            load_rope(rr_d, tix, 128)
            fence(U_TOKENS)
            cos = rope[:, 0:tt]
            sin = rope[:, TT:TT + tt]
            VT0 = 32 * 512
            SG0 = 36 * 512
            sg = hid[:, SG0:SG0 + 4096].bitcast(F32)

            def bufs(h):
                b = h % 2
                qh = UA[:, b * 1024:(b + 1) * 1024]
                kh = UB[:, b * 1024:(b + 1) * 1024]
                kt = ktok2[:, b * 1024:(b + 1) * 1024]
                return b, qh, kh, kt

            def vt(b, j):
                if b == 0:
                    return hid[:L, VT0 + j * 512:VT0 + (j + 1) * 512], ("hid", 32 + j)
                return vtokB[:L, j * 512:(j + 1) * 512], ("vtokB", j)

            def genA(h):
                b, qh, kh, kt = bufs(h)
                dqb = dqh[:, b * 256:(b + 1) * 256]
                dqtok = ("dqh", b)
                s.add("sp", lambda e: e.dma_start(out=dqb, in_=dqk_d[:, h * 256:(h + 1) * 256]), writes=[dqtok], dma=True)
                for which, dst_t, dsc, nm, dtok in ((0, qh, dqb[:, 0:L], "q", ("qhT", b)), (1, kh, dqb[:, 128:128 + L], "k", ("khT", b))):
                    wt, wtok = R.get(f"rin_{h}_{nm}")
                    wv = wt.rearrange("p (k c) -> p k c", k=NKC)
                    for c in range(2):
                        for kc in range(NKC):
                            mm(ps[c][:, :tt], wv[:, kc, c * 128:(c + 1) * 128], xnT[:, kc * TT:kc * TT + tt],
                               kc == 0, kc == NKC - 1, reads=[wtok, ("xnT", kc)], writes=[PS(c)])
                            if kc % 8 == 7:
                                yield
                    t0 = tmpf[:, 0:tt]
                    t1 = tmpf[:, TT:TT + tt]
                    t2 = tmpf[:, 2 * TT:2 * TT + tt]
                    t3 = tmpf[:, 3 * TT:3 * TT + tt]
                    s.add("dve", lambda e: e.tensor_tensor(out=t0, in0=ps[0][:, :tt], in1=cos, op=ALU.mult),
                          reads=[PS(0), "rope"], writes=[("tmpf", 0)])
                    s.add("dve", lambda e: e.tensor_tensor(out=t1, in0=ps[1][:, :tt], in1=sin, op=ALU.mult),
                          reads=[PS(1), "rope"], writes=[("tmpf", 1)])
                    s.add("dve", lambda e: e.tensor_tensor(out=t2, in0=ps[0][:, :tt], in1=sin, op=ALU.mult),
                          reads=[PS(0), "rope"], writes=[("tmpf", 2)])
                    s.add("dve", lambda e: e.tensor_tensor(out=t3, in0=ps[1][:, :tt], in1=cos, op=ALU.mult),
                          reads=[PS(1), "rope"], writes=[("tmpf", 3)])
                    yield
                    s.add("dve", lambda e: e.tensor_tensor(out=t0, in0=t0, in1=t1, op=ALU.subtract),
                          reads=[("tmpf", 0), ("tmpf", 1)], writes=[("tmpf", 0)])
                    s.add("dve", lambda e: e.tensor_tensor(out=t2, in0=t2, in1=t3, op=ALU.add),
                          reads=[("tmpf", 2), ("tmpf", 3)], writes=[("tmpf", 2)])
                    bc = dsc.unsqueeze(1).to_broadcast([128, NB, L])
                    for c, tsrc, ti in ((0, t0, 0), (1, t2, 2)):
                        dst = dst_t[:, c * TT:c * TT + tt].rearrange("p (b l) -> p b l", b=NB)
                        srcv = tsrc.rearrange("p (b l) -> p b l", b=NB)
                        s.add("dve", lambda e: e.tensor_tensor(out=dst, in0=srcv, in1=bc, op=ALU.mult),
                              reads=[("tmpf", ti), dqtok], writes=[dtok])
                    yield
                psb = ps[2][:, :].bitcast(BF16)
                for j in range(NB):
                    for c in range(2):
                        tr(psb[:L, (j * 2 + c) * 128:(j * 2 + c + 1) * 128], kh[:, c * TT + j * L:c * TT + (j + 1) * L],
                           identb[:, :], reads=[("khT", b), "identb"], writes=[PS(2)])
                s.add("act", lambda e: e.copy(out=kt[:L, :], in_=psb[:L, :]), reads=[PS(2)], writes=[("ktok", b)])
                yield

            def genB(h):
                b = h % 2
                wv0, wtok0 = R.get(f"rin_{h}_v0")
                wv1, wtok1 = R.get(f"rin_{h}_v1")
                wv0v = wv0.rearrange("p (k c) -> p k c", k=NKC)
                wv1v = wv1.rearrange("p (k c) -> p k c", k=NKC)
                for j in range(NB):
                    pv = 2 + j % 2
                    for kc in range(NKC):
                        mm(ps[pv][:L, 0:256], xnT[:, kc * TT + j * L:kc * TT + (j + 1) * L], wv0v[:, kc, :],
                           kc == 0, kc == NKC - 1, reads=[wtok0, ("xnT", kc)], writes=[PS(pv)])
                        if kc % 8 == 7:
                            yield
                    for kc in range(NKC):
                        mm(ps[pv][:L, 256:512], xnT[:, kc * TT + j * L:kc * TT + (j + 1) * L], wv1v[:, kc, :],
                           kc == 0, kc == NKC - 1, reads=[wtok1, ("xnT", kc)], writes=[PS(pv)])
                        if kc % 8 == 7:
                            yield
                    dst, dtk = vt(b, j)
                    s.add("act", lambda e: e.copy(out=dst, in_=ps[pv][:L, :]), reads=[PS(pv)], writes=[dtk])

            def doC(h):
                for gi in range(2):
                    wg, wtokg = R.get(f"rin_{h}_g{gi}")
                    wgv = wg.rearrange("p (k c) -> p k c", k=NKC)
                    for cc in range(2):
                        c = gi * 2 + cc
                        pg = c % 2
                        for kc in range(NKC):
                            mm(ps[pg][:, :tt], wgv[:, kc, cc * 128:(cc + 1) * 128], xnT[:, kc * TT:kc * TT + tt],
                               kc == 0, kc == NKC - 1, reads=[wtokg, ("xnT", kc)], writes=[PS(pg)])
                        dst = sg[:, c * 512:c * 512 + tt]
                        s.add("act", lambda e: e.activation(out=dst, in_=ps[pg][:, :tt], func=AF.Silu),
                              reads=[PS(pg)], writes=[("hid", 36 + 2 * c), ("hid", 37 + 2 * c)])

            def genD(h):
                b, qh, kh, kt = bufs(h)
                gL = GAMMA[h] ** L
                sbuf_i = h % 2
                Sfh = Sf[:, sbuf_i * 1024:(sbuf_i + 1) * 1024]
                Stok = ("Sf", sbuf_i)
                if not sample:
                    if tix == 0:
                        s.add("dve", lambda e: e.memset(Sfh, 0.0), writes=[Stok])
                    else:
                        s.add("sp", lambda e: e.dma_start(
                            out=Sfh.rearrange("p (c v) -> p c v", c=2),
                            in_=retp[h].rearrange("(c p) v -> p c v", p=128)),
                            reads=[("retp", h)], writes=[Stok], dma=True)
                    s.add("act", lambda e: e.copy(out=Sb[:, :], in_=Sfh), reads=[Stok], writes=[("Sb",)])
                psT = ps[4][:, :].bitcast(BF16)
                for j in range(NB):
                    vj, vtk = vt(b, j)
                    if sample:
                        Sfh = Sf[:, (j % 2) * 1024:(j % 2 + 1) * 1024]
                        Stok = ("Sf", j % 2)
                        s.add("sp", lambda e: e.dma_start(
                            out=Sfh.rearrange("p (c v) -> p c v", c=2),
                            in_=sret[j, h].rearrange("(c p) v -> p c v", p=128)),
                            writes=[Stok], dma=True)
                        s.add("act", lambda e: e.copy(out=Sb[:, :], in_=Sfh), reads=[Stok], writes=[("Sb",)])
                    for c in range(2):
                        mm(ps[4][:L, :L], kh[:, c * TT + j * L:c * TT + (j + 1) * L], qh[:, c * TT + j * L:c * TT + (j + 1) * L],
                           c == 0, c == 1, reads=[("khT", b), ("qhT", b)], writes=[PS(4)])
                    s.add("dve", lambda e: e.tensor_tensor(out=innb[:L, :L], in0=ps[4][:L, :L], in1=rmask[:L, :L], op=ALU.mult),
                          reads=[PS(4), "cmat"], writes=[("innb",)])
                    for c in range(2):
                        pk = 6 + c
                        mm(ps[pk][:, :], kt[:L, (j * 2 + c) * 128:(j * 2 + c + 1) * 128], vj,
                           True, True, reads=[("ktok", b), vtk], writes=[PS(pk)])
                    yield
                    warm(5, 3, rows=L)
                    mm(ps[5][:L, :], innb[:L, :L], vj, True, False, reads=[("innb",), vtk], writes=[PS(5)])
                    for c in range(2):
                        mm(ps[5][:L, :], qh[:, c * TT + j * L:c * TT + (j + 1) * L], Sb[:, c * 512:(c + 1) * 512], False, c == 1,
                           reads=[("qhT", b), ("Sb",)], writes=[PS(5)])
                    yield
                    s.add("dve", lambda e: e.bn_stats(out=stt[:L, 0:6], in_=ps[5][:L, :]), reads=[PS(5)], writes=[("stt",)])
                    s.add("dve", lambda e: e.bn_aggr(out=stt[:L, 6:8], in_=stt[:L, 0:6]), reads=[("stt",)], writes=[("stt",)])
                    rstd_from(stt[:L, 7:8], stt[:L, 8:9], [("stt",)], [("stt2",)])
                    s.add("dve", lambda e: e.tensor_scalar(out=onb[:L, :], in0=ps[5][:L, :], scalar1=stt[:L, 6:7], scalar2=stt[:L, 8:9],
                                                           op0=ALU.subtract, op1=ALU.mult),
                          reads=[PS(5), ("stt",), ("stt2",)], writes=[("onb",)])
                    for c in range(2):
                        pk = 6 + c
                        Sc = Sfh[:, c * 512:(c + 1) * 512]
                        s.add("dve", lambda e: e.tensor_tensor(out=Sc, in0=Sc, in1=ps[pk][:, :], op=ALU.add),
                              reads=[PS(pk), Stok], writes=[Stok])
                    s.add("act", lambda e: e.mul(out=Sfh, in_=Sfh, mul=float(gL)), reads=[Stok], writes=[Stok])
                    if sample:
                        s.add("sp", lambda e: e.dma_start(
                            out=rets[j, h].rearrange("(c p) v -> p c v", p=128),
                            in_=Sfh.rearrange("p (c v) -> p c v", c=2)),
                            reads=[Stok], dma=True)
                    elif j < NB - 1:
                        s.add("act", lambda e: e.copy(out=Sb[:, :], in_=Sfh), reads=[Stok], writes=[("Sb",)])
                    yield
                    for c in range(4):
                        tr(psT[:, 512 + c * 128:512 + c * 128 + L], onb[:L, c * 128:(c + 1) * 128], identb[:L, :L],
                           reads=[("onb",), "identb"], writes=[PS(4)])
                    for c in range(4):
                        kc = h * 4 + c
                        dst = hid[:, kc * 512 + j * L:kc * 512 + (j + 1) * L]
                        sgs = sg[:, c * 512 + j * L:c * 512 + (j + 1) * L]
                        s.add("dve", lambda e: e.scalar_tensor_tensor(
                            out=dst, in0=psT[:, 512 + c * 128:512 + c * 128 + L], scalar=gains[:, G_GN + kc:G_GN + kc + 1], in1=sgs,
                            op0=ALU.mult, op1=ALU.mult),
                            reads=[PS(4), "gains", ("hid", 36 + 2 * c), ("hid", 37 + 2 * c)], writes=[("hid", kc)])
                    yield
                if not sample:
                    s.add("sp", lambda e: e.dma_start(
                        out=retp[h].rearrange("(c p) v -> p c v", p=128),
                        in_=Sfh.rearrange("p (c v) -> p c v", c=2)),
                        reads=[Stok], writes=[("retp", h)], dma=True)

            run(genA(0))
            run(genB(0))
            doC(0)
            for h in range(RH):
                if h + 1 < RH:
                    interleave(genD(h), chain(genA(h + 1), genB(h + 1)), r=2)
                    doC(h + 1)
                else:
                    run(genD(h))
            for c in range(NKC):
                wt, wtok = R.get(f"rout_{c}")
                wv = wt.rearrange("p (k c) -> p k c", k=32)
                po = c % 2
                for kc in range(32):
                    mm(ps[po][:, :tt], wv[:, kc, :], hid[:, kc * 512:kc * 512 + tt], kc == 0, kc == 31,
                       reads=[wtok, ("hid", kc)], writes=[PS(po)])
                stat_flush()
                dst = hT[:, c * TT:c * TT + tt]
                s.add("dve", lambda e: e.tensor_tensor(out=dst, in0=ps[po][:, :tt], in1=dst, op=ALU.add),
                      reads=[PS(po), ("hT", c)], writes=[("hT", c)])
                stat_chunk(c, tt)

        CF0 = 16 * 512
        KF0 = 24 * 512

        def latent(tix, tt, key0, ckv_rows, kr_rows):
            rmsnorm(G_KVN, tt)
            load_rope(rm_d, tix, 64)
            C = rope[:64, 0:tt]
            Ss = rope[:64, TT:TT + tt]
            cf = hid[:, CF0:CF0 + 4096].bitcast(F32)
            kf = hid[:64, KF0:KF0 + 1024].bitcast(F32)
            w = [R.get("dkv_0"), R.get("dkv_1"), R.get("dkv_2")]
            for c in range(4):
                wt, wtok = w[c // 2]
                wv = wt.rearrange("p (k c) -> p k c", k=NKC)
                pb = c % 2
                for kc in range(NKC):
                    mm(ps[pb][:, :tt], wv[:, kc, (c % 2) * 128:(c % 2 + 1) * 128], xnT[:, kc * TT:kc * TT + tt],
                       kc == 0, kc == NKC - 1, reads=[wtok, ("xnT", kc)], writes=[PS(pb)])
                cfc = cf[:, c * 512:c * 512 + tt]
                ctok = [("hid", 16 + 2 * c), ("hid", 17 + 2 * c)]
                s.add("act", lambda e, cfc=cfc, pb=pb: e.copy(out=cfc, in_=ps[pb][:, :tt]), reads=[PS(pb)], writes=ctok)
                q = sqb[:, (c % 2) * TT:(c % 2) * TT + tt]
                s.add("dve", lambda e, q=q, cfc=cfc: e.tensor_tensor(out=q, in0=cfc, in1=cfc, op=ALU.mult),
                      reads=ctok, writes=[("sqb", c % 2)])
                mm(ps[7][:, :tt], on512, q, c == 0, c == 3, reads=["cmat", ("sqb", c % 2)], writes=[PS(7)])
            rs = rstd[:, TT:TT + tt]
            rstd_from(ps[7][:, :tt], rs, [PS(7)], [("rstd", 1)])
            for c in range(4):
                cfc = cf[:, c * 512:c * 512 + tt]
                ctok = [("hid", 16 + 2 * c), ("hid", 17 + 2 * c)]
                s.add("dve", lambda e, cfc=cfc, c=c: e.scalar_tensor_tensor(
                    out=cfc, in0=cfc, scalar=gains[:, G_KVLAT + c:G_KVLAT + c + 1], in1=rs, op0=ALU.mult, op1=ALU.mult),
                    reads=ctok + [("rstd", 1), "gains"], writes=ctok)
                dst = cT[:, c * SEQ + key0:c * SEQ + key0 + tt]
                s.add("act", lambda e, cfc=cfc, dst=dst: e.copy(out=dst, in_=cfc), reads=ctok, writes=[("cT",)])
            wt, wtok = w[2]
            wv = wt.rearrange("p (k c) -> p k c", k=NKC)
            for i in range(2):
                for kc in range(NKC):
                    mm(ps[2 + i][:64, :tt], wv[:, kc, i * 64:(i + 1) * 64], xnT[:, kc * TT:kc * TT + tt],
                       kc == 0, kc == NKC - 1, reads=[wtok, ("xnT", kc)], writes=[PS(2 + i)])
            t0 = tmpf[:64, 0:tt]
            kff = kf[:, 0:tt]
            ktk = [("hid", 24), ("hid", 25)]
            s.add("dve", lambda e: e.tensor_tensor(out=t0, in0=ps[2][:64, :tt], in1=C, op=ALU.mult),
                  reads=[PS(2), "rope"], writes=[("tmpf", 0)])
            s.add("dve", lambda e: e.tensor_tensor(out=kff, in0=ps[3][:64, :tt], in1=Ss, op=ALU.mult),
                  reads=[PS(3), "rope"], writes=ktk)
            s.add("dve", lambda e: e.tensor_tensor(out=kff, in0=kff, in1=t0, op=ALU.add),
                  reads=ktk + [("tmpf", 0)], writes=ktk)
            s.add("act", lambda e: e.copy(out=krT[:64, key0:key0 + tt], in_=kff), reads=ktk, writes=[("krT",)])
            for j in range(tt // 128):
                stg = hid[:, (j % 2) * 1024:(j % 2) * 1024 + 1024].bitcast(F32)
                sttok = hidtok((j % 2) * 1024, (j % 2) * 1024 + 1024)
                pb = 4 + j % 2
                for c in range(4):
                    tr(ps[pb][:, c * 128:(c + 1) * 128], cf[:, c * 512 + j * 128:c * 512 + (j + 1) * 128], ident,
                       reads=[("hid", 16 + 2 * c), ("hid", 17 + 2 * c), "cmat"], writes=[PS(pb)])
                s.add("dve", lambda e, stg=stg, pb=pb: e.tensor_copy(out=stg, in_=ps[pb][:, :]), reads=[PS(pb)], writes=sttok)
                s.add("sp", lambda e, stg=stg, j=j: e.dma_start(out=ckv_rows[j * 128:(j + 1) * 128, :], in_=stg),
                      reads=sttok, dma=True)
                stk = hid[:, 2048 + (j % 2) * 128:2048 + (j % 2) * 128 + 128].bitcast(F32)
                stktok = [("hid", 4)]
                tr(ps[6][:, j * 64:(j + 1) * 64], kf[:, j * 128:(j + 1) * 128], ident[:64, :64],
                   reads=ktk + ["cmat"], writes=[PS(6)])
                s.add("dve", lambda e, stk=stk, j=j: e.tensor_copy(out=stk, in_=ps[6][:, j * 64:(j + 1) * 64]),
                      reads=[PS(6)], writes=stktok)
                s.add("sp", lambda e, stk=stk, j=j: e.dma_start(out=kr_rows[j * 128:(j + 1) * 128, :], in_=stk),
                      reads=stktok, dma=True)

        QN0 = 0
        QR0 = 16 * 512
        CS0 = 32 * 512
        KS0 = CS0 + 4 * 1088

        def mla(tix, tt, sample):
            rmsnorm(G_MIX[1], tt)
            load_rope(rm_d, tix, 64)
            C = rope[:64, 0:tt]
            Ss = rope[:64, TT:TT + tt]
            w = [R.get("dq_0"), R.get("dq_1")]
            qf = tmpf
            qlT = hid[:, 32 * 512:36 * 512]
            qltok = hidtok(32 * 512, 36 * 512)
            for c in range(4):
                wt, wtok = w[c // 2]
                wv = wt.rearrange("p (k c) -> p k c", k=NKC)
                pb = c % 2
                for kc in range(NKC):
                    mm(ps[pb][:, :tt], wv[:, kc, (c % 2) * 128:(c % 2 + 1) * 128], xnT[:, kc * TT:kc * TT + tt],
                       kc == 0, kc == NKC - 1, reads=[wtok, ("xnT", kc)], writes=[PS(pb)])
                qfc = qf[:, c * TT:c * TT + tt]
                s.add("act", lambda e, qfc=qfc, pb=pb: e.copy(out=qfc, in_=ps[pb][:, :tt]), reads=[PS(pb)], writes=[("tmpf", c)])
                q = sqb[:, (c % 2) * TT:(c % 2) * TT + tt]
                s.add("dve", lambda e, q=q, qfc=qfc: e.tensor_tensor(out=q, in0=qfc, in1=qfc, op=ALU.mult),
                      reads=[("tmpf", c)], writes=[("sqb", c % 2)])
                mm(ps[7][:, :tt], on512, q, c == 0, c == 3, reads=["cmat", ("sqb", c % 2)], writes=[PS(7)])
            rs = rstd[:, 0:tt]
            rstd_from(ps[7][:, :tt], rs, [PS(7)], [("rstd", 0)])
            for c in range(4):
                qfc = qf[:, c * TT:c * TT + tt]
                dst = qlT[:, c * TT:c * TT + tt]
                s.add("dve", lambda e, qfc=qfc, dst=dst, c=c: e.scalar_tensor_tensor(
                    out=dst, in0=qfc, scalar=gains[:, G_QLAT + c:G_QLAT + c + 1], in1=rs, op0=ALU.mult, op1=ALU.mult),
                    reads=[("tmpf", c), ("rstd", 0), "gains"], writes=qltok)
            for hg in range(4):
                wt, wtok = R.get(f"uq_{hg}")
                wv = wt.rearrange("p (k c) -> p k c", k=4)
                for hl in range(4):
                    h = hg * 4 + hl
                    base = hl * 256
                    for (pb, c0, c1, M) in ((0, 0, 128, 128), (1, 128, 192, 64), (2, 192, 256, 64)):
                        for cc in range(4):
                            mm(ps[pb][:M, :tt], wv[:, cc, base + c0:base + c1], qlT[:, cc * TT:cc * TT + tt],
                               cc == 0, cc == 3, reads=[wtok] + qltok, writes=[PS(pb)])
                    t0 = tmpf[:64, 0:tt]
                    t1 = tmpf[:64, TT:TT + tt]
                    s.add("dve", lambda e: e.tensor_tensor(out=t0, in0=ps[1][:64, :tt], in1=C, op=ALU.mult),
                          reads=[PS(1), "rope"], writes=[("tmpf", 0)])
                    s.add("dve", lambda e: e.tensor_tensor(out=t1, in0=ps[2][:64, :tt], in1=Ss, op=ALU.mult),
                          reads=[PS(2), "rope"], writes=[("tmpf", 1)])
                    s.add("dve", lambda e: e.tensor_tensor(out=t0, in0=t0, in1=t1, op=ALU.add),
                          reads=[("tmpf", 0), ("tmpf", 1)], writes=[("tmpf", 0)])
                    sq0 = sqb[:, 0:tt]
                    sq1 = sqb[:64, TT:TT + tt]
                    s.add("act", lambda e: e.activation(out=sq0, in_=ps[0][:, :tt], func=AF.Square),
                          reads=[PS(0)], writes=[("sqb", 0)])
                    s.add("dve", lambda e: e.tensor_tensor(out=sq1, in0=t0, in1=t0, op=ALU.mult),
                          reads=[("tmpf", 0)], writes=[("sqb", 1)])
                    mm(ps[7][:, :tt], on192, sq0, True, False, reads=["cmat", ("sqb", 0)], writes=[PS(7)])
                    mm(ps[7][:, :tt], on192[:64, :], sq1, False, True, reads=["cmat", ("sqb", 1)], writes=[PS(7)])
                    rstd_from(ps[7][:, :tt], rs, [PS(7)], [("rstd", 0)])
                    dstn = hid[:, QN0 + h * 512:QN0 + h * 512 + tt]
                    dstr = hid[:64, QR0 + h * 512:QR0 + h * 512 + tt]
                    s.add("dve", lambda e, dstn=dstn: e.scalar_tensor_tensor(
                        out=dstn, in0=ps[0][:, :tt], scalar=gains[:, G_QNOPE:G_QNOPE + 1], in1=rs, op0=ALU.mult, op1=ALU.mult),
                        reads=[PS(0), ("rstd", 0), "gains"], writes=[("hid", h)])
                    s.add("dve", lambda e, dstr=dstr: e.scalar_tensor_tensor(
                        out=dstr, in0=t0, scalar=gains[:64, G_QROPE:G_QROPE + 1], in1=rs[:64, :], op0=ALU.mult, op1=ALU.mult),
                        reads=[("tmpf", 0), ("rstd", 0), "gains"], writes=[("hid", 16 + h)])
            fence(U_TOKENS)
            rl = rstd[:, TT:2 * TT]
            iters = []
            if not sample:
                for h in range(MH):
                    for kb in range(tix + 1):
                        iters.append(dict(h=h, q0=0, TQ=tt, csrc=cT, cstride=SEQ, ksrc=krT, k0=kb * 512, nk=512,
                                          diag=(kb == tix), ctk=[("cT",)], ktk=[("krT",)], first=(kb == 0), last=(kb == tix),
                                          load=None, nsteps=4 * (tix + 1)))
            else:
                cTs = hid[:, CS0:CS0 + 4 * 1088]
                krTs = hid[:, KS0:KS0 + 1088]
                alltok = sorted(set(hidtok(CS0, KS0) + hidtok(KS0, KS0 + 1088)))
                for sq_ in range(4):
                    for h in range(MH):
                        for kbi, (k0, nk) in enumerate(((0, 512), (512, 512), (1024, 64))):
                            iters.append(dict(h=h, q0=sq_ * 64, TQ=64, csrc=cTs, cstride=1088, ksrc=krTs, k0=k0, nk=nk,
                                              diag=False, ctk=alltok, ktk=alltok, first=(kbi == 0), last=(kbi == 2),
                                              load=(sq_ if (h == 0 and kbi == 0) else None), nsteps=9))

            def load_cache(sq_):
                q0 = sq_ * 64
                for blk in range(8):
                    cs = Sf[:, (blk % 2) * 1024:(blk % 2) * 1024 + 512]
                    s.add("sp", lambda e: e.dma_start(out=cs, in_=cckv[sq_, blk * 128:(blk + 1) * 128, :]),
                          writes=[("Sf", blk % 2)], dma=True)
                    pb = blk % 2
                    for c in range(4):
                        tr(ps[pb][:, c * 128:(c + 1) * 128], cs[:, c * 128:(c + 1) * 128], ident,
                           reads=[("Sf", blk % 2), "cmat"], writes=[PS(pb)])
                    dst = cTs.rearrange("p (c k) -> p c k", c=4)[:, :, blk * 128:(blk + 1) * 128]
                    s.add("act", lambda e: e.copy(out=dst, in_=ps[pb][:, :].rearrange("p (c k) -> p c k", c=4)),
                          reads=[PS(pb)], writes=alltok)
                krs_ = tmpf[:, 0:512]
                s.add("sp", lambda e: e.dma_start(out=krs_.rearrange("p (b r) -> p b r", b=8),
                                                  in_=ckr[sq_].rearrange("(b p) r -> p b r", p=128)),
                      writes=[("tmpf", 0)], dma=True)
                for blk in range(8):
                    pb = blk // 4
                    tr(ps[pb][:64, (blk % 4) * 128:(blk % 4 + 1) * 128], krs_[:, blk * 64:(blk + 1) * 64], ident,
                       reads=[("tmpf", 0), "cmat"], writes=[PS(pb)])
                for i in range(2):
                    s.add("act", lambda e: e.copy(out=krTs[:64, i * 512:(i + 1) * 512], in_=ps[i][:64, :]),
                          reads=[PS(i)], writes=alltok)
                for c in range(4):
                    s.add("dve", lambda e: e.tensor_copy(out=cTs[:, c * 1088 + 1024:c * 1088 + 1088],
                                                         in_=cT[:, c * SEQ + q0:c * SEQ + q0 + 64]),
                          reads=[("cT",)], writes=alltok)
                s.add("dve", lambda e: e.tensor_copy(out=krTs[:64, 1024:1088], in_=krT[:64, q0:q0 + 64]),
                      reads=[("krT",)], writes=alltok)

            wcur = {}

            def kphase(i, it):
                b2 = i % 2
                if it["load"] is not None:
                    load_cache(it["load"])
                h = it["h"]
                hg, hl = h // 4, h % 4
                if it["first"] and hl == 0:
                    wcur["w"] = R.get(f"ukv_{hg}")
                wt, wtok = wcur["w"]
                wv = wt.rearrange("p (k c) -> p k c", k=4)
                base = hl * 256
                csrc, cstride, ksrc, k0, nk = it["csrc"], it["cstride"], it["ksrc"], it["k0"], it["nk"]
                ctk, ktk = it["ctk"], it["ktk"]
                pk = b2
                knb = kn2[:, b2 * 512:b2 * 512 + nk]
                knrb = knr2[:64, b2 * 512:b2 * 512 + nk]
                for cc in range(4):
                    mm(ps[pk][:, :nk], wv[:, cc, base:base + 128], csrc[:, cc * cstride + k0:cc * cstride + k0 + nk],
                       cc == 0, cc == 3, reads=[wtok] + ctk, writes=[PS(pk)])
                nsub = (nk + 127) // 128
                for ks in range(nsub):
                    nkk = min(128, nk - ks * 128)
                    for cc in range(4):
                        mm(ps[2][:nkk, ks * 128:(ks + 1) * 128],
                           csrc[:, cc * cstride + k0 + ks * 128:cc * cstride + k0 + ks * 128 + nkk],
                           wv[:, cc, base + 128:base + 256], cc == 0, cc == 3, reads=[wtok] + ctk, writes=[PS(2)])
                sq0 = sqb[:, 0:nk]
                sq1 = sqb[:64, TT:TT + nk]
                s.add("act", lambda e: e.activation(out=sq0, in_=ps[pk][:, :nk], func=AF.Square),
                      reads=[PS(pk)], writes=[("sqb", 0)])
                ksl = ksrc[:64, k0:k0 + nk]
                s.add("dve", lambda e: e.tensor_tensor(out=sq1, in0=ksl, in1=ksl, op=ALU.mult),
                      reads=ktk, writes=[("sqb", 1)])
                warm(7, 4)
                mm(ps[7][:, :nk], on192, sq0, True, False, reads=["cmat", ("sqb", 0)], writes=[PS(7)])
                mm(ps[7][:, :nk], on192[:64, :], sq1, False, True, reads=["cmat", ("sqb", 1)], writes=[PS(7)])
                rk = rstd[:, 0:nk]
                rstd_from(ps[7][:, :nk], rk, [PS(7)], [("rstd", 0)])
                s.add("dve", lambda e: e.scalar_tensor_tensor(
                    out=knb, in0=ps[pk][:, :nk], scalar=gains[:, G_KNOPE:G_KNOPE + 1], in1=rk,
                    op0=ALU.mult, op1=ALU.mult), reads=[PS(pk), ("rstd", 0), "gains"], writes=[("kn", b2)])
                s.add("dve", lambda e: e.scalar_tensor_tensor(
                    out=knrb, in0=ksl, scalar=gains[:64, G_KROPE:G_KROPE + 1], in1=rk[:64, :],
                    op0=ALU.mult, op1=ALU.mult), reads=ktk + [("rstd", 0), "gains"], writes=[("knr", b2)])
                np_ = min(128, nk)
                s.add("act", lambda e: e.copy(out=vsb2[:np_, b2 * 512:b2 * 512 + nsub * 128], in_=ps[2][:np_, :nsub * 128]),
                      reads=[PS(2)], writes=[("vsb", b2)])

            stepc = {}

            def sphase(i, it):
                b2 = i % 2
                h, q0, TQ, nk, diag = it["h"], it["q0"], it["TQ"], it["nk"], it["diag"]
                if it["first"]:
                    stepc["n"] = 0
                qn = hid[:, QN0 + h * 512 + q0:QN0 + h * 512 + q0 + TQ]
                qr = hid[:64, QR0 + h * 512 + q0:QR0 + h * 512 + q0 + TQ]
                nsub = (nk + 127) // 128
                pend = []
                for ks in range(nsub):
                    step = stepc["n"]
                    nkk = min(128, nk - ks * 128)
                    qlo = ks * 128 if diag else 0
                    ncol = TQ - qlo
                    pscore = 3 + step % 2
                    mm(ps[pscore][:nkk, :ncol], kn2[:, b2 * 512 + ks * 128:b2 * 512 + ks * 128 + nkk], qn[:, qlo:TQ], True, False,
                       reads=[("kn", b2), ("hid", h)], writes=[PS(pscore)])
                    mm(ps[pscore][:nkk, :ncol], knr2[:64, b2 * 512 + ks * 128:b2 * 512 + ks * 128 + nkk], qr[:, qlo:TQ], False, True,
                       reads=[("knr", b2), ("hid", 16 + h)], writes=[PS(pscore)])
                    pb_ = pbuf[:nkk, (step % 2) * 512:(step % 2) * 512 + ncol]
                    ptok = ("pbuf", step % 2)
                    s.add("act", lambda e: e.activation(
                        out=pb_, in_=ps[pscore][:nkk, :ncol], func=AF.Exp, scale=float(192 ** -0.5)),
                        reads=[PS(pscore)], writes=[ptok])
                    if diag:
                        pd = pbuf[:nkk, (step % 2) * 512:(step % 2) * 512 + 128]
                        s.add("dve", lambda e: e.tensor_tensor(out=pd, in0=pd, in1=amask, op=ALU.mult),
                              reads=[ptok, "cmat"], writes=[ptok])
                    first = step == 0
                    last = step == it["nsteps"] - 1

                    def pv(nkk=nkk, ks=ks, pb_=pb_, ptok=ptok, first=first, last=last, qlo=qlo):
                        mm(ps[5][:, qlo:TQ], vsb2[:nkk, b2 * 512 + ks * 128:b2 * 512 + (ks + 1) * 128], pb_, first, last,
                           reads=[("vsb", b2), ptok], writes=[PS(5)])
                        mm(ps[6][:, qlo:TQ], onesb[:nkk, :], pb_, first, last,
                           reads=["onesb", ptok], writes=[PS(6)])
                    if pend:
                        pend.pop()()
                    pend.append(pv)
                    stepc["n"] = step + 1
                if pend:
                    pend.pop()()
                if it["last"]:
                    assert stepc["n"] == it["nsteps"], (stepc["n"], it["nsteps"])
                    s.add("dve", lambda e: e.reciprocal(out=rl[:, :TQ], in_=ps[6][:, :TQ]), reads=[PS(6)], writes=[("rstd", 1)])
                    dst = xnT[:, h * TT + q0:h * TT + q0 + TQ]
                    s.add("dve", lambda e: e.tensor_tensor(out=dst, in0=ps[5][:, :TQ], in1=rl[:, :TQ], op=ALU.mult),
                          reads=[PS(5), ("rstd", 1)], writes=[("xnT", h)])

            kphase(0, iters[0])
            for i, it in enumerate(iters):
                if i + 1 < len(iters):
                    kphase(i + 1, iters[i + 1])
                sphase(i, it)
            for np_ in range(8):
                wt, wtok = R.get(f"wo_{np_}")
                wv = wt.rearrange("p (k c) -> p k c", k=NKC)
                for i in range(2):
                    c = np_ * 2 + i
                    po = c % 2
                    for kc in range(NKC):
                        mm(ps[po][:, :tt], wv[:, kc, i * 128:(i + 1) * 128], xnT[:, kc * TT:kc * TT + tt], kc == 0, kc == NKC - 1,
                           reads=[wtok, ("xnT", kc)], writes=[PS(po)])
                    stat_flush()
                    dst = hT[:, c * TT:c * TT + tt]
                    s.add("dve", lambda e, dst=dst, po=po: e.tensor_tensor(out=dst, in0=ps[po][:, :tt], in1=dst, op=ALU.add),
                          reads=[PS(po), ("hT", c)], writes=[("hT", c)])
                    stat_chunk(c, tt)

        passes = [(t, False) for t in range(n_prompt_tiles)] + ([(NPT, True)] if do_sample else [])
        for (tix, sample) in passes:
            R.plan(pass_weight_names(sample)[:None])
        for (tix, sample) in passes:
            tt = 256 if sample else TT
            if sample:
                load_x(xs, 2)
            else:
                load_x(xp[tix * TT:(tix + 1) * TT, :], 4)
            ffn(0, 1, tt)
            retention(tix, tt, 64 if sample else 128, sample)
            ffn(0, 2, tt)
            if sample:
                latent(tix, tt, 0, ckvs, krs)
            else:
                latent(tix, tt, tix * TT, ckvp[tix * TT:(tix + 1) * TT, :], krp[tix * TT:(tix + 1) * TT, :])
            ffn(1, 1, tt, reuse=True)
            mla(tix, tt, sample)
            ffn(1, 2, tt, next_norm=False)
            if sample:
                store_y(ys, 2)
            else:
                store_y(yp[tix * TT:(tix + 1) * TT, :], 4)
        assert R.pos == len(R.sched), (R.pos, len(R.sched))
        s.emit(sems_eng, lanes)
    return nc


_CACHE = {}


def kernel(**inp):
    inp = {k: np.asarray(v) for k, v in inp.items()}
    W = pack_weights(inp)
    G = pack_gains(inp)
    cm, dqk, rr, rm = const_tables()
    if "nc" not in _CACHE:
        _CACHE["nc"] = build_program()
    nc = _CACHE["nc"]
    in_maps = []
    for c in range(NCORES):
        in_maps.append({
            "xp": np.ascontiguousarray(inp["x_prompt"][c]),
            "xs": np.ascontiguousarray(inp["x_sample"][4 * c:4 * c + 4].reshape(256, D)),
            "sret": np.ascontiguousarray(inp["state_ret"][0, 4 * c:4 * c + 4]),
            "cckv": np.ascontiguousarray(inp["cache_ckv"][4 * c:4 * c + 4]),
            "ckr": np.ascontiguousarray(inp["cache_krope"][4 * c:4 * c + 4]),
            "wts": W, "gains": G, "cmat": cm, "dqk": dqk, "rope_ret": rr, "rope_mla": rm,
        })
    res = run_bass_kernel_spmd(nc, in_maps, core_ids=list(range(NCORES)))
    r = res.results
    y_prompt = np.stack([r[c]["yp"] for c in range(NCORES)], 0)
    y_sample = np.concatenate([r[c]["ys"].reshape(4, DEC_S, D) for c in range(NCORES)], 0)
    ret_p = np.stack([r[c]["retp"] for c in range(NCORES)], 0)[None]
    ckv_p = np.stack([r[c]["ckvp"] for c in range(NCORES)], 0)
    kr_p = np.stack([r[c]["krp"] for c in range(NCORES)], 0)
    ret_s = np.concatenate([r[c]["rets"] for c in range(NCORES)], 0)[None]
    ckv_s = np.concatenate([r[c]["ckvs"].reshape(4, DEC_S, 512) for c in range(NCORES)], 0)
    kr_s = np.concatenate([r[c]["krs"].reshape(4, DEC_S, 64) for c in range(NCORES)], 0)
    return (y_prompt.astype(np.float32), y_sample.astype(np.float32), ret_p.astype(np.float32),
            ckv_p.astype(np.float32), kr_p.astype(np.float32), ret_s.astype(np.float32),
            ckv_s.astype(np.float32), kr_s.astype(np.float32))
```

```python
import math
from contextlib import ExitStack
import numpy as np
import concourse.bass as bass
import concourse.mybir as mybir
from concourse.bass_utils import run_bass_kernel_spmd

F32 = mybir.dt.float32
BF16 = mybir.dt.bfloat16
AF = mybir.ActivationFunctionType
ALU = mybir.AluOpType

D = 2048
DFF = 5632
NKC = 16
NHC = 44
RH = 8
MH = 16
SEQ = 2048
PAST = 1024
DEC_S = 64
TT = 512
NPT = SEQ // TT
EPS = 1e-6
SLOT = 4096
NSLOT = 5
HOLD = 2
NCORES = 8
GAMMA = [1.0 - 2.0 ** (-5.0 - h) for h in range(RH)]

G_F1 = [0, 16]
G_MIX = [32, 48]
G_F2 = [64, 80]
G_KVN = 96
G_KVLAT = 112
G_QLAT = 116
G_GN = 120
G_KNOPE = 152
G_KROPE = 153
G_QNOPE = 154
G_QROPE = 155
NG = 160


def weight_tiles():
    tiles = []
    for l in range(2):
        for f in (1, 2):
            for hc in range(NHC):
                tiles.append((f"f{f}in{l}_{hc}", 4096))
            for nc_ in range(NKC):
                for half in range(2):
                    tiles.append((f"f{f}out{l}_{nc_}_{half}", 2816))
    for h in range(RH):
        for nm in ("q", "k", "v0", "v1", "g0", "g1"):
            tiles.append((f"rin_{h}_{nm}", 4096))
    for nc_ in range(NKC):
        tiles.append((f"rout_{nc_}", 4096))
    tiles += [("dkv_0", 4096), ("dkv_1", 4096), ("dkv_2", 2048)]
    tiles += [("dq_0", 4096), ("dq_1", 4096)]
    for hg in range(4):
        tiles.append((f"uq_{hg}", 4096))
    for hg in range(4):
        tiles.append((f"ukv_{hg}", 4096))
    for np_ in range(8):
        tiles.append((f"wo_{np_}", 4096))
    off = 0
    out = {}
    for nm, e in tiles:
        out[nm] = (off, e)
        off += e
    return out, off


WL, WTOT = weight_tiles()


def pack_weights(inp):
    W = np.empty((128, WTOT), np.float32)

    def put(nm, arr):
        off, e = WL[nm]
        W[:, off:off + e] = arr.reshape(128, e)

    def kc_tile(M):
        K, n = M.shape
        return M.reshape(K // 128, 128, n).transpose(1, 0, 2)

    for l in range(2):
        for f in (1, 2):
            win = inp[f"ffn{f}_w_in"][l]
            a = win.reshape(16, 128, 2, NHC, 128).transpose(1, 3, 0, 2, 4)
            for hc in range(NHC):
                put(f"f{f}in{l}_{hc}", a[:, hc])
            wout = inp[f"ffn{f}_w_out"][l]
            b = wout.reshape(2, 22, 128, 16, 128).transpose(2, 3, 0, 1, 4)
            for nc_ in range(NKC):
                for half in range(2):
                    put(f"f{f}out{l}_{nc_}_{half}", b[:, nc_, half])
    rin = inp["ret_w_in"][0]
    for h in range(RH):
        put(f"rin_{h}_q", kc_tile(rin[:, h * 256:(h + 1) * 256]))
        put(f"rin_{h}_k", kc_tile(rin[:, 2048 + h * 256:2048 + (h + 1) * 256]))
        put(f"rin_{h}_v0", kc_tile(rin[:, 4096 + h * 512:4096 + h * 512 + 256]))
        put(f"rin_{h}_v1", kc_tile(rin[:, 4096 + h * 512 + 256:4096 + (h + 1) * 512]))
        put(f"rin_{h}_g0", kc_tile(rin[:, 8192 + h * 512:8192 + h * 512 + 256]))
        put(f"rin_{h}_g1", kc_tile(rin[:, 8192 + h * 512 + 256:8192 + (h + 1) * 512]))
    rout = inp["ret_w_out"][0]
    for nc_ in range(NKC):
        put(f"rout_{nc_}", kc_tile(rout[:, nc_ * 128:(nc_ + 1) * 128]))
    dkv = inp["w_dkv"]
    put("dkv_0", kc_tile(dkv[:, 0:256]))
    put("dkv_1", kc_tile(dkv[:, 256:512]))
    put("dkv_2", kc_tile(np.concatenate([dkv[:, 512:576], dkv[:, 544:576], dkv[:, 512:544]], 1)))
    dq = inp["w_dq"][0]
    put("dq_0", kc_tile(dq[:, 0:256]))
    put("dq_1", kc_tile(dq[:, 256:512]))
    uq = inp["w_uq"][0].reshape(512, MH, 192)
    uq = np.concatenate([uq, uq[:, :, 160:192], uq[:, :, 128:160]], 2)
    for hg in range(4):
        put(f"uq_{hg}", kc_tile(uq[:, hg * 4:(hg + 1) * 4].reshape(512, 1024)))
    ukv = inp["w_ukv"]
    for hg in range(4):
        put(f"ukv_{hg}", kc_tile(ukv[:, hg * 1024:(hg + 1) * 1024]))
    wo = inp["w_o"][0]
    for np_ in range(8):
        put(f"wo_{np_}", kc_tile(wo[:, np_ * 256:(np_ + 1) * 256]))
    return W


def pack_gains(inp):
    G = np.zeros((128, NG), np.float32)

    def fm(v):
        return v.reshape(-1, 128).T

    for l in range(2):
        G[:, G_F1[l]:G_F1[l] + 16] = fm(inp["ffn1_norm"][l])
        G[:, G_MIX[l]:G_MIX[l] + 16] = fm(inp["mix_norm"][l])
        G[:, G_F2[l]:G_F2[l] + 16] = fm(inp["ffn2_norm"][l])
    G[:, G_KVN:G_KVN + 16] = fm(inp["kv_norm"])
    G[:, G_KVLAT:G_KVLAT + 4] = fm(inp["kv_lat_norm"])
    G[:, G_QLAT:G_QLAT + 4] = fm(inp["q_lat_norm"][0])
    G[:, G_GN:G_GN + 32] = fm(inp["ret_gn_gain"][0])
    G[:, G_KNOPE] = inp["k_norm"][:128]
    G[:64, G_KROPE] = inp["k_norm"][128:]
    G[:, G_QNOPE] = inp["q_norm"][0][:128]
    G[:64, G_QROPE] = inp["q_norm"][0][128:]
    return G


def const_tables():
    cm = np.zeros((128, 6 * 128), np.float32)
    cm[:, 0:128] = np.eye(128)
    cm[:, 128:256] = 1.0 / 2048
    cm[:, 256:384] = 1.0 / 512
    cm[:, 384:512] = 1.0 / 192
    m = np.arange(128)
    cm[:, 512:640] = (m[None, :] >= m[:, None])
    cm[:, 640:768] = ((m[:, None] // 64) <= (m[None, :] // 64))
    dqk = np.zeros((128, RH, 2, 128), np.float32)
    n = np.arange(128, dtype=np.float64)
    for h in range(RH):
        lg = math.log1p(-2.0 ** (-5.0 - h))
        dqk[:, h, 0, :] = (np.exp(lg * (n + 1.0)) / 16.0)[None, :]
        dqk[:, h, 1, :] = np.exp(-lg * (n + 1.0))[None, :]
    pos = np.zeros((NPT + 1, TT), np.float32)
    for t in range(NPT):
        pos[t] = np.arange(t * TT, (t + 1) * TT)
    pos[NPT, :256] = np.tile(PAST + np.arange(DEC_S), 4)
    inv_r = np.power(np.float32(10000.0), -np.arange(128, dtype=np.float32) / np.float32(128))
    inv_m = np.power(np.float32(10000.0), -np.arange(32, dtype=np.float32) / np.float32(32))
    rr = np.zeros((NPT + 1, 128, 2 * TT), np.float32)
    rm = np.zeros((NPT + 1, 64, 2 * TT), np.float32)
    for t in range(NPT + 1):
        ang = (pos[t][None, :] * inv_r[:, None]).astype(np.float32)
        rr[t, :, :TT] = np.cos(ang)
        rr[t, :, TT:] = np.sin(ang)
        angm = (pos[t][None, :] * inv_m[:, None]).astype(np.float32)
        c = np.cos(angm)
        s = np.sin(angm)
        rm[t, :, :TT] = np.concatenate([c, c], 0)
        rm[t, :, TT:] = np.concatenate([-s, s], 0)
    return cm, dqk.reshape(128, RH * 2 * 128), rr, rm


class Op:
    __slots__ = ("eng", "fn", "deps", "needed", "sem", "val", "is_dma", "lane")


class _Rec:
    def __getattr__(self, name):
        def f(*a, **k):
            self.__dict__["call"] = (name, a, k)
            return self
        return f


class Sched:
    def __init__(self, nc, n_lanes):
        self.nc = nc
        self.E = {"pe": nc.tensor, "act": nc.scalar, "dve": nc.vector, "pool": nc.gpsimd, "sp": nc.sync}
        self.ops = []
        self.lw = {}
        self.rd = {}
        self.n_lanes = n_lanes
        self.lane_last = {}
        self.lane_next = 0

    def add(self, eng, fn, reads=(), writes=(), dma=False, lane=None):
        op = Op()
        op.eng = eng
        rec = _Rec()
        fn(rec)
        op.fn = rec.call
        op.is_dma = dma
        op.needed = dma
        op.sem = None
        op.val = 0
        op.lane = None
        deps = []
        for t in reads:
            w = self.lw.get(t)
            if w is not None:
                deps.append(w)
        for t in writes:
            w = self.lw.get(t)
            if w is not None:
                deps.append(w)
            r = self.rd.get(t)
            if r:
                deps.extend(r.values())
        if dma:
            if lane is None:
                lane = self.lane_next
                self.lane_next = (self.lane_next + 1) % self.n_lanes
            op.lane = lane
            prev = self.lane_last.get(lane)
            if prev is not None:
                deps.append(prev)
            self.lane_last[lane] = op
        key = ("dma", op.lane) if dma else eng
        for t in reads:
            self.rd.setdefault(t, {})[key] = op
        for t in writes:
            self.lw[t] = op
            self.rd[t] = {}
        fdeps = []
        seen = set()
        for d in deps:
            if id(d) in seen or d is op:
                continue
            seen.add(id(d))
            if (not d.is_dma) and (not dma) and d.eng == eng and eng == "pe":
                continue
            d.needed = True
            fdeps.append(d)
        op.deps = fdeps
        self.ops.append(op)
        return op

    def emit(self, sems_eng, sems_lane):
        cnt = {e: 0 for e in self.E}
        lane_cnt = {}
        waited = {e: {} for e in self.E}
        for op in self.ops:
            eng = self.E[op.eng]
            need = {}
            for d in op.deps:
                k = id(d.sem)
                if k not in need or need[k][1] < d.val:
                    need[k] = (d.sem, d.val)
            w = waited[op.eng]
            for k, (sem, val) in need.items():
                if w.get(k, 0) < val:
                    eng.wait_ge(sem, val)
                    w[k] = val
            nm_, a_, k_ = op.fn
            inst = getattr(eng, nm_)(*a_, **k_)
            if op.is_dma:
                lane_cnt[op.lane] = lane_cnt.get(op.lane, 0) + 1
                op.sem = sems_lane[op.lane]
                op.val = 16 * lane_cnt[op.lane]
                inst.then_inc(op.sem, 16)
            elif op.needed:
                cnt[op.eng] += 1
                op.sem = sems_eng[op.eng]
                op.val = cnt[op.eng]
                inst.then_inc(op.sem, 1)
        sp = self.E["sp"]
        for lane, c in lane_cnt.items():
            sp.wait_ge(sems_lane[lane], 16 * c)


def build_program(n_prompt_tiles=NPT, do_sample=True, stop_after=None):
    nc = bass.Bass("TRN2", target_bir_lowering=False)

    def din(name, shape):
        return nc.dram_tensor(name, shape, F32, kind="ExternalInput").ap()

    def dout(name, shape):
        return nc.dram_tensor(name, shape, F32, kind="ExternalOutput").ap()

    xp = din("xp", [SEQ, D])
    xs = din("xs", [256, D])
    sret = din("sret", [4, RH, 256, 512])
    cckv = din("cckv", [4, PAST, 512])
    ckr = din("ckr", [4, PAST, 64])
    wts = din("wts", [128, WTOT])
    gains_d = din("gains", [128, NG])
    cmat_d = din("cmat", [128, 768])
    dqk_d = din("dqk", [128, RH * 2 * 128])
    rr_d = din("rope_ret", [NPT + 1, 128, 2 * TT])
    rm_d = din("rope_mla", [NPT + 1, 64, 2 * TT])
    yp = dout("yp", [SEQ, D])
    ys = dout("ys", [256, D])
    retp = dout("retp", [RH, 256, 512])
    ckvp = dout("ckvp", [SEQ, 512])
    krp = dout("krp", [SEQ, 64])
    rets = dout("rets", [4, RH, 256, 512])
    ckvs = dout("ckvs", [256, 512])
    krs = dout("krs", [256, 64])
    WSPLIT = min(off for (off, _) in WL.values() if off >= WTOT // 2)
    wbf0 = nc.dram_tensor("wbf0", [128, WSPLIT], BF16, kind="Internal").ap()
    wbf1 = nc.dram_tensor("wbf1", [128, WTOT - WSPLIT], BF16, kind="Internal").ap()

    def wbf_view(off, e_):
        if off >= WSPLIT:
            return wbf1[:, off - WSPLIT:off - WSPLIT + e_]
        return wbf0[:, off:off + e_]

    N_GEN_LANES = 16
    with ExitStack() as es:
        def sb(name, shape, dt):
            return es.enter_context(nc.sbuf_tensor(name, shape, dt))

        hT = sb("hT", [128, NKC * TT], F32)
        xnT = sb("xnT", [128, NKC * TT], BF16)
        hid = sb("hid", [128, NHC * TT], BF16)
        ring = sb("ring", [128, NSLOT * SLOT], BF16)
        cT = sb("cT", [128, 4 * SEQ], BF16)
        krT = sb("krT", [128, SEQ], BF16)
        gains = sb("gains_sb", [128, NG], F32)
        cmat = sb("cmat_sb", [128, 768], F32)
        dqh = sb("dqh", [128, 2 * 256], F32)
        identb = sb("identb", [128, 128], BF16)
        onesb = sb("onesb", [128, 128], BF16)
        epst = sb("epst", [128, 1], F32)
        rope = sb("rope", [128, 2 * TT], F32)
        rstd = sb("rstd", [128, 2 * TT], F32)
        sqb = sb("sqb", [128, 2 * TT], F32)
        tmpf = sb("tmpf", [128, 4 * TT], F32)
        UA = sb("UA", [128, 2048], BF16)
        UB = sb("UB", [128, 2048], BF16)
        ktok2 = sb("ktok2", [128, 2048], BF16)
        vtokB = sb("vtokB", [128, 2048], BF16)
        innb = sb("innb", [128, 128], BF16)
        onb = sb("onb", [128, 512], BF16)
        Sf = sb("Sf", [128, 2 * 1024], F32)
        Sb = sb("Sb", [128, 1024], BF16)
        stt = sb("stt", [128, 16], F32)
        kn2 = UA[:, 0:1024]
        knr2 = UA[:, 1024:2048]
        vsb2 = UB[:, 0:1024]
        pbuf = UB[:, 1024:2048]
        ps = [es.enter_context(nc.psum_tensor(f"ps{i}", [128, 512], F32)) for i in range(8)]
        sems_eng = {e: es.enter_context(nc.semaphore("s_" + e)) for e in ["pe", "act", "dve", "pool", "sp"]}
        lanes = [es.enter_context(nc.semaphore(f"ln{i}")) for i in range(N_GEN_LANES + NSLOT)]

        s = Sched(nc, N_GEN_LANES)
        ident = cmat[:, 0:128]
        on2048 = cmat[:, 128:256]
        on512 = cmat[:, 256:384]
        on192 = cmat[:, 384:512]
        rmask = cmat[:, 512:640]
        amask = cmat[:, 640:768]

        def hidtok(c0, c1):
            return [("hid", b) for b in range(c0 // 512, (c1 + 511) // 512)]

        def PS(i):
            return ("ps", i)

        s.add("sp", lambda e: e.dma_start(out=gains[:], in_=gains_d[:, :]), writes=["gains"], dma=True)
        s.add("sp", lambda e: e.dma_start(out=cmat[:], in_=cmat_d[:, :]), writes=["cmat"], dma=True)
        s.add("dve", lambda e: e.tensor_copy(out=identb[:], in_=ident), reads=["cmat"], writes=["identb"])
        s.add("dve", lambda e: e.memset(onesb[:], 1.0), writes=["onesb"])
        s.add("dve", lambda e: e.memset(epst[:], EPS), writes=["eps"])

        WORDER = {nm: j for j, nm in enumerate(WL.keys())}

        class Ring:
            def __init__(self):
                self.sched = []
                self.issued = 0
                self.pos = 0
                self.npass = 0
                self.wb_done = set()

            def plan(self, names):
                self.sched.extend((nm, self.npass) for nm in names)
                self.npass += 1

            def conv_pass(self, nm):
                if self.npass < 2:
                    return None
                return WORDER[nm] % (self.npass - 1)

            def _issue(self, i):
                nm, p = self.sched[i]
                off, e_ = WL[nm]
                slot = i % NSLOT
                dst = ring[:, slot * SLOT:slot * SLOT + e_]
                cp = self.conv_pass(nm)
                if cp is not None and p > cp:
                    src = wbf_view(off, e_)
                    s.add("pool", lambda e: e.dma_start(out=dst, in_=src), reads=[("wbf", nm)],
                          writes=[("ring", slot)], dma=True, lane=N_GEN_LANES + slot)
                else:
                    src = wts[:, off:off + e_]
                    s.add("pool", lambda e: e.dma_start(out=dst, in_=src),
                          writes=[("ring", slot)], dma=True, lane=N_GEN_LANES + slot)

            def get(self, nm):
                i = self.pos
                assert self.sched[i][0] == nm, (self.sched[i], nm)
                while self.issued < min(len(self.sched), i + NSLOT - HOLD):
                    self._issue(self.issued)
                    self.issued += 1
                self.pos += 1
                slot = i % NSLOT
                p = self.sched[i][1]
                off, e_ = WL[nm]
                view = ring[:, slot * SLOT:slot * SLOT + e_]
                if self.conv_pass(nm) == p and nm not in self.wb_done:
                    self.wb_done.add(nm)
                    s.add("act", lambda e: e.dma_start(out=wbf_view(off, e_), in_=view), reads=[("ring", slot)],
                          writes=[("wbf", nm)], dma=True)
                return view, ("ring", slot)

        R = Ring()

        def pass_weight_names(sample):
            n = []
            for l in range(2):
                n += [f"f1in{l}_{hc}" for hc in range(NHC)]
                n += [f"f1out{l}_{c}_{hf}" for c in range(NKC) for hf in range(2)]
                if l == 0:
                    for h in range(RH):
                        n += [f"rin_{h}_{x}" for x in ("q", "k", "v0", "v1", "g0", "g1")]
                    n += [f"rout_{c}" for c in range(NKC)]
                else:
                    n += ["dq_0", "dq_1"] + [f"uq_{hg}" for hg in range(4)]
                    for _ in range(4 if sample else 1):
                        n += [f"ukv_{hg}" for hg in range(4)]
                    n += [f"wo_{i}" for i in range(8)]
                n += [f"f2in{l}_{hc}" for hc in range(NHC)]
                n += [f"f2out{l}_{c}_{hf}" for c in range(NKC) for hf in range(2)]
                if l == 0:
                    n += ["dkv_0", "dkv_1", "dkv_2"]
            return n

        def mm(out, lhsT, rhs, start, stop, reads, writes):
            s.add("pe", lambda e: e.matmul(out, lhsT, rhs, start=start, stop=stop), reads=reads, writes=writes)

        def tr(out, in_, idn, reads, writes):
            s.add("pe", lambda e: e.transpose(out, in_, idn), reads=reads, writes=writes)

        def rstd_from(psum_ap, out_ap, reads, writes):
            s.add("act", lambda e: e.activation(out=out_ap, in_=psum_ap, func=AF.Ln, bias=epst[:psum_ap.shape[0], 0:1], scale=1.0),
                  reads=reads + ["eps"], writes=writes)
            s.add("act", lambda e: e.activation(out=out_ap, in_=out_ap, func=AF.Exp, scale=-0.5),
                  reads=writes, writes=writes)

        def load_x(src_rows, nblk):
            for j in range(nblk):
                st = hid[:, (j % 2) * 4096:(j % 2) * 4096 + 4096].bitcast(F32)
                sttok = hidtok((j % 2) * 4096, (j % 2) * 4096 + 4096)
                s.add("sp", lambda e, st=st, j=j: e.dma_start(out=st, in_=src_rows[j * 128:(j + 1) * 128, :]),
                      writes=sttok, dma=True)
                for g in range(4):
                    b = g % 2
                    for i in range(4):
                        kc = 4 * g + i
                        tr(ps[b][:, i * 128:(i + 1) * 128], st[:, kc * 128:(kc + 1) * 128], ident,
                           reads=sttok + ["cmat"], writes=[PS(b)])
                    dst = hT[:, :].rearrange("p (k t) -> p k t", k=NKC)[:, 4 * g:4 * g + 4, j * 128:(j + 1) * 128]
                    src = ps[b][:, :].rearrange("p (k t) -> p k t", k=4)
                    s.add("act" if g % 2 else "dve",
                          (lambda e, dst=dst, src=src: e.copy(out=dst, in_=src)) if g % 2 else
                          (lambda e, dst=dst, src=src: e.tensor_copy(out=dst, in_=src)),
                          reads=[PS(b)], writes=[("hT", 4 * g + i) for i in range(4)])
                    if j == nblk - 1:
                        for i in range(4):
                            stat_chunk(4 * g + i, nblk * 128, defer=False)

        def store_y(dst_rows, nblk):
            for j in range(nblk):
                st = hid[:, (j % 2) * 4096:(j % 2) * 4096 + 4096].bitcast(F32)
                sttok = hidtok((j % 2) * 4096, (j % 2) * 4096 + 4096)
                for g in range(4):
                    b = g % 2
                    for i in range(4):
                        kc = 4 * g + i
                        tr(ps[b][:, i * 128:(i + 1) * 128], hT[:, kc * TT + j * 128:kc * TT + (j + 1) * 128], ident,
                           reads=[("hT", kc), "cmat"], writes=[PS(b)])
                    dst = st[:, g * 512:(g + 1) * 512]
                    s.add("act" if g % 2 else "dve",
                          (lambda e, dst=dst, b=b: e.copy(out=dst, in_=ps[b][:, :])) if g % 2 else
                          (lambda e, dst=dst, b=b: e.tensor_copy(out=dst, in_=ps[b][:, :])),
                          reads=[PS(b)], writes=sttok)
                s.add("sp", lambda e, st=st, j=j: e.dma_start(out=dst_rows[j * 128:(j + 1) * 128, :], in_=st),
                      reads=sttok, dma=True)

        pend_stat = []

        def stat_flush():
            while pend_stat:
                pend_stat.pop(0)()

        def stat_chunk(kc, tt, defer=True):
            q = sqb[:, (kc % 2) * TT:(kc % 2) * TT + tt]
            src = hT[:, kc * TT:kc * TT + tt]
            s.add("dve", lambda e: e.tensor_tensor(out=q, in0=src, in1=src, op=ALU.mult),
                  reads=[("hT", kc)], writes=[("sqb", kc % 2)])

            def f():
                mm(ps[7][:, :tt], on2048, q, kc == 0, kc == NKC - 1, reads=["cmat", ("sqb", kc % 2)], writes=[PS(7)])
            if defer:
                pend_stat.append(f)
            else:
                f()

        def rmsnorm(gcol, tt, reuse=False):
            stat_flush()
            rs = rstd[:, 0:tt]
            if not reuse:
                rstd_from(ps[7][:, :tt], rs, [PS(7)], [("rstd", 0)])
            for kc in range(NKC):
                src = hT[:, kc * TT:kc * TT + tt]
                dst = xnT[:, kc * TT:kc * TT + tt]
                s.add("dve", lambda e: e.scalar_tensor_tensor(
                    out=dst, in0=src, scalar=gains[:, gcol + kc:gcol + kc + 1], in1=rs, op0=ALU.mult, op1=ALU.mult),
                    reads=[("hT", kc), ("rstd", 0), "gains"], writes=[("xnT", kc)])

        def ffn(l, f, tt, reuse=False, next_norm=True):
            rmsnorm((G_F1 if f == 1 else G_F2)[l], tt, reuse=reuse)
            for hc in range(NHC):
                wt, wtok = R.get(f"f{f}in{l}_{hc}")
                wv = wt.rearrange("p (k a c) -> p k a c", k=NKC, a=2)
                pa, pb = hc % 2, 2 + hc % 2
                for kc in range(NKC):
                    mm(ps[pa][:, :tt], wv[:, kc, 0, :], xnT[:, kc * TT:kc * TT + tt], kc == 0, kc == NKC - 1,
                       reads=[wtok, ("xnT", kc)], writes=[PS(pa)])
                for kc in range(NKC):
                    mm(ps[pb][:, :tt], wv[:, kc, 1, :], xnT[:, kc * TT:kc * TT + tt], kc == 0, kc == NKC - 1,
                       reads=[wtok, ("xnT", kc)], writes=[PS(pb)])
                sa = tmpf[:, (hc % 2) * TT:(hc % 2) * TT + tt]
                s.add("act", lambda e, sa=sa, pa=pa: e.activation(out=sa, in_=ps[pa][:, :tt], func=AF.Silu),
                      reads=[PS(pa)], writes=[("tmpf", hc % 2)])
                dst = hid[:, hc * TT:hc * TT + tt]
                s.add("dve", lambda e, sa=sa, pb=pb, dst=dst: e.tensor_tensor(out=dst, in0=sa, in1=ps[pb][:, :tt], op=ALU.mult),
                      reads=[PS(pb), ("tmpf", hc % 2)], writes=[("hid", hc)])
            for c in range(NKC):
                po = 4 + c % 2
                for half in range(2):
                    wt, wtok = R.get(f"f{f}out{l}_{c}_{half}")
                    wv = wt.rearrange("p (r c) -> p r c", r=22)
                    for r in range(22):
                        hc = half * 22 + r
                        mm(ps[po][:, :tt], wv[:, r, :], hid[:, hc * TT:hc * TT + tt], hc == 0, hc == NHC - 1,
                           reads=[wtok, ("hid", hc)], writes=[PS(po)])
                stat_flush()
                dst = hT[:, c * TT:c * TT + tt]
                s.add("dve", lambda e, dst=dst, po=po: e.scalar_tensor_tensor(
                    out=dst, in0=ps[po][:, :tt], scalar=0.5, in1=dst, op0=ALU.mult, op1=ALU.add),
                    reads=[PS(po), ("hT", c)], writes=[("hT", c)])
                if next_norm:
                    stat_chunk(c, tt)

        def load_rope(src_d, tix, nparts):
            s.add("sp", lambda e: e.dma_start(out=rope[:nparts, :], in_=src_d[tix, :, :]), writes=["rope"], dma=True)

        def fence(tokens):
            s.add("dve", lambda e: e.memset(stt[:, 15:16], 0.0), writes=list(tokens) + [("fence",)])

        U_TOKENS = [("qhT", 0), ("qhT", 1), ("khT", 0), ("khT", 1), ("kn", 0), ("kn", 1), ("knr", 0), ("knr", 1),
                    ("vsb", 0), ("vsb", 1), ("pbuf", 0), ("pbuf", 1)]

        def run(gen):
            for _ in gen:
                pass

        def chain(*gens):
            for g in gens:
                for _ in g:
                    yield

        def interleave(g1, g2, r=1):
            a = b = True
            while a or b:
                if a:
                    try:
                        next(g1)
                    except StopIteration:
                        a = False
                for _ in range(r):
                    if b:
                        try:
                            next(g2)
                        except StopIteration:
                            b = False

        def retention(tix, tt, L, sample):
            NB = tt // L
            rmsnorm(G_MIX[0], tt)
            load_rope(rr_d, tix, 128)
            fence(U_TOKENS)
            cos = rope[:, 0:tt]
            sin = rope[:, TT:TT + tt]
            VT0 = 32 * 512
            SG0 = 36 * 512
            sg = hid[:, SG0:SG0 + 4096].bitcast(F32)

            def bufs(h):
                b = h % 2
                qh = UA[:, b * 1024:(b + 1) * 1024]
                kh = UB[:, b * 1024:(b + 1) * 1024]
                kt = ktok2[:, b * 1024:(b + 1) * 1024]
                return b, qh, kh, kt

            def vt(b, j):
                if b == 0:
                    return hid[:L, VT0 + j * 512:VT0 + (j + 1) * 512], ("hid", 32 + j)
                return vtokB[:L, j * 512:(j + 1) * 512], ("vtokB", j)

            def genA(h):
                b, qh, kh, kt = bufs(h)
                dqb = dqh[:, b * 256:(b + 1) * 256]
                dqtok = ("dqh", b)
                s.add("sp", lambda e: e.dma_start(out=dqb, in_=dqk_d[:, h * 256:(h + 1) * 256]), writes=[dqtok], dma=True)
                for which, dst_t, dsc, nm, dtok in ((0, qh, dqb[:, 0:L], "q", ("qhT", b)), (1, kh, dqb[:, 128:128 + L], "k", ("khT", b))):
                    wt, wtok = R.get(f"rin_{h}_{nm}")
                    wv = wt.rearrange("p (k c) -> p k c", k=NKC)
                    for c in range(2):
                        for kc in range(NKC):
                            mm(ps[c][:, :tt], wv[:, kc, c * 128:(c + 1) * 128], xnT[:, kc * TT:kc * TT + tt],
                               kc == 0, kc == NKC - 1, reads=[wtok, ("xnT", kc)], writes=[PS(c)])
                            if kc % 8 == 7:
                                yield
                    t0 = tmpf[:, 0:tt]
                    t1 = tmpf[:, TT:TT + tt]
                    t2 = tmpf[:, 2 * TT:2 * TT + tt]
                    t3 = tmpf[:, 3 * TT:3 * TT + tt]
                    s.add("dve", lambda e: e.tensor_tensor(out=t0, in0=ps[0][:, :tt], in1=cos, op=ALU.mult),
                          reads=[PS(0), "rope"], writes=[("tmpf", 0)])
                    s.add("dve", lambda e: e.tensor_tensor(out=t1, in0=ps[1][:, :tt], in1=sin, op=ALU.mult),
                          reads=[PS(1), "rope"], writes=[("tmpf", 1)])
                    s.add("dve", lambda e: e.tensor_tensor(out=t2, in0=ps[0][:, :tt], in1=sin, op=ALU.mult),
                          reads=[PS(0), "rope"], writes=[("tmpf", 2)])
                    s.add("dve", lambda e: e.tensor_tensor(out=t3, in0=ps[1][:, :tt], in1=cos, op=ALU.mult),
                          reads=[PS(1), "rope"], writes=[("tmpf", 3)])
                    yield
                    s.add("dve", lambda e: e.tensor_tensor(out=t0, in0=t0, in1=t1, op=ALU.subtract),
                          reads=[("tmpf", 0), ("tmpf", 1)], writes=[("tmpf", 0)])
                    s.add("dve", lambda e: e.tensor_tensor(out=t2, in0=t2, in1=t3, op=ALU.add),
                          reads=[("tmpf", 2), ("tmpf", 3)], writes=[("tmpf", 2)])
                    bc = dsc.unsqueeze(1).to_broadcast([128, NB, L])
                    for c, tsrc, ti in ((0, t0, 0), (1, t2, 2)):
                        dst = dst_t[:, c * TT:c * TT + tt].rearrange("p (b l) -> p b l", b=NB)
                        srcv = tsrc.rearrange("p (b l) -> p b l", b=NB)
                        s.add("dve", lambda e: e.tensor_tensor(out=dst, in0=srcv, in1=bc, op=ALU.mult),
                              reads=[("tmpf", ti), dqtok], writes=[dtok])
                    yield
                psb = ps[2][:, :].bitcast(BF16)
                for j in range(NB):
                    for c in range(2):
                        tr(psb[:L, (j * 2 + c) * 128:(j * 2 + c + 1) * 128], kh[:, c * TT + j * L:c * TT + (j + 1) * L],
                           identb[:, :], reads=[("khT", b), "identb"], writes=[PS(2)])
                s.add("act", lambda e: e.copy(out=kt[:L, :], in_=psb[:L, :]), reads=[PS(2)], writes=[("ktok", b)])
                yield

            def genB(h):
                b = h % 2
                wv0, wtok0 = R.get(f"rin_{h}_v0")
                wv1, wtok1 = R.get(f"rin_{h}_v1")
                wv0v = wv0.rearrange("p (k c) -> p k c", k=NKC)
                wv1v = wv1.rearrange("p (k c) -> p k c", k=NKC)
                for j in range(NB):
                    pv = 2 + j % 2
                    for kc in range(NKC):
                        mm(ps[pv][:L, 0:256], xnT[:, kc * TT + j * L:kc * TT + (j + 1) * L], wv0v[:, kc, :],
                           kc == 0, kc == NKC - 1, reads=[wtok0, ("xnT", kc)], writes=[PS(pv)])
                        if kc % 8 == 7:
                            yield
                    for kc in range(NKC):
                        mm(ps[pv][:L, 256:512], xnT[:, kc * TT + j * L:kc * TT + (j + 1) * L], wv1v[:, kc, :],
                           kc == 0, kc == NKC - 1, reads=[wtok1, ("xnT", kc)], writes=[PS(pv)])
                        if kc % 8 == 7:
                            yield
                    dst, dtk = vt(b, j)
                    s.add("act", lambda e: e.copy(out=dst, in_=ps[pv][:L, :]), reads=[PS(pv)], writes=[dtk])

            def doC(h):
                for gi in range(2):
                    wg, wtokg = R.get(f"rin_{h}_g{gi}")
                    wgv = wg.rearrange("p (k c) -> p k c", k=NKC)
                    for cc in range(2):
                        c = gi * 2 + cc
                        pg = c % 2
                        for kc in range(NKC):
                            mm(ps[pg][:, :tt], wgv[:, kc, cc * 128:(cc + 1) * 128], xnT[:, kc * TT:kc * TT + tt],
                               kc == 0, kc == NKC - 1, reads=[wtokg, ("xnT", kc)], writes=[PS(pg)])
                        dst = sg[:, c * 512:c * 512 + tt]
                        s.add("act", lambda e: e.activation(out=dst, in_=ps[pg][:, :tt], func=AF.Silu),
                              reads=[PS(pg)], writes=[("hid", 36 + 2 * c), ("hid", 37 + 2 * c)])

            def genD(h):
                b, qh, kh, kt = bufs(h)
                gL = GAMMA[h] ** L
                sbuf_i = h % 2
                Sfh = Sf[:, sbuf_i * 1024:(sbuf_i + 1) * 1024]
                Stok = ("Sf", sbuf_i)
                if not sample:
                    if tix == 0:
                        s.add("dve", lambda e: e.memset(Sfh, 0.0), writes=[Stok])
                    else:
                        s.add("sp", lambda e: e.dma_start(
                            out=Sfh.rearrange("p (c v) -> p c v", c=2),
                            in_=retp[h].rearrange("(c p) v -> p c v", p=128)),
                            reads=[("retp", h)], writes=[Stok], dma=True)
                    s.add("act", lambda e: e.copy(out=Sb[:, :], in_=Sfh), reads=[Stok], writes=[("Sb",)])
                psT = ps[4][:, :].bitcast(BF16)
                for j in range(NB):
                    vj, vtk = vt(b, j)
                    if sample:
                        Sfh = Sf[:, (j % 2) * 1024:(j % 2 + 1) * 1024]
                        Stok = ("Sf", j % 2)
                        s.add("sp", lambda e: e.dma_start(
                            out=Sfh.rearrange("p (c v) -> p c v", c=2),
                            in_=sret[j, h].rearrange("(c p) v -> p c v", p=128)),
                            writes=[Stok], dma=True)
                        s.add("act", lambda e: e.copy(out=Sb[:, :], in_=Sfh), reads=[Stok], writes=[("Sb",)])
                    for c in range(2):
                        mm(ps[4][:L, :L], kh[:, c * TT + j * L:c * TT + (j + 1) * L], qh[:, c * TT + j * L:c * TT + (j + 1) * L],
                           c == 0, c == 1, reads=[("khT", b), ("qhT", b)], writes=[PS(4)])
                    s.add("dve", lambda e: e.tensor_tensor(out=innb[:L, :L], in0=ps[4][:L, :L], in1=rmask[:L, :L], op=ALU.mult),
                          reads=[PS(4), "cmat"], writes=[("innb",)])
                    for c in range(2):
                        pk = 6 + c
                        mm(ps[pk][:, :], kt[:L, (j * 2 + c) * 128:(j * 2 + c + 1) * 128], vj,
                           True, True, reads=[("ktok", b), vtk], writes=[PS(pk)])
                    yield
                    mm(ps[5][:L, :], innb[:L, :L], vj, True, False, reads=[("innb",), vtk], writes=[PS(5)])
                    for c in range(2):
                        mm(ps[5][:L, :], qh[:, c * TT + j * L:c * TT + (j + 1) * L], Sb[:, c * 512:(c + 1) * 512], False, c == 1,
                           reads=[("qhT", b), ("Sb",)], writes=[PS(5)])
                    yield
                    s.add("dve", lambda e: e.bn_stats(out=stt[:L, 0:6], in_=ps[5][:L, :]), reads=[PS(5)], writes=[("stt",)])
                    s.add("dve", lambda e: e.bn_aggr(out=stt[:L, 6:8], in_=stt[:L, 0:6]), reads=[("stt",)], writes=[("stt",)])
                    rstd_from(stt[:L, 7:8], stt[:L, 8:9], [("stt",)], [("stt2",)])
                    s.add("dve", lambda e: e.tensor_scalar(out=onb[:L, :], in0=ps[5][:L, :], scalar1=stt[:L, 6:7], scalar2=stt[:L, 8:9],
                                                           op0=ALU.subtract, op1=ALU.mult),
                          reads=[PS(5), ("stt",), ("stt2",)], writes=[("onb",)])
                    for c in range(2):
                        pk = 6 + c
                        Sc = Sfh[:, c * 512:(c + 1) * 512]
                        s.add("dve", lambda e: e.tensor_tensor(out=Sc, in0=Sc, in1=ps[pk][:, :], op=ALU.add),
                              reads=[PS(pk), Stok], writes=[Stok])
                    s.add("act", lambda e: e.mul(out=Sfh, in_=Sfh, mul=float(gL)), reads=[Stok], writes=[Stok])
                    if sample:
                        s.add("sp", lambda e: e.dma_start(
                            out=rets[j, h].rearrange("(c p) v -> p c v", p=128),
                            in_=Sfh.rearrange("p (c v) -> p c v", c=2)),
                            reads=[Stok], dma=True)
                    elif j < NB - 1:
                        s.add("act", lambda e: e.copy(out=Sb[:, :], in_=Sfh), reads=[Stok], writes=[("Sb",)])
                    yield
                    for c in range(4):
                        tr(psT[:, 512 + c * 128:512 + c * 128 + L], onb[:L, c * 128:(c + 1) * 128], identb[:L, :L],
                           reads=[("onb",), "identb"], writes=[PS(4)])
                    for c in range(4):
                        kc = h * 4 + c
                        dst = hid[:, kc * 512 + j * L:kc * 512 + (j + 1) * L]
                        sgs = sg[:, c * 512 + j * L:c * 512 + (j + 1) * L]
                        s.add("dve", lambda e: e.scalar_tensor_tensor(
                            out=dst, in0=psT[:, 512 + c * 128:512 + c * 128 + L], scalar=gains[:, G_GN + kc:G_GN + kc + 1], in1=sgs,
                            op0=ALU.mult, op1=ALU.mult),
                            reads=[PS(4), "gains", ("hid", 36 + 2 * c), ("hid", 37 + 2 * c)], writes=[("hid", kc)])
                    yield
                if not sample:
                    s.add("sp", lambda e: e.dma_start(
                        out=retp[h].rearrange("(c p) v -> p c v", p=128),
                        in_=Sfh.rearrange("p (c v) -> p c v", c=2)),
                        reads=[Stok], writes=[("retp", h)], dma=True)

            run(genA(0))
            run(genB(0))
            doC(0)
            for h in range(RH):
                if h + 1 < RH:
                    interleave(genD(h), chain(genA(h + 1), genB(h + 1)), r=2)
                    doC(h + 1)
                else:
                    run(genD(h))
            for c in range(NKC):
                wt, wtok = R.get(f"rout_{c}")
                wv = wt.rearrange("p (k c) -> p k c", k=32)
                po = c % 2
                for kc in range(32):
                    mm(ps[po][:, :tt], wv[:, kc, :], hid[:, kc * 512:kc * 512 + tt], kc == 0, kc == 31,
                       reads=[wtok, ("hid", kc)], writes=[PS(po)])
                stat_flush()
                dst = hT[:, c * TT:c * TT + tt]
                s.add("dve", lambda e: e.tensor_tensor(out=dst, in0=ps[po][:, :tt], in1=dst, op=ALU.add),
                      reads=[PS(po), ("hT", c)], writes=[("hT", c)])
                stat_chunk(c, tt)

        CF0 = 16 * 512
        KF0 = 24 * 512

        def latent(tix, tt, key0, ckv_rows, kr_rows):
            rmsnorm(G_KVN, tt)
            load_rope(rm_d, tix, 64)
            C = rope[:64, 0:tt]
            Ss = rope[:64, TT:TT + tt]
            cf = hid[:, CF0:CF0 + 4096].bitcast(F32)
            kf = hid[:64, KF0:KF0 + 1024].bitcast(F32)
            w = [R.get("dkv_0"), R.get("dkv_1"), R.get("dkv_2")]
            for c in range(4):
                wt, wtok = w[c // 2]
                wv = wt.rearrange("p (k c) -> p k c", k=NKC)
                pb = c % 2
                for kc in range(NKC):
                    mm(ps[pb][:, :tt], wv[:, kc, (c % 2) * 128:(c % 2 + 1) * 128], xnT[:, kc * TT:kc * TT + tt],
                       kc == 0, kc == NKC - 1, reads=[wtok, ("xnT", kc)], writes=[PS(pb)])
                cfc = cf[:, c * 512:c * 512 + tt]
                ctok = [("hid", 16 + 2 * c), ("hid", 17 + 2 * c)]
                s.add("act", lambda e, cfc=cfc, pb=pb: e.copy(out=cfc, in_=ps[pb][:, :tt]), reads=[PS(pb)], writes=ctok)
                q = sqb[:, (c % 2) * TT:(c % 2) * TT + tt]
                s.add("dve", lambda e, q=q, cfc=cfc: e.tensor_tensor(out=q, in0=cfc, in1=cfc, op=ALU.mult),
                      reads=ctok, writes=[("sqb", c % 2)])
                mm(ps[7][:, :tt], on512, q, c == 0, c == 3, reads=["cmat", ("sqb", c % 2)], writes=[PS(7)])
            rs = rstd[:, TT:TT + tt]
            rstd_from(ps[7][:, :tt], rs, [PS(7)], [("rstd", 1)])
            for c in range(4):
                cfc = cf[:, c * 512:c * 512 + tt]
                ctok = [("hid", 16 + 2 * c), ("hid", 17 + 2 * c)]
                s.add("dve", lambda e, cfc=cfc, c=c: e.scalar_tensor_tensor(
                    out=cfc, in0=cfc, scalar=gains[:, G_KVLAT + c:G_KVLAT + c + 1], in1=rs, op0=ALU.mult, op1=ALU.mult),
                    reads=ctok + [("rstd", 1), "gains"], writes=ctok)
                dst = cT[:, c * SEQ + key0:c * SEQ + key0 + tt]
                s.add("act", lambda e, cfc=cfc, dst=dst: e.copy(out=dst, in_=cfc), reads=ctok, writes=[("cT",)])
            wt, wtok = w[2]
            wv = wt.rearrange("p (k c) -> p k c", k=NKC)
            for i in range(2):
                for kc in range(NKC):
                    mm(ps[2 + i][:64, :tt], wv[:, kc, i * 64:(i + 1) * 64], xnT[:, kc * TT:kc * TT + tt],
                       kc == 0, kc == NKC - 1, reads=[wtok, ("xnT", kc)], writes=[PS(2 + i)])
            t0 = tmpf[:64, 0:tt]
            kff = kf[:, 0:tt]
            ktk = [("hid", 24), ("hid", 25)]
            s.add("dve", lambda e: e.tensor_tensor(out=t0, in0=ps[2][:64, :tt], in1=C, op=ALU.mult),
                  reads=[PS(2), "rope"], writes=[("tmpf", 0)])
            s.add("dve", lambda e: e.tensor_tensor(out=kff, in0=ps[3][:64, :tt], in1=Ss, op=ALU.mult),
                  reads=[PS(3), "rope"], writes=ktk)
            s.add("dve", lambda e: e.tensor_tensor(out=kff, in0=kff, in1=t0, op=ALU.add),
                  reads=ktk + [("tmpf", 0)], writes=ktk)
            s.add("act", lambda e: e.copy(out=krT[:64, key0:key0 + tt], in_=kff), reads=ktk, writes=[("krT",)])
            for j in range(tt // 128):
                stg = hid[:, (j % 2) * 1024:(j % 2) * 1024 + 1024].bitcast(F32)
                sttok = hidtok((j % 2) * 1024, (j % 2) * 1024 + 1024)
                pb = 4 + j % 2
                for c in range(4):
                    tr(ps[pb][:, c * 128:(c + 1) * 128], cf[:, c * 512 + j * 128:c * 512 + (j + 1) * 128], ident,
                       reads=[("hid", 16 + 2 * c), ("hid", 17 + 2 * c), "cmat"], writes=[PS(pb)])
                s.add("dve", lambda e, stg=stg, pb=pb: e.tensor_copy(out=stg, in_=ps[pb][:, :]), reads=[PS(pb)], writes=sttok)
                s.add("sp", lambda e, stg=stg, j=j: e.dma_start(out=ckv_rows[j * 128:(j + 1) * 128, :], in_=stg),
                      reads=sttok, dma=True)
                stk = hid[:, 2048 + (j % 2) * 128:2048 + (j % 2) * 128 + 128].bitcast(F32)
                stktok = [("hid", 4)]
                tr(ps[6][:, j * 64:(j + 1) * 64], kf[:, j * 128:(j + 1) * 128], ident[:64, :64],
                   reads=ktk + ["cmat"], writes=[PS(6)])
                s.add("dve", lambda e, stk=stk, j=j: e.tensor_copy(out=stk, in_=ps[6][:, j * 64:(j + 1) * 64]),
                      reads=[PS(6)], writes=stktok)
                s.add("sp", lambda e, stk=stk, j=j: e.dma_start(out=kr_rows[j * 128:(j + 1) * 128, :], in_=stk),
                      reads=stktok, dma=True)

        QN0 = 0
        QR0 = 16 * 512
        CS0 = 32 * 512
        KS0 = CS0 + 4 * 1088

        def mla(tix, tt, sample):
            rmsnorm(G_MIX[1], tt)
            load_rope(rm_d, tix, 64)
            C = rope[:64, 0:tt]
            Ss = rope[:64, TT:TT + tt]
            w = [R.get("dq_0"), R.get("dq_1")]
            qf = tmpf
            qlT = hid[:, 32 * 512:36 * 512]
            qltok = hidtok(32 * 512, 36 * 512)
            for c in range(4):
                wt, wtok = w[c // 2]
                wv = wt.rearrange("p (k c) -> p k c", k=NKC)
                pb = c % 2
                for kc in range(NKC):
                    mm(ps[pb][:, :tt], wv[:, kc, (c % 2) * 128:(c % 2 + 1) * 128], xnT[:, kc * TT:kc * TT + tt],
                       kc == 0, kc == NKC - 1, reads=[wtok, ("xnT", kc)], writes=[PS(pb)])
                qfc = qf[:, c * TT:c * TT + tt]
                s.add("act", lambda e, qfc=qfc, pb=pb: e.copy(out=qfc, in_=ps[pb][:, :tt]), reads=[PS(pb)], writes=[("tmpf", c)])
                q = sqb[:, (c % 2) * TT:(c % 2) * TT + tt]
                s.add("dve", lambda e, q=q, qfc=qfc: e.tensor_tensor(out=q, in0=qfc, in1=qfc, op=ALU.mult),
                      reads=[("tmpf", c)], writes=[("sqb", c % 2)])
                mm(ps[7][:, :tt], on512, q, c == 0, c == 3, reads=["cmat", ("sqb", c % 2)], writes=[PS(7)])
            rs = rstd[:, 0:tt]
            rstd_from(ps[7][:, :tt], rs, [PS(7)], [("rstd", 0)])
            for c in range(4):
                qfc = qf[:, c * TT:c * TT + tt]
                dst = qlT[:, c * TT:c * TT + tt]
                s.add("dve", lambda e, qfc=qfc, dst=dst, c=c: e.scalar_tensor_tensor(
                    out=dst, in0=qfc, scalar=gains[:, G_QLAT + c:G_QLAT + c + 1], in1=rs, op0=ALU.mult, op1=ALU.mult),
                    reads=[("tmpf", c), ("rstd", 0), "gains"], writes=qltok)
            for hg in range(4):
                wt, wtok = R.get(f"uq_{hg}")
                wv = wt.rearrange("p (k c) -> p k c", k=4)
                for hl in range(4):
                    h = hg * 4 + hl
                    base = hl * 256
                    for (pb, c0, c1, M) in ((0, 0, 128, 128), (1, 128, 192, 64), (2, 192, 256, 64)):
                        for cc in range(4):
                            mm(ps[pb][:M, :tt], wv[:, cc, base + c0:base + c1], qlT[:, cc * TT:cc * TT + tt],
                               cc == 0, cc == 3, reads=[wtok] + qltok, writes=[PS(pb)])
                    t0 = tmpf[:64, 0:tt]
                    t1 = tmpf[:64, TT:TT + tt]
                    s.add("dve", lambda e: e.tensor_tensor(out=t0, in0=ps[1][:64, :tt], in1=C, op=ALU.mult),
                          reads=[PS(1), "rope"], writes=[("tmpf", 0)])
                    s.add("dve", lambda e: e.tensor_tensor(out=t1, in0=ps[2][:64, :tt], in1=Ss, op=ALU.mult),
                          reads=[PS(2), "rope"], writes=[("tmpf", 1)])
                    s.add("dve", lambda e: e.tensor_tensor(out=t0, in0=t0, in1=t1, op=ALU.add),
                          reads=[("tmpf", 0), ("tmpf", 1)], writes=[("tmpf", 0)])
                    sq0 = sqb[:, 0:tt]
                    sq1 = sqb[:64, TT:TT + tt]
                    s.add("act", lambda e: e.activation(out=sq0, in_=ps[0][:, :tt], func=AF.Square),
                          reads=[PS(0)], writes=[("sqb", 0)])
                    s.add("dve", lambda e: e.tensor_tensor(out=sq1, in0=t0, in1=t0, op=ALU.mult),
                          reads=[("tmpf", 0)], writes=[("sqb", 1)])
                    mm(ps[7][:, :tt], on192, sq0, True, False, reads=["cmat", ("sqb", 0)], writes=[PS(7)])
                    mm(ps[7][:, :tt], on192[:64, :], sq1, False, True, reads=["cmat", ("sqb", 1)], writes=[PS(7)])
                    rstd_from(ps[7][:, :tt], rs, [PS(7)], [("rstd", 0)])
                    dstn = hid[:, QN0 + h * 512:QN0 + h * 512 + tt]
                    dstr = hid[:64, QR0 + h * 512:QR0 + h * 512 + tt]
                    s.add("dve", lambda e, dstn=dstn: e.scalar_tensor_tensor(
                        out=dstn, in0=ps[0][:, :tt], scalar=gains[:, G_QNOPE:G_QNOPE + 1], in1=rs, op0=ALU.mult, op1=ALU.mult),
                        reads=[PS(0), ("rstd", 0), "gains"], writes=[("hid", h)])
                    s.add("dve", lambda e, dstr=dstr: e.scalar_tensor_tensor(
                        out=dstr, in0=t0, scalar=gains[:64, G_QROPE:G_QROPE + 1], in1=rs[:64, :], op0=ALU.mult, op1=ALU.mult),
                        reads=[("tmpf", 0), ("rstd", 0), "gains"], writes=[("hid", 16 + h)])
            fence(U_TOKENS)
            rl = rstd[:, TT:2 * TT]
            iters = []
            if not sample:
                for h in range(MH):
                    for kb in range(tix + 1):
                        iters.append(dict(h=h, q0=0, TQ=tt, csrc=cT, cstride=SEQ, ksrc=krT, k0=kb * 512, nk=512,
                                          diag=(kb == tix), ctk=[("cT",)], ktk=[("krT",)], first=(kb == 0), last=(kb == tix),
                                          load=None, nsteps=4 * (tix + 1)))
            else:
                cTs = hid[:, CS0:CS0 + 4 * 1088]
                krTs = hid[:, KS0:KS0 + 1088]
                alltok = sorted(set(hidtok(CS0, KS0) + hidtok(KS0, KS0 + 1088)))
                for sq_ in range(4):
                    for h in range(MH):
                        for kbi, (k0, nk) in enumerate(((0, 512), (512, 512), (1024, 64))):
                            iters.append(dict(h=h, q0=sq_ * 64, TQ=64, csrc=cTs, cstride=1088, ksrc=krTs, k0=k0, nk=nk,
                                              diag=False, ctk=alltok, ktk=alltok, first=(kbi == 0), last=(kbi == 2),
                                              load=(sq_ if (h == 0 and kbi == 0) else None), nsteps=9))

            def load_cache(sq_):
                q0 = sq_ * 64
                for blk in range(8):
                    cs = Sf[:, (blk % 2) * 1024:(blk % 2) * 1024 + 512]
                    s.add("sp", lambda e: e.dma_start(out=cs, in_=cckv[sq_, blk * 128:(blk + 1) * 128, :]),
                          writes=[("Sf", blk % 2)], dma=True)
                    pb = blk % 2
                    for c in range(4):
                        tr(ps[pb][:, c * 128:(c + 1) * 128], cs[:, c * 128:(c + 1) * 128], ident,
                           reads=[("Sf", blk % 2), "cmat"], writes=[PS(pb)])
                    dst = cTs.rearrange("p (c k) -> p c k", c=4)[:, :, blk * 128:(blk + 1) * 128]
                    s.add("act", lambda e: e.copy(out=dst, in_=ps[pb][:, :].rearrange("p (c k) -> p c k", c=4)),
                          reads=[PS(pb)], writes=alltok)
                krs_ = tmpf[:, 0:512]
                s.add("sp", lambda e: e.dma_start(out=krs_.rearrange("p (b r) -> p b r", b=8),
                                                  in_=ckr[sq_].rearrange("(b p) r -> p b r", p=128)),
                      writes=[("tmpf", 0)], dma=True)
                for blk in range(8):
                    pb = blk // 4
                    tr(ps[pb][:64, (blk % 4) * 128:(blk % 4 + 1) * 128], krs_[:, blk * 64:(blk + 1) * 64], ident,
                       reads=[("tmpf", 0), "cmat"], writes=[PS(pb)])
                for i in range(2):
                    s.add("act", lambda e: e.copy(out=krTs[:64, i * 512:(i + 1) * 512], in_=ps[i][:64, :]),
                          reads=[PS(i)], writes=alltok)
                for c in range(4):
                    s.add("dve", lambda e: e.tensor_copy(out=cTs[:, c * 1088 + 1024:c * 1088 + 1088],
                                                         in_=cT[:, c * SEQ + q0:c * SEQ + q0 + 64]),
                          reads=[("cT",)], writes=alltok)
                s.add("dve", lambda e: e.tensor_copy(out=krTs[:64, 1024:1088], in_=krT[:64, q0:q0 + 64]),
                      reads=[("krT",)], writes=alltok)

            wcur = {}

            def kphase(i, it):
                b2 = i % 2
                if it["load"] is not None:
                    load_cache(it["load"])
                h = it["h"]
                hg, hl = h // 4, h % 4
                if it["first"] and hl == 0:
                    wcur["w"] = R.get(f"ukv_{hg}")
                wt, wtok = wcur["w"]
                wv = wt.rearrange("p (k c) -> p k c", k=4)
                base = hl * 256
                csrc, cstride, ksrc, k0, nk = it["csrc"], it["cstride"], it["ksrc"], it["k0"], it["nk"]
                ctk, ktk = it["ctk"], it["ktk"]
                pk = b2
                knb = kn2[:, b2 * 512:b2 * 512 + nk]
                knrb = knr2[:64, b2 * 512:b2 * 512 + nk]
                for cc in range(4):
                    mm(ps[pk][:, :nk], wv[:, cc, base:base + 128], csrc[:, cc * cstride + k0:cc * cstride + k0 + nk],
                       cc == 0, cc == 3, reads=[wtok] + ctk, writes=[PS(pk)])
                nsub = (nk + 127) // 128
                for ks in range(nsub):
                    nkk = min(128, nk - ks * 128)
                    for cc in range(4):
                        mm(ps[2][:nkk, ks * 128:(ks + 1) * 128],
                           csrc[:, cc * cstride + k0 + ks * 128:cc * cstride + k0 + ks * 128 + nkk],
                           wv[:, cc, base + 128:base + 256], cc == 0, cc == 3, reads=[wtok] + ctk, writes=[PS(2)])
                sq0 = sqb[:, 0:nk]
                sq1 = sqb[:64, TT:TT + nk]
                s.add("act", lambda e: e.activation(out=sq0, in_=ps[pk][:, :nk], func=AF.Square),
                      reads=[PS(pk)], writes=[("sqb", 0)])
                ksl = ksrc[:64, k0:k0 + nk]
                s.add("dve", lambda e: e.tensor_tensor(out=sq1, in0=ksl, in1=ksl, op=ALU.mult),
                      reads=ktk, writes=[("sqb", 1)])
                mm(ps[7][:, :nk], on192, sq0, True, False, reads=["cmat", ("sqb", 0)], writes=[PS(7)])
                mm(ps[7][:, :nk], on192[:64, :], sq1, False, True, reads=["cmat", ("sqb", 1)], writes=[PS(7)])
                rk = rstd[:, 0:nk]
                rstd_from(ps[7][:, :nk], rk, [PS(7)], [("rstd", 0)])
                s.add("dve", lambda e: e.scalar_tensor_tensor(
                    out=knb, in0=ps[pk][:, :nk], scalar=gains[:, G_KNOPE:G_KNOPE + 1], in1=rk,
                    op0=ALU.mult, op1=ALU.mult), reads=[PS(pk), ("rstd", 0), "gains"], writes=[("kn", b2)])
                s.add("dve", lambda e: e.scalar_tensor_tensor(
                    out=knrb, in0=ksl, scalar=gains[:64, G_KROPE:G_KROPE + 1], in1=rk[:64, :],
                    op0=ALU.mult, op1=ALU.mult), reads=ktk + [("rstd", 0), "gains"], writes=[("knr", b2)])
                np_ = min(128, nk)
                s.add("act", lambda e: e.copy(out=vsb2[:np_, b2 * 512:b2 * 512 + nsub * 128], in_=ps[2][:np_, :nsub * 128]),
                      reads=[PS(2)], writes=[("vsb", b2)])

            stepc = {}

            def sphase(i, it):
                b2 = i % 2
                h, q0, TQ, nk, diag = it["h"], it["q0"], it["TQ"], it["nk"], it["diag"]
                if it["first"]:
                    stepc["n"] = 0
                qn = hid[:, QN0 + h * 512 + q0:QN0 + h * 512 + q0 + TQ]
                qr = hid[:64, QR0 + h * 512 + q0:QR0 + h * 512 + q0 + TQ]
                nsub = (nk + 127) // 128
                pend = []
                for ks in range(nsub):
                    step = stepc["n"]
                    nkk = min(128, nk - ks * 128)
                    qlo = ks * 128 if diag else 0
                    ncol = TQ - qlo
                    pscore = 3 + step % 2
                    mm(ps[pscore][:nkk, :ncol], kn2[:, b2 * 512 + ks * 128:b2 * 512 + ks * 128 + nkk], qn[:, qlo:TQ], True, False,
                       reads=[("kn", b2), ("hid", h)], writes=[PS(pscore)])
                    mm(ps[pscore][:nkk, :ncol], knr2[:64, b2 * 512 + ks * 128:b2 * 512 + ks * 128 + nkk], qr[:, qlo:TQ], False, True,
                       reads=[("knr", b2), ("hid", 16 + h)], writes=[PS(pscore)])
                    pb_ = pbuf[:nkk, (step % 2) * 512:(step % 2) * 512 + ncol]
                    ptok = ("pbuf", step % 2)
                    s.add("act", lambda e: e.activation(
                        out=pb_, in_=ps[pscore][:nkk, :ncol], func=AF.Exp, scale=float(192 ** -0.5)),
                        reads=[PS(pscore)], writes=[ptok])
                    if diag:
                        pd = pbuf[:nkk, (step % 2) * 512:(step % 2) * 512 + 128]
                        s.add("dve", lambda e: e.tensor_tensor(out=pd, in0=pd, in1=amask, op=ALU.mult),
                              reads=[ptok, "cmat"], writes=[ptok])
                    first = step == 0
                    last = step == it["nsteps"] - 1

                    def pv(nkk=nkk, ks=ks, pb_=pb_, ptok=ptok, first=first, last=last, qlo=qlo):
                        mm(ps[5][:, qlo:TQ], vsb2[:nkk, b2 * 512 + ks * 128:b2 * 512 + (ks + 1) * 128], pb_, first, last,
                           reads=[("vsb", b2), ptok], writes=[PS(5)])
                        mm(ps[6][:, qlo:TQ], onesb[:nkk, :], pb_, first, last,
                           reads=["onesb", ptok], writes=[PS(6)])
                    if pend:
                        pend.pop()()
                    pend.append(pv)
                    stepc["n"] = step + 1
                if pend:
                    pend.pop()()
                if it["last"]:
                    assert stepc["n"] == it["nsteps"], (stepc["n"], it["nsteps"])
                    s.add("dve", lambda e: e.reciprocal(out=rl[:, :TQ], in_=ps[6][:, :TQ]), reads=[PS(6)], writes=[("rstd", 1)])
                    dst = xnT[:, h * TT + q0:h * TT + q0 + TQ]
                    s.add("dve", lambda e: e.tensor_tensor(out=dst, in0=ps[5][:, :TQ], in1=rl[:, :TQ], op=ALU.mult),
                          reads=[PS(5), ("rstd", 1)], writes=[("xnT", h)])

            kphase(0, iters[0])
            for i, it in enumerate(iters):
                if i + 1 < len(iters):
                    kphase(i + 1, iters[i + 1])
                sphase(i, it)
            for np_ in range(8):
                wt, wtok = R.get(f"wo_{np_}")
                wv = wt.rearrange("p (k c) -> p k c", k=NKC)
                for i in range(2):
                    c = np_ * 2 + i
                    po = c % 2
                    for kc in range(NKC):
                        mm(ps[po][:, :tt], wv[:, kc, i * 128:(i + 1) * 128], xnT[:, kc * TT:kc * TT + tt], kc == 0, kc == NKC - 1,
                           reads=[wtok, ("xnT", kc)], writes=[PS(po)])
                    stat_flush()
                    dst = hT[:, c * TT:c * TT + tt]
                    s.add("dve", lambda e, dst=dst, po=po: e.tensor_tensor(out=dst, in0=ps[po][:, :tt], in1=dst, op=ALU.add),
                          reads=[PS(po), ("hT", c)], writes=[("hT", c)])
                    stat_chunk(c, tt)

        passes = [(t, False) for t in range(n_prompt_tiles)] + ([(NPT, True)] if do_sample else [])
        for (tix, sample) in passes:
            R.plan(pass_weight_names(sample)[:None])
        for (tix, sample) in passes:
            tt = 256 if sample else TT
            if sample:
                load_x(xs, 2)
            else:
                load_x(xp[tix * TT:(tix + 1) * TT, :], 4)
            ffn(0, 1, tt)
            retention(tix, tt, 64 if sample else 128, sample)
            ffn(0, 2, tt)
            if sample:
                latent(tix, tt, 0, ckvs, krs)
            else:
                latent(tix, tt, tix * TT, ckvp[tix * TT:(tix + 1) * TT, :], krp[tix * TT:(tix + 1) * TT, :])
            ffn(1, 1, tt, reuse=True)
            mla(tix, tt, sample)
            ffn(1, 2, tt, next_norm=False)
            if sample:
                store_y(ys, 2)
            else:
                store_y(yp[tix * TT:(tix + 1) * TT, :], 4)
        assert R.pos == len(R.sched), (R.pos, len(R.sched))
        s.emit(sems_eng, lanes)
    return nc


_CACHE = {}


def kernel(**inp):
    inp = {k: np.asarray(v) for k, v in inp.items()}
    W = pack_weights(inp)
    G = pack_gains(inp)
    cm, dqk, rr, rm = const_tables()
    if "nc" not in _CACHE:
        _CACHE["nc"] = build_program()
    nc = _CACHE["nc"]
    in_maps = []
    for c in range(NCORES):
        in_maps.append({
            "xp": np.ascontiguousarray(inp["x_prompt"][c]),
            "xs": np.ascontiguousarray(inp["x_sample"][4 * c:4 * c + 4].reshape(256, D)),
            "sret": np.ascontiguousarray(inp["state_ret"][0, 4 * c:4 * c + 4]),
            "cckv": np.ascontiguousarray(inp["cache_ckv"][4 * c:4 * c + 4]),
            "ckr": np.ascontiguousarray(inp["cache_krope"][4 * c:4 * c + 4]),
            "wts": W, "gains": G, "cmat": cm, "dqk": dqk, "rope_ret": rr, "rope_mla": rm,
        })
    res = run_bass_kernel_spmd(nc, in_maps, core_ids=list(range(NCORES)))
    r = res.results
    y_prompt = np.stack([r[c]["yp"] for c in range(NCORES)], 0)
    y_sample = np.concatenate([r[c]["ys"].reshape(4, DEC_S, D) for c in range(NCORES)], 0)
    ret_p = np.stack([r[c]["retp"] for c in range(NCORES)], 0)[None]
    ckv_p = np.stack([r[c]["ckvp"] for c in range(NCORES)], 0)
    kr_p = np.stack([r[c]["krp"] for c in range(NCORES)], 0)
    ret_s = np.concatenate([r[c]["rets"] for c in range(NCORES)], 0)[None]
    ckv_s = np.concatenate([r[c]["ckvs"].reshape(4, DEC_S, 512) for c in range(NCORES)], 0)
    kr_s = np.concatenate([r[c]["krs"].reshape(4, DEC_S, 64) for c in range(NCORES)], 0)
    return (y_prompt.astype(np.float32), y_sample.astype(np.float32), ret_p.astype(np.float32),
            ckv_p.astype(np.float32), kr_p.astype(np.float32), ret_s.astype(np.float32),
            ckv_s.astype(np.float32), kr_s.astype(np.float32))
```
